# Optimizing a Trainium2 kernel written in Bass

```python
import jax
import jax.numpy as jnp
from jax import lax
import numpy as np

D_MODEL = 2048
BATCH = 4
SEQ = 2048
DEPTH = 4

A_HEADS = 8
A_KV_HEADS = 2
A_HEAD_DIM = 128
IDX_HEADS = 16
IDX_DIM = 64
TOPK_MAX = 256
M_HEADS = 8
M_Q_LORA = 512
M_KV_LORA = 256
M_NOPE = 128
M_ROPE = 64
M_V = 128
D_FF = 5632
CONV_W = 3
ROPE_THETA = 500000.0
ROT_FRACTION = 4
EPS = 1e-6
Q_BLOCK = 128
N_ADA = 6
IN_SIZES = (A_HEADS * A_HEAD_DIM, A_KV_HEADS * A_HEAD_DIM, A_KV_HEADS * A_HEAD_DIM,
            IDX_HEADS * IDX_DIM, IDX_DIM, IDX_HEADS,
            M_Q_LORA, M_KV_LORA, M_ROPE,
            D_MODEL, D_MODEL)
IN_DIM = sum(IN_SIZES)
IDX_W_SCALE = (IDX_HEADS * IDX_DIM) ** -0.5

kernel_name = 'hybrid_dsa_mla_convffn_adaln'


def rms_norm(t, g):
    tf = t.astype(jnp.float32)
    y = tf * lax.rsqrt(jnp.mean(tf * tf, axis=-1, keepdims=True) + EPS)
    return (y * g.astype(jnp.float32)).astype(t.dtype)


def rope_tables(pos, rot_dim):
    inv = ROPE_THETA ** (-jnp.arange(0, rot_dim, 2, dtype=jnp.float32) / rot_dim)
    ang = pos[..., None] * inv
    return jnp.cos(ang)[:, :, None, :], jnp.sin(ang)[:, :, None, :]


def apply_rope(t, cos, sin):
    r = cos.shape[-1] * 2
    tf = t[..., :r].astype(jnp.float32)
    t1, t2 = tf[..., : r // 2], tf[..., r // 2:]
    rot = jnp.concatenate([t1 * cos - t2 * sin, t2 * cos + t1 * sin], axis=-1).astype(t.dtype)
    return jnp.concatenate([rot, t[..., r:]], axis=-1)


def split_cols(t, sizes):
    offsets = np.cumsum(sizes)[:-1]
    return jnp.split(t, [int(o) for o in offsets], axis=-1)


def to_blocks(t, nb):
    b = t.shape[0]
    return jnp.moveaxis(t.reshape((b, nb, Q_BLOCK) + t.shape[2:]), 1, 0)


def from_blocks(t):
    nb, b, tq = t.shape[:3]
    return jnp.moveaxis(t, 0, 1).reshape((b, nb * tq) + t.shape[3:])


def dsa_attention(q, k, v, qi, ki, wi):
    B, S, H, dh = q.shape
    G = k.shape[2]
    n_sel = min(TOPK_MAX, S // 4)
    nb = S // Q_BLOCK
    key_pos = jnp.arange(S)
    scale = dh ** -0.5

    def block(args):
        q_b, qi_b, wi_b, start = args
        q_pos = start + jnp.arange(Q_BLOCK)
        causal = key_pos[None, :] <= q_pos[:, None]
        dots = jnp.einsum('bthd,bsd->bths', qi_b, ki, preferred_element_type=jnp.float32)
        score = jnp.einsum('bths,bth->bts', jax.nn.relu(dots), wi_b.astype(jnp.float32))
        score = jnp.where(causal[None], score, -jnp.inf)
        _, sel = lax.top_k(score, n_sel)
        valid = sel <= q_pos[None, :, None]
        k_sel = jax.vmap(lambda kk, ii: kk[ii])(k, sel)
        v_sel = jax.vmap(lambda vv, ii: vv[ii])(v, sel)
        qg = q_b.reshape(B, Q_BLOCK, G, H // G, dh)
        logits = jnp.einsum('btgrd,btkgd->btgrk', qg, k_sel,
                            preferred_element_type=jnp.float32) * scale
        logits = jnp.where(valid[:, :, None, None, :], logits, -jnp.inf)
        p = jax.nn.softmax(logits, axis=-1).astype(v.dtype)
        o = jnp.einsum('btgrk,btkgd->btgrd', p, v_sel)
        return o.reshape(B, Q_BLOCK, H * dh)

    starts = jnp.arange(nb) * Q_BLOCK
    out = lax.map(block, (to_blocks(q, nb), to_blocks(qi, nb), to_blocks(wi, nb), starts))
    return from_blocks(out)


def causal_dense_attention(q, k, v):
    B, S, H, dqk = q.shape
    dv = v.shape[-1]
    nb = S // Q_BLOCK
    key_pos = jnp.arange(S)
    scale = dqk ** -0.5

    def block(args):
        q_b, start = args
        q_pos = start + jnp.arange(Q_BLOCK)
        causal = key_pos[None, :] <= q_pos[:, None]
        logits = jnp.einsum('bthd,bshd->bhts', q_b, k, preferred_element_type=jnp.float32) * scale
        logits = jnp.where(causal[None, None], logits, -jnp.inf)
        p = jax.nn.softmax(logits, axis=-1).astype(v.dtype)
        return jnp.einsum('bhts,bshd->bthd', p, v).reshape(B, Q_BLOCK, H * dv)

    starts = jnp.arange(nb) * Q_BLOCK
    out = lax.map(block, (to_blocks(q, nb), starts))
    return from_blocks(out)


def parallel_mixers(h, w_in, g_qa, g_ka, g_mq_lat, w_mq_up, g_mkv_lat, w_mkv_up,
                    g_qm, g_km, w_pa, w_pb, w_o, rope_a, rope_i, rope_m):
    B, S, _ = h.shape
    proj = h @ w_in
    (qa, ka, va, qi, ki, wi, mq_lat, mkv_lat, mk_rope, gate_a, gate_b) = split_cols(proj, IN_SIZES)

    qa = apply_rope(rms_norm(qa.reshape(B, S, A_HEADS, A_HEAD_DIM), g_qa), *rope_a)
    ka = apply_rope(rms_norm(ka.reshape(B, S, A_KV_HEADS, A_HEAD_DIM), g_ka), *rope_a)
    va = va.reshape(B, S, A_KV_HEADS, A_HEAD_DIM)
    qi = apply_rope(qi.reshape(B, S, IDX_HEADS, IDX_DIM), *rope_i)
    ki = apply_rope(ki[:, :, None, :], *rope_i)[:, :, 0, :]
    o_a = dsa_attention(qa, ka, va, qi, ki, wi * IDX_W_SCALE)

    mq = (rms_norm(mq_lat, g_mq_lat) @ w_mq_up).reshape(B, S, M_HEADS, M_NOPE + M_ROPE)
    mkv = (rms_norm(mkv_lat, g_mkv_lat) @ w_mkv_up).reshape(B, S, M_HEADS, M_NOPE + M_V)
    mk_nope, mv = mkv[..., :M_NOPE], mkv[..., M_NOPE:]
    mk_r = jnp.broadcast_to(mk_rope[:, :, None, :], (B, S, M_HEADS, M_ROPE))
    mk = rms_norm(jnp.concatenate([mk_nope, mk_r], axis=-1), g_km)
    mq = rms_norm(mq, g_qm)
    mq = jnp.concatenate([mq[..., :M_NOPE], apply_rope(mq[..., M_NOPE:], *rope_m)], axis=-1)
    mk = jnp.concatenate([mk[..., :M_NOPE], apply_rope(mk[..., M_NOPE:], *rope_m)], axis=-1)
    o_b = causal_dense_attention(mq, mk, mv)

    merged = jax.nn.sigmoid(gate_a) * (o_a @ w_pa) + jax.nn.sigmoid(gate_b) * (o_b @ w_pb)
    return merged @ w_o


def conv_ffn(h, w_up, w_conv, b_conv, w_down):
    S = h.shape[1]
    u = h @ w_up
    up = jnp.pad(u, ((0, 0), (CONV_W - 1, 0), (0, 0)))
    conv = b_conv
    for j in range(CONV_W):
        conv = conv + w_conv[j] * up[:, j:j + S]
    gate, val = jnp.split(conv, 2, axis=-1)
    return (jax.nn.silu(gate) * val) @ w_down


def setup_inputs(seed: int = 0) -> dict:
    key = jax.random.key(seed)
    ks = jax.random.split(key, 24)
    f32 = jnp.float32

    def nrm(k, shape, scale):
        return jax.random.normal(k, shape, f32) * scale

    def gain(k, shape):
        return 1.0 + 0.02 * jax.random.normal(k, shape, f32)

    L, D = DEPTH, D_MODEL
    x = nrm(ks[0], (BATCH, SEQ, D), 1.0)
    c = nrm(ks[1], (BATCH, D), 1.0)
    offset = jax.random.randint(ks[2], (BATCH, 1), 0, 1024, dtype=jnp.int32)
    positions = (jnp.arange(SEQ, dtype=jnp.int32)[None, :] + offset).astype(jnp.int32)
    return {
        'x': x,
        'c': c,
        'positions': positions,
        'g_attn': gain(ks[3], (L, D)),
        'g_ffn': gain(ks[4], (L, D)),
        'w_ada': nrm(ks[5], (L, D, N_ADA * D), 0.5 * D ** -0.5),
        'b_ada': nrm(ks[6], (L, N_ADA * D), 0.01),
        'w_in': nrm(ks[7], (L, D, IN_DIM), D ** -0.5),
        'g_qa': gain(ks[8], (L, A_HEAD_DIM)),
        'g_ka': gain(ks[9], (L, A_HEAD_DIM)),
        'g_mq_lat': gain(ks[10], (L, M_Q_LORA)),
        'w_mq_up': nrm(ks[11], (L, M_Q_LORA, M_HEADS * (M_NOPE + M_ROPE)), M_Q_LORA ** -0.5),
        'g_mkv_lat': gain(ks[12], (L, M_KV_LORA)),
        'w_mkv_up': nrm(ks[13], (L, M_KV_LORA, M_HEADS * (M_NOPE + M_V)), M_KV_LORA ** -0.5),
        'g_qm': gain(ks[14], (L, M_NOPE + M_ROPE)),
        'g_km': gain(ks[15], (L, M_NOPE + M_ROPE)),
        'w_pa': nrm(ks[16], (L, A_HEADS * A_HEAD_DIM, D), (A_HEADS * A_HEAD_DIM) ** -0.5),
        'w_pb': nrm(ks[17], (L, M_HEADS * M_V, D), (M_HEADS * M_V) ** -0.5),
        'w_o': nrm(ks[18], (L, D, D), D ** -0.5),
        'w_up': nrm(ks[19], (L, D, 2 * D_FF), D ** -0.5),
        'w_conv': nrm(ks[20], (L, CONV_W, 2 * D_FF), CONV_W ** -0.5),
        'b_conv': nrm(ks[21], (L, 2 * D_FF), 0.01),
        'w_down': nrm(ks[22], (L, D_FF, D), D_FF ** -0.5),
    }


def reference(x, c, positions, g_attn, g_ffn, w_ada, b_ada, w_in, g_qa, g_ka,
              g_mq_lat, w_mq_up, g_mkv_lat, w_mkv_up, g_qm, g_km, w_pa, w_pb, w_o,
              w_up, w_conv, b_conv, w_down):
    pos = positions.astype(jnp.float32)
    rope_a = rope_tables(pos, A_HEAD_DIM // ROT_FRACTION)
    rope_i = rope_tables(pos, IDX_DIM // ROT_FRACTION)
    rope_m = rope_tables(pos, M_ROPE)
    c_act = jax.nn.silu(c)
    for l in range(DEPTH):
        mod = (c_act @ w_ada[l] + b_ada[l])[:, None, :]
        sh_a, sc_a, gt_a, sh_f, sc_f, gt_f = jnp.split(mod, N_ADA, axis=-1)
        h = rms_norm(x, g_attn[l]) * (1.0 + sc_a) + sh_a
        y = parallel_mixers(h, w_in[l], g_qa[l], g_ka[l], g_mq_lat[l], w_mq_up[l],
                            g_mkv_lat[l], w_mkv_up[l], g_qm[l], g_km[l], w_pa[l], w_pb[l],
                            w_o[l], rope_a, rope_i, rope_m)
        x = x + gt_a * y
        h = rms_norm(x, g_ffn[l]) * (1.0 + sc_f) + sh_f
        x = x + gt_f * conv_ffn(h, w_up[l], w_conv[l], b_conv[l], w_down[l])
    return x
```

```python
from contextlib import ExitStack
import numpy as np
import concourse.bass as bass
import concourse.mybir as mybir
from concourse.bass_utils import run_bass_kernel_spmd

F32 = mybir.dt.float32
BF16 = mybir.dt.bfloat16
I32 = mybir.dt.int32
AF = mybir.ActivationFunctionType
ALU = mybir.AluOpType

ENGS = ("pe", "act", "dve", "pool", "sp")
NRING = 12

D = 2048
KC = 16
IN_DIM = 7568
DFF = 5632
EPS = 1e-6
PI = float(np.pi)
TWO_PI = float(2 * np.pi)


class Buf:
    __slots__ = ("w", "r", "excl")

    def __init__(self, excl=False):
        self.w = None
        self.r = {}
        self.excl = excl


class Prog:
    def __init__(self, nc, stack):
        self.nc = nc
        self.q = {e: [] for e in ENGS}
        self.cnt = {e: 0 for e in ENGS}
        self.dcnt = {e: 0 for e in ENGS}
        self.seen = {e: {} for e in ENGS}
        self.sem = {}
        for e in ENGS:
            self.sem[("c", e)] = stack.enter_context(nc.semaphore("c_" + e))
        for e in ("sp", "pool", "act"):
            for r in range(NRING):
                self.sem[("d", e, r)] = stack.enter_context(nc.semaphore("d_%s_%d" % (e, r)))

    def _need(self, eng, waits, tok):
        if tok is None:
            return
        key, val = tok
        if eng == "pe" and key == ("c", "pe"):
            return
        if self.seen[eng].get(key, 0) >= val:
            return
        self.seen[eng][key] = val
        waits.append((key, val))

    def _deps(self, eng, reads, writes):
        waits = []
        own = ("c", eng)
        for b in reads:
            self._need(eng, waits, b.w)
            if b.excl:
                for k, v in b.r.items():
                    if k != own:
                        self._need(eng, waits, (k, v))
        for b in writes:
            self._need(eng, waits, b.w)
            for k, v in b.r.items():
                self._need(eng, waits, (k, v))
        return waits

    def _mark(self, tok, reads, writes):
        k, v = tok
        for b in reads:
            if b.r.get(k, 0) < v:
                b.r[k] = v
        for b in writes:
            b.w = tok
            b.r = {}

    def op(self, eng, fn, reads=(), writes=()):
        waits = self._deps(eng, reads, writes)
        self.cnt[eng] += 1
        tok = (("c", eng), self.cnt[eng])
        self.q[eng].append((fn, waits, tok[0], 1))
        self._mark(tok, reads, writes)

    def dma(self, eng, fn, reads=(), writes=()):
        waits = self._deps(eng, reads, writes)
        j = self.dcnt[eng]
        self.dcnt[eng] += 1
        slot = j % NRING
        use = j // NRING
        key = ("d", eng, slot)
        if use > 0:
            self._need(eng, waits, (key, 16 * use))
        tok = (key, 16 * (use + 1))
        self.q[eng].append((fn, waits, key, 16))
        self._mark(tok, reads, writes)

    def wait_all(self, eng):
        waits = []
        for e in ENGS:
            if self.cnt[e] > 0:
                self._need(eng, waits, (("c", e), self.cnt[e]))
        for e in ("sp", "pool", "act"):
            j = self.dcnt[e]
            for slot in range(NRING):
                if j > slot:
                    uses = (j - slot + NRING - 1) // NRING
                    self._need(eng, waits, (("d", e, slot), 16 * uses))
        self.q[eng].append((None, waits, None, 0))

    def barrier(self):
        for e in ENGS:
            self.wait_all(e)

    def _replay(self, name, e):
        sem = self.sem
        for fn, waits, key, inc in self.q[name]:
            for k, v in waits:
                e.wait_ge(sem[k], v)
            if fn is not None:
                fn(e).then_inc(sem[key], inc)

    def emit(self, block):
        P = self

        @block.tensor
        def _(e):
            P._replay("pe", e)

        @block.scalar
        def _(e):
            P._replay("act", e)

        @block.vector
        def _(e):
            P._replay("dve", e)

        @block.gpsimd
        def _(e):
            P._replay("pool", e)

        @block.sync
        def _(e):
            P._replay("sp", e)


W_SHAPES = {
    "g_attn": (D,), "g_ffn": (D,), "w_ada": (D, 6 * D), "b_ada": (6 * D,), "w_in": (D, IN_DIM),
    "g_qa": (128,), "g_ka": (128,), "g_mq_lat": (512,), "w_mq_up": (512, 1536), "g_mkv_lat": (256,),
    "w_mkv_up": (256, 2048), "g_qm": (192,), "g_km": (192,), "w_pa": (1024, D), "w_pb": (1024, D),
    "w_o": (D, D), "w_up": (D, 2 * DFF), "w_conv": (3, 2 * DFF), "b_conv": (2 * DFF,), "w_down": (DFF, D),
}


def inv_freq_table():
    def inv(rot):
        return (500000.0 ** (-np.arange(0, rot, 2, dtype=np.float32) / np.float32(rot))).astype(np.float32)
    t = np.concatenate([inv(32), inv(16), inv(64)]).astype(np.float32)
    return np.ascontiguousarray(np.broadcast_to(t[None, :], (128, 56))).astype(np.float32)


def build(S, L, n_sel, dbg=()):
    NT = S // 128
    NCH = S // 512
    nc = bass.Bass("TRN2", target_bir_lowering=False)

    def din(name, shape, dt=F32):
        return nc.dram_tensor(name, list(shape), dt, kind="ExternalInput").ap()

    def dscr(name, shape, dt):
        return nc.dram_tensor(name, list(shape), dt, kind="Internal").ap()

    x_in = din("x", (S, D))
    c_in = din("c", (D,))
    pos_in = din("positions", (S,), I32)
    invf_in = din("invf", (128, 56))
    Wd = {k: din(k, (L,) + v) for k, v in W_SHAPES.items()}
    out = nc.dram_tensor("out", [S, D], F32, kind="ExternalOutput").ap()
    dbg_out = {k: nc.dram_tensor("dbg_" + k, list(shp), dt, kind="ExternalOutput").ap() for k, (shp, dt) in dbg}

    xT = dscr("xT", (D, S), F32)
    qaT = dscr("qaT", (1024, S), BF16)
    kaT = dscr("kaT", (256, S), BF16)
    va = dscr("va", (S, 256), BF16)
    qiT = dscr("qiT", (1024, S), BF16)
    kiT2 = dscr("kiT2", (128, S), BF16)
    wis = dscr("wis", (S, 16), F32)
    mkr = dscr("mkr", (S, 64), F32)
    mqlT = dscr("mqlT", (512, S), BF16)
    mkvlT = dscr("mkvlT", (256, S), BF16)
    sgaT = dscr("sgaT", (D, S), F32)
    sgbT = dscr("sgbT", (D, S), F32)
    mqTn = dscr("mqTn", (1024, S), BF16)
    mqTr = dscr("mqTr", (512, S), BF16)
    mkTn = dscr("mkTn", (1024, S), BF16)
    mkTr = dscr("mkTr", (512, S), BF16)
    mv = dscr("mv", (S, 1024), BF16)
    MT = dscr("MT", (S, S), BF16)
    oaT = dscr("oaT", (1024, S), BF16)
    obT = dscr("obT", (1024, S), BF16)
    actT = dscr("actT", (DFF, S), BF16)

    with ExitStack() as st0:
        P = Prog(nc, st0)

        def MM(o, lhsT, rhs, start, stop, rd, wr):
            P.op("pe", lambda e: e.matmul(o, lhsT=lhsT, rhs=rhs, start=start, stop=stop), rd, wr)

        def TR(o, in_, ident, rd, wr):
            P.op("pe", lambda e: e.transpose(out=o, in_=in_, identity=ident), rd, wr)

        def ACT(o, in_, func, rd, wr, **kw):
            P.op("act", lambda e: e.activation(out=o, in_=in_, func=func, **kw), rd, wr)

        def TS(eng, o, in0, s1, s2, op0, op1, rd, wr):
            if s2 is None:
                P.op(eng, lambda e: e.tensor_scalar(out=o, in0=in0, scalar1=s1, scalar2=None, op0=op0), rd, wr)
            else:
                P.op(eng, lambda e: e.tensor_scalar(out=o, in0=in0, scalar1=s1, scalar2=s2, op0=op0, op1=op1), rd, wr)

        def TT(eng, o, in0, in1, op, rd, wr):
            P.op(eng, lambda e: e.tensor_tensor(out=o, in0=in0, in1=in1, op=op), rd, wr)

        def STT(eng, o, in0, scalar, in1, op0, op1, rd, wr):
            eng = "dve"
            P.op(eng, lambda e: e.scalar_tensor_tensor(out=o, in0=in0, scalar=scalar, in1=in1, op0=op0, op1=op1), rd, wr)

        def CP(eng, o, in_, rd, wr):
            if eng == "act":
                P.op("act", lambda e: e.copy(out=o, in_=in_), rd, wr)
            else:
                P.op(eng, lambda e: e.tensor_copy(out=o, in_=in_), rd, wr)

        def MS(eng, o, val, wr):
            P.op(eng, lambda e: e.memset(o, val), (), wr)

        def DMA(eng, o, in_, rd, wr):
            P.dma(eng, lambda e: e.dma_start(out=o, in_=in_), rd, wr)

        def MAX8(o, in_, rd, wr):
            P.op("dve", lambda e: e.max(out=o, in_=in_), rd, wr)

        def MREP(o, rep, vals, rd, wr):
            P.op("dve", lambda e: e.match_replace(out=o, in_to_replace=rep, in_values=vals, imm_value=-1e30), rd, wr)

        def RECIP(o, in_, rd, wr):
            P.op("dve", lambda e: e.reciprocal(out=o, in_=in_), rd, wr)

        def RSTD(t, inv_n, rd_wr):
            TS("dve", t, t, inv_n, EPS, ALU.mult, ALU.add, rd_wr, rd_wr)
            ACT(t, t, AF.Sqrt, rd_wr, rd_wr)
            P.op("dve", lambda e: e.reciprocal(out=t, in_=t), rd_wr, rd_wr)

        def skewed_steps(units):
            ns = max(len(x) for x in units)
            steps = []
            for t in range(len(units) + ns - 1):
                def step(t=t):
                    for k in reversed(range(ns)):
                        u = t - k
                        if 0 <= u < len(units) and k < len(units[u]):
                            units[u][k]()
                steps.append(step)
            return steps

        def interleave(A, B):
            tot = sum(w for _, w in A) or 1.0
            j = 0
            acc = 0.0
            for fn, w in A:
                fn()
                acc += w
                target = int(round(acc / tot * len(B)))
                while j < target:
                    B[j]()
                    j += 1
            while j < len(B):
                B[j]()
                j += 1

        def run_skewed(units):
            ns = max(len(x) for x in units)
            for t in range(len(units) + ns - 1):
                for k in reversed(range(ns)):
                    u = t - k
                    if 0 <= u < len(units) and k < len(units[u]):
                        units[u][k]()

        PB = [st0.enter_context(nc.psum_tensor("pb%d" % i, [128, 512], F32)) for i in range(8)]
        PBb = [b[:].bitcast(BF16) for b in PB]
        BK = [Buf(excl=True) for _ in range(8)]

        uid = [0]

        def sbuf(st, name, shape, dt):
            uid[0] += 1
            return st.enter_context(nc.sbuf_tensor("%s_u%d" % (name, uid[0]), list(shape), dt))

        identf = sbuf(st0, "identf", [128, 128], F32); b_idf = Buf()
        identb = sbuf(st0, "identb", [128, 128], BF16); b_idb = Buf()
        ones_f = sbuf(st0, "ones_f", [128, 128], F32); b_1f = Buf()
        ones_b = sbuf(st0, "ones_b", [128, 128], BF16); b_1b = Buf()
        tri = sbuf(st0, "tri", [128, 128], BF16); b_tri = Buf()
        trif = sbuf(st0, "trif", [128, 128], F32); b_trif = Buf()
        negm = sbuf(st0, "negm", [128, 128], F32); b_negm = Buf()
        tab = sbuf(st0, "tab", [128, NT, 112], F32); b_tab = Buf()
        modT = sbuf(st0, "modT", [128, L, 96], F32); b_mod = Buf()
        gT = sbuf(st0, "gT", [128, L, 32], F32); b_gT = Buf()
        AB = sbuf(st0, "AB", [128, L, 32], F32); b_AB = Buf()
        eps_t = sbuf(st0, "eps_t", [128, 1], F32); b_eps = Buf()
        CONST = [b_idf, b_idb, b_1f, b_1b, b_tri, b_negm, b_tab, b_mod, b_AB]

        MS("pool", eps_t[:], EPS, [b_eps])
        MS("pool", identf[:], 1.0, [b_idf])
        P.op("pool", lambda e: e.affine_select(out=identf[:], in_=identf[:], pattern=[[-1, 128]], compare_op=ALU.is_equal, fill=0.0, base=0, channel_multiplier=1), [b_idf], [b_idf])
        CP("dve", identb[:], identf[:], [b_idf], [b_idb])
        MS("pool", ones_f[:], 1.0, [b_1f])
        MS("pool", ones_b[:], 1.0, [b_1b])
        MS("pool", trif[:], 1.0, [b_trif])
        P.op("pool", lambda e: e.affine_select(out=trif[:], in_=trif[:], pattern=[[1, 128]], compare_op=ALU.is_ge, fill=0.0, base=0, channel_multiplier=-1), [b_trif], [b_trif])
        CP("dve", tri[:], trif[:], [b_trif], [b_tri])
        MS("pool", negm[:], 0.0, [b_negm])
        P.op("pool", lambda e: e.affine_select(out=negm[:], in_=negm[:], pattern=[[-1, 128]], compare_op=ALU.is_ge, fill=-1e30, base=0, channel_multiplier=1), [b_negm], [b_negm])

        def rows_to_cols(st, src_rows_ap, nrows, dst, dst_buf, bank, nm):
            rt = sbuf(st, "rt_" + nm, [128, 128], F32); b_rt = Buf()
            DMA("sp", rt[0:nrows, :], src_rows_ap, [], [b_rt])
            TR(PB[bank][:, 0:nrows], rt[0:nrows, :], identf[0:nrows, 0:nrows], [b_rt, b_idf], [BK[bank]])
            CP("dve", dst, PB[bank][:, 0:nrows], [BK[bank]], [dst_buf])

        with ExitStack() as st:
            xt = [sbuf(st, "xt%d" % i, [128, D], F32) for i in range(2)]; b_xt = [Buf(), Buf()]
            xs = [sbuf(st, "xs%d" % i, [128, KC, 128], F32) for i in range(2)]; b_xs = [Buf(), Buf()]
            xTv = xT.rearrange("(kc p) s -> p kc s", p=128)
            for tt in range(NT):
                i = tt % 2
                DMA("sp", xt[i][:], x_in[tt * 128:(tt + 1) * 128, :], [], [b_xt[i]])
                for g4 in range(4):
                    bk = (tt * 4 + g4) % 8
                    for j in range(4):
                        kc = g4 * 4 + j
                        TR(PB[bk][:, j * 128:(j + 1) * 128], xt[i][:, kc * 128:(kc + 1) * 128], identf[:], [b_xt[i], b_idf], [BK[bk]])
                    CP("act" if g4 % 2 else "dve", xs[i][:, g4 * 4:(g4 + 1) * 4, :], PB[bk][:].rearrange("p (j t) -> p j t", j=4), [BK[bk]], [b_xs[i]])
                DMA("sp", xTv[:, :, tt * 128:(tt + 1) * 128], xs[i][:], [b_xs[i]], [])

            posr = sbuf(st, "posr", [16, 128], I32); b_posr = Buf()
            posf = sbuf(st, "posf", [16, 128], F32); b_posf = Buf()
            posT = sbuf(st, "posT", [128, 16], F32); b_posT = Buf()
            invf = sbuf(st, "invf", [128, 56], F32); b_invf = Buf()
            kf = sbuf(st, "kf", [128, NT, 56], F32); b_kf = Buf()
            ki_ = sbuf(st, "ki_", [128, NT, 56], I32); b_ki = Buf()
            DMA("sp", posr[0:NT, :], pos_in.rearrange("(t p) -> t p", p=128), [], [b_posr])
            DMA("sp", invf[:], invf_in[:, :], [], [b_invf])
            CP("dve", posf[0:NT, :], posr[0:NT, :], [b_posr], [b_posf])
            TR(PB[0][:, 0:NT], posf[0:NT, :], identf[0:NT, 0:NT], [b_posf, b_idf], [BK[0]])
            CP("dve", posT[:, 0:NT], PB[0][:, 0:NT], [BK[0]], [b_posT])
            sinv = tab[:, :, 0:56]
            cosv = tab[:, :, 56:112]
            for tt in range(NT):
                TS("dve", tab[:, tt, 0:56], invf[:], posT[:, tt:tt + 1], None, ALU.mult, None, [b_invf, b_posT], [b_tab])
            C1 = 6.28125
            C2 = float(2 * np.pi - 6.28125)
            RW = [b_tab, b_kf, b_ki]
            TS("dve", kf[:], sinv, float(1.0 / (2 * np.pi)), None, ALU.mult, None, RW, RW)
            CP("dve", ki_[:], kf[:], RW, RW)
            CP("dve", kf[:], ki_[:], RW, RW)
            STT("dve", sinv, kf[:], -C1, sinv, ALU.mult, ALU.add, RW, RW)
            STT("dve", sinv, kf[:], -C2, sinv, ALU.mult, ALU.add, RW, RW)
            TS("dve", kf[:], sinv, PI, -TWO_PI, ALU.is_gt, ALU.mult, RW, RW)
            TT("dve", sinv, sinv, kf[:], ALU.add, RW, RW)
            TS("dve", kf[:], sinv, -PI, TWO_PI, ALU.is_lt, ALU.mult, RW, RW)
            TT("dve", sinv, sinv, kf[:], ALU.add, RW, RW)
            TS("dve", cosv, sinv, PI / 2, None, ALU.add, None, RW, RW)
            TS("dve", kf[:], cosv, PI, -TWO_PI, ALU.is_gt, ALU.mult, RW, RW)
            TT("dve", cosv, cosv, kf[:], ALU.add, RW, RW)
            ACT(tab[:], tab[:], AF.Sin, RW, RW)

            cT = sbuf(st, "cT", [128, 16], F32); b_cT = Buf()
            cTb = sbuf(st, "cTb", [128, 16], BF16); b_cTb = Buf()
            rows_to_cols(st, c_in.rearrange("(k p) -> k p", p=128), 16, cT[:], b_cT, 1, "c")
            ACT(cT[:], cT[:], AF.Silu, [b_cT], [b_cT])
            CP("dve", cTb[:], cT[:], [b_cT], [b_cTb])
            for l in range(L):
                rows_to_cols(st, Wd["g_attn"][l].rearrange("(k p) -> k p", p=128), 16, gT[:, l, 0:16], b_gT, 2, "ga%d" % l)
                rows_to_cols(st, Wd["g_ffn"][l].rearrange("(k p) -> k p", p=128), 16, gT[:, l, 16:32], b_gT, 3, "gf%d" % l)
            wsl = [sbuf(st, "adasl%d" % i, [128, KC, 512], BF16) for i in range(2)]; b_wsl = [Buf(), Buf()]
            badaT = sbuf(st, "badaT", [128, 96], F32); b_bada = Buf()
            for l in range(L):
                rows_to_cols(st, Wd["b_ada"][l].rearrange("(k p) -> k p", p=128), 96, badaT[:], b_bada, 4, "ba%d" % l)
                bk = 5 + (l % 2)
                for sl in range(24):
                    i = sl % 2
                    DMA("pool", wsl[i][:], Wd["w_ada"][l][:, sl * 512:(sl + 1) * 512].rearrange("(kc p) n -> p kc n", p=128), [], [b_wsl[i]])
                    for jj in range(4):
                        j = sl * 4 + jj
                        for kc in range(KC):
                            MM(PB[bk][:, j:j + 1], wsl[i][:, kc, jj * 128:(jj + 1) * 128], cTb[:, kc:kc + 1], kc == 0, kc == KC - 1, [b_wsl[i], b_cTb], [BK[bk]])
                TT("dve", modT[:, l, :], PB[bk][:, 0:96], badaT[:], ALU.add, [BK[bk], b_bada], [b_mod])
                STT("dve", AB[:, l, 0:16], modT[:, l, 16:32], 1.0, gT[:, l, 0:16], ALU.add, ALU.mult, [b_mod, b_gT], [b_AB])
                STT("dve", AB[:, l, 16:32], modT[:, l, 64:80], 1.0, gT[:, l, 16:32], ALU.add, ALU.mult, [b_mod, b_gT], [b_AB])
        P.barrier()

        xTv = xT.rearrange("(kc p) s -> p kc s", p=128)

        def norm_phase(st, hT, b_hT, A_ap, B_ap):
            xc = [sbuf(st, "n_xc%d" % i, [128, KC, 512], F32) for i in range(2)]; b_xc = [Buf(), Buf()]
            sq = [sbuf(st, "n_sq%d" % i, [128, 512], F32) for i in range(2)]; b_sq = [Buf(), Buf()]
            rs = sbuf(st, "n_rs", [128, 512], F32); b_rs = Buf()
            tmp = [sbuf(st, "n_tmp%d" % i, [128, 512], F32) for i in range(2)]; b_tmp = [Buf(), Buf()]
            for ch in range(NCH):
                i = ch % 2
                cs = slice(ch * 512, (ch + 1) * 512)
                DMA("sp", xc[i][:], xTv[:, :, cs], [], [b_xc[i]])
                bk = ch % 2
                for kc in range(KC):
                    j = kc % 2
                    ACT(sq[j][:], xc[i][:, kc, :], AF.Square, [b_xc[i]], [b_sq[j]])
                    MM(PB[bk][:], ones_f[:], sq[j][:], kc == 0, kc == KC - 1, [b_1f, b_sq[j]], [BK[bk]])
                CP("dve", rs[:], PB[bk][:], [BK[bk]], [b_rs])
                RSTD(rs[:], 1.0 / D, [b_rs])
                for kc in range(KC):
                    j = kc % 2
                    TT("dve" if kc % 2 else "pool", tmp[j][:], xc[i][:, kc, :], rs[:], ALU.mult, [b_xc[i], b_rs], [b_tmp[j]])
                    ACT(hT[:, kc, cs], tmp[j][:], AF.Identity, [b_tmp[j], b_AB, b_mod], [b_hT], scale=A_ap[:, kc:kc + 1], bias=B_ap[:, kc:kc + 1])

        def bcast_load(st, name, src_row_ap, n):
            t = sbuf(st, name, [128, n], F32)
            b = Buf()
            DMA("sp", t[:], src_row_ap.to_broadcast([128, n]), [], [b])
            return t, b

        def rope(st_tiles, src3, dst3, nh, half, cos_ap, sin_ap, rd, wr):
            ta, tb_, b_t = st_tiles
            cosb = cos_ap.unsqueeze(1).to_broadcast([128, nh, half])
            sinb = sin_ap.unsqueeze(1).to_broadcast([128, nh, half])
            t1 = src3[:, :, 0:half]
            t2 = src3[:, :, half:2 * half]
            av = ta[:, 0:nh * half].rearrange("p (h d) -> p h d", h=nh)
            bv = tb_[:, 0:nh * half].rearrange("p (h d) -> p h d", h=nh)
            TT("pool", av, t1, cosb, ALU.mult, rd + [b_tab], [b_t])
            TT("pool", bv, t2, sinb, ALU.mult, rd + [b_tab], [b_t])
            TT("pool", dst3[:, :, 0:half], av, bv, ALU.subtract, [b_t], wr)
            TT("pool", av, t2, cosb, ALU.mult, rd + [b_tab], [b_t] + wr)
            TT("pool", bv, t1, sinb, ALU.mult, rd + [b_tab], [b_t])
            TT("pool", dst3[:, :, half:2 * half], av, bv, ALU.add, [b_t], wr)

        def make_p3(st, qt_list, banks):
            kiS = sbuf(st, "kiS", [128, S], BF16); b_kiS = Buf()
            qit = [sbuf(st, "qit%d" % i, [128, 8, 128], BF16) for i in range(2)]; b_qit = [Buf(), Buf()]
            sc = [sbuf(st, "sc%d" % i, [128, S], F32) for i in range(2)]; b_sc = [Buf(), Buf()]
            wk = sbuf(st, "wk", [128, S], F32); b_wk = Buf()
            rl = [sbuf(st, "rl%d" % i, [128, 512], BF16) for i in range(3)]; b_rl = [Buf(), Buf(), Buf()]
            wv = [sbuf(st, "wv%d" % i, [128, 16], F32) for i in range(2)]; b_wv = [Buf(), Buf()]
            dg = [sbuf(st, "dg%d" % i, [128, 16, 128], BF16) for i in range(2)]; b_dg = [Buf(), Buf()]
            mx = sbuf(st, "mx", [128, 8], F32); b_mx = Buf()
            Mb = [sbuf(st, "Mb%d" % i, [128, S], BF16) for i in range(2)]; b_Mb = [Buf(), Buf()]
            mstg = [sbuf(st, "mstg%d" % i, [128, 4, 128], BF16) for i in range(2)]; b_mstg = [Buf(), Buf()]
            qiTv = qiT.rearrange("(j p) s -> p j s", p=128)
            cnt = {"s": 0, "a": 0, "t": 0, "r": 0}
            items = []

            def first():
                DMA("sp", kiS[:], kiT2[:, :], [], [b_kiS])
            items.append((first, 0.1))

            def tile_items(qt, i):
                W = (qt + 1) * 128
                qs = slice(qt * 128, (qt + 1) * 128)
                out = []

                def prep():
                    DMA("sp", wv[i][:], wis[qs, :], [], [b_wv[i]])
                    DMA("sp", qit[i][:], qiTv[:, :, qs], [], [b_qit[i]])
                    for h in range(16):
                        TS("pool", dg[i][:, h, :], identb[:], wv[i][:, h:h + 1], None, ALU.mult, None, [b_idb, b_wv[i]], [b_dg[i]])
                out.append((prep, 0.5))

                def chunk(k0, kw):
                    def f():
                        ab = banks["acc"][cnt["a"] % len(banks["acc"])]
                        cnt["a"] += 1
                        sbs = []

                        def dots(h):
                            sb = banks["score"][cnt["s"] % len(banks["score"])]
                            cnt["s"] += 1
                            pr = slice((h % 2) * 64, (h % 2) * 64 + 64)
                            MM(PB[sb][:, 0:kw], qit[i][pr, h // 2, :], kiS[pr, k0:k0 + kw], True, True, [b_qit[i], b_kiS], [BK[sb]])
                            sbs.append(sb)
                        dots(0)
                        for h in range(16):
                            if h + 1 < 16:
                                dots(h + 1)
                            r = cnt["r"] % 3
                            cnt["r"] += 1
                            ACT(rl[r][:, 0:kw], PB[sbs[h]][:, 0:kw], AF.Relu, [BK[sbs[h]]], [b_rl[r]])
                            MM(PB[ab][:, 0:kw], dg[i][:, h, :], rl[r][:, 0:kw], h == 0, h == 15, [b_dg[i], b_rl[r]], [BK[ab]])
                        CP("act", sc[i][:, k0:k0 + kw], PB[ab][:, 0:kw], [BK[ab]], [b_sc[i]])
                    return f
                for k0 in range(0, W, 512):
                    out.append((chunk(k0, min(512, W - k0)), 2.0 * min(512, W - k0) / 512))

                def causal():
                    TT("dve", sc[i][:, qt * 128:W], sc[i][:, qt * 128:W], negm[:], ALU.add, [b_sc[i], b_negm], [b_sc[i]])
                out.append((causal, 0.2))
                if W > n_sel:
                    rounds = n_sel // 8

                    def topk(r0, r1):
                        def f():
                            for r in range(r0, r1):
                                src = sc[i] if r == 0 else wk
                                MAX8(mx[:], src[:, 0:W], [b_sc[i], b_wk], [b_mx])
                                if r < rounds - 1:
                                    MREP(wk[:, 0:W], mx[:], src[:, 0:W], [b_sc[i], b_mx, b_wk], [b_wk])
                        return f
                    for r0 in range(0, rounds, 4):
                        out.append((topk(r0, min(rounds, r0 + 4)), (min(rounds, r0 + 4) - r0) * 2.0 * W / 1000.0))

                    def mask():
                        TS("dve", Mb[i][:, 0:W], sc[i][:, 0:W], mx[:, 7:8], None, ALU.is_ge, None, [b_sc[i], b_mx], [b_Mb[i]])
                else:
                    def mask():
                        TS("dve", Mb[i][:, 0:W], sc[i][:, 0:W], -1e29, None, ALU.is_ge, None, [b_sc[i]], [b_Mb[i]])
                out.append((mask, W / 1000.0))

                def trans(kb0, nb):
                    def f():
                        bk = banks["tr"][cnt["t"] % len(banks["tr"])]
                        sgi = cnt["t"] % 2
                        cnt["t"] += 1
                        for j in range(nb):
                            TR(PBb[bk][:, j * 128:(j + 1) * 128], Mb[i][:, (kb0 + j) * 128:(kb0 + j + 1) * 128], identb[:], [b_Mb[i], b_idb], [BK[bk]])
                        CP("act", mstg[sgi][:, 0:nb, :], PBb[bk][:, 0:nb * 128].rearrange("p (j t) -> p j t", j=nb), [BK[bk]], [b_mstg[sgi]])
                        DMA("sp", MT[kb0 * 128:(kb0 + nb) * 128, qs].rearrange("(j p) t -> p j t", p=128), mstg[sgi][:, 0:nb, :], [b_mstg[sgi]], [])
                    return f
                for kb0 in range(0, qt + 1, 4):
                    out.append((trans(kb0, min(4, qt + 1 - kb0)), 0.5))
                nfront = 1 + len(range(0, W, 512))
                return out[:nfront], out[nfront:]

            fb = [tile_items(qt, n_ % 2) for n_, qt in enumerate(qt_list)]
            if fb:
                items.extend(fb[0][0])
            for n_ in range(len(fb)):
                if n_ + 1 < len(fb):
                    items.extend(fb[n_ + 1][0])
                items.extend(fb[n_][1])
            return items

        for l in range(L):
            with ExitStack() as st:
                hT = sbuf(st, "hT", [128, KC, S], BF16); b_hT = Buf()
                with ExitStack() as st1:
                    norm_phase(st1, hT, b_hT, AB[:, l, 0:16], modT[:, l, 0:16])
                P.barrier()
                slab = [sbuf(st, "slab%d" % i, [128, KC, 512], BF16) for i in range(2)]; b_slab = [Buf(), Buf()]
                gqa, b_gqa = bcast_load(st, "gqa", Wd["g_qa"][l:l + 1, :], 128)
                gka, b_gka = bcast_load(st, "gka", Wd["g_ka"][l:l + 1, :], 128)
                NB = 8
                qn = [sbuf(st, "qn%d" % i, [128, 512], F32) for i in range(NB)]; b_qn = [Buf() for _ in range(NB)]
                qb = [sbuf(st, "qb%d" % i, [128, 512], BF16) for i in range(NB)]; b_qb = [Buf() for _ in range(NB)]
                ssq = [sbuf(st, "ssq%d" % i, [128, 8], F32) for i in range(NB)]; b_ssq = [Buf() for _ in range(NB)]
                junk = [sbuf(st, "junk%d" % i, [128, 128], F32) for i in range(2)]; b_junk = [Buf(), Buf()]
                rts = [(sbuf(st, "rt_a%d" % i, [128, 64], F32), sbuf(st, "rt_b%d" % i, [128, 64], F32), Buf()) for i in range(4)]
                stg = [sbuf(st, "stg%d" % i, [128, 4, 128], BF16) for i in range(4)]; b_stg = [Buf() for _ in range(4)]
                ki2 = [sbuf(st, "ki2_%d" % i, [128, 128], BF16) for i in range(4)]; b_ki2 = [Buf() for _ in range(4)]
                slab_n = [0]

                def load_slab(col_ranges):
                    i = slab_n[0] % 2
                    slab_n[0] += 1
                    off = 0
                    for (c0, n) in col_ranges:
                        DMA("pool", slab[i][:, :, off:off + n], Wd["w_in"][l][:, c0:c0 + n].rearrange("(kc p) n -> p kc n", p=128), [], [b_slab[i]])
                        off += n
                    return i, off

                def tok_mm(i, ncols, tt, bk):
                    for kc in range(KC):
                        MM(PB[bk][:, 0:ncols], hT[:, kc, tt * 128:(tt + 1) * 128], slab[i][:, kc, 0:ncols], kc == 0, kc == KC - 1, [b_hT, b_slab[i]], [BK[bk]])

                units = []

                def qk_unit(u, i, tt, nh, g_t, b_g, dst_rows, nblk, has_va):
                    b = u % NB; bk = u % 3; tb = 4 + u % 4; sg = u % 4; rt = rts[u % 4]; jk = u % 2
                    ts_ = slice(tt * 128, (tt + 1) * 128)

                    def s0():
                        tok_mm(i, 512, tt, bk)

                    def s1():
                        CP("act", qn[b][:], PB[bk][:], [BK[bk]], [b_qn[b]])

                    def s2():
                        for h in range(nh):
                            ACT(junk[jk][:], qn[b][:, h * 128:(h + 1) * 128], AF.Square, [b_qn[b]], [b_junk[jk], b_ssq[b]], accum_out=ssq[b][:, h:h + 1])

                    def s3():
                        ACT(ssq[b][:, 0:nh], ssq[b][:, 0:nh], AF.Sqrt, [b_ssq[b], b_eps], [b_ssq[b]], scale=1.0 / 128, bias=eps_t[:, 0:1])

                    def s4():
                        RECIP(ssq[b][:, 0:nh], ssq[b][:, 0:nh], [b_ssq[b]], [b_ssq[b]])

                    def s5():
                        for h in range(nh):
                            hs = slice(h * 128, (h + 1) * 128)
                            STT("dve", qn[b][:, hs], qn[b][:, hs], ssq[b][:, h:h + 1], g_t[:], ALU.mult, ALU.mult, [b_qn[b], b_ssq[b], b_g], [b_qn[b]])

                    def s6():
                        CP("pool", qb[b][:], qn[b][:], [b_qn[b]], [b_qb[b]])
                        s3_ = qn[b][:, 0:nh * 128].rearrange("p (h d) -> p h d", h=nh)[:, :, 0:32]
                        d3_ = qb[b][:, 0:nh * 128].rearrange("p (h d) -> p h d", h=nh)[:, :, 0:32]
                        rope(rt, s3_, d3_, nh, 16, tab[:, tt, 56:72], tab[:, tt, 0:16], [b_qn[b]], [b_qb[b]])

                    def s7():
                        for j in range(nblk):
                            TR(PBb[tb][:, j * 128:(j + 1) * 128], qb[b][:, j * 128:(j + 1) * 128], identb[:], [b_qb[b], b_idb], [BK[tb]])

                    def s8():
                        CP("act", stg[sg][:, 0:nblk, :], PBb[tb][:, 0:nblk * 128].rearrange("p (j t) -> p j t", j=nblk), [BK[tb]], [b_stg[sg]])
                        DMA("sp", dst_rows[0:nblk * 128, ts_].rearrange("(j p) t -> p j t", p=128), stg[sg][:, 0:nblk, :], [b_stg[sg]], [])
                        if has_va:
                            DMA("sp", va[ts_, :], qb[b][:, 256:512], [b_qb[b]], [])
                    return [s0, s1, s2, s3, s4, s5, s6, s7, s8]

                def qi_unit(u, i, tt, dst_rows):
                    b = u % NB; bk = u % 3; tb = 4 + u % 4; sg = u % 4; rt = rts[u % 4]
                    ts_ = slice(tt * 128, (tt + 1) * 128)

                    def s0():
                        tok_mm(i, 512, tt, bk)

                    def s1():
                        CP("act", qn[b][:], PB[bk][:], [BK[bk]], [b_qn[b]])

                    def s2():
                        CP("pool", qb[b][:], qn[b][:], [b_qn[b]], [b_qb[b]])
                        s3_ = qn[b][:].rearrange("p (h d) -> p h d", h=8)[:, :, 0:16]
                        d3_ = qb[b][:].rearrange("p (h d) -> p h d", h=8)[:, :, 0:16]
                        rope(rt, s3_, d3_, 8, 8, tab[:, tt, 72:80], tab[:, tt, 16:24], [b_qn[b]], [b_qb[b]])

                    def s3():
                        for j in range(4):
                            TR(PBb[tb][:, j * 128:(j + 1) * 128], qb[b][:, j * 128:(j + 1) * 128], identb[:], [b_qb[b], b_idb], [BK[tb]])

                    def s4():
                        CP("act", stg[sg][:, 0:4, :], PBb[tb][:, 0:512].rearrange("p (j t) -> p j t", j=4), [BK[tb]], [b_stg[sg]])
                        DMA("sp", dst_rows[0:512, ts_].rearrange("(j p) t -> p j t", p=128), stg[sg][:, 0:4, :], [b_stg[sg]], [])
                    nop = lambda: None
                    return [s0, s1, nop, nop, nop, nop, s2, s3, s4]

                def misc_unit(u, i, tt):
                    b = u % NB; bk = u % 3; tb = 4 + u % 4; sg = u % 4; rt = rts[u % 4]; kk = u % 4
                    ts_ = slice(tt * 128, (tt + 1) * 128)

                    def s0():
                        tok_mm(i, 144, tt, bk)

                    def s1():
                        CP("act", qn[b][:, 0:144], PB[bk][:, 0:144], [BK[bk]], [b_qn[b]])

                    def s2():
                        CP("pool", ki2[kk][:, 0:64], qn[b][:, 0:64], [b_qn[b]], [b_ki2[kk]])
                        s3_ = qn[b][:, 0:64].rearrange("p (h d) -> p h d", h=1)[:, :, 0:16]
                        d3_ = ki2[kk][:, 0:64].rearrange("p (h d) -> p h d", h=1)[:, :, 0:16]
                        rope(rt, s3_, d3_, 1, 8, tab[:, tt, 72:80], tab[:, tt, 16:24], [b_qn[b]], [b_ki2[kk]])
                        CP("pool", ki2[kk][:, 64:128], ki2[kk][:, 0:64], [b_ki2[kk]], [b_ki2[kk]])
                        TS("pool", qn[b][:, 64:80], qn[b][:, 64:80], float(1024 ** -0.5), None, ALU.mult, None, [b_qn[b]], [b_qn[b]])

                    def s3():
                        TR(PBb[tb][:, 0:128], ki2[kk][:], identb[:], [b_ki2[kk], b_idb], [BK[tb]])
                        DMA("sp", wis[ts_, :], qn[b][:, 64:80], [b_qn[b]], [])
                        DMA("sp", mkr[ts_, :], qn[b][:, 80:144], [b_qn[b]], [])

                    def s4():
                        CP("act", stg[sg][:, 0, :], PBb[tb][:, 0:128], [BK[tb]], [b_stg[sg]])
                        DMA("sp", kiT2[:, ts_], stg[sg][:, 0, :], [b_stg[sg]], [])
                    nop = lambda: None
                    return [s0, s1, nop, nop, nop, nop, s2, s3, s4]

                groups = [("qa", [(0, 512)]), ("qa", [(512, 512)]), ("ka", [(1024, 512)]), ("qi", [(1536, 512)]), ("qi", [(2048, 512)]), ("misc", [(2560, 80), (3408, 64)])]

                def slab_load(gi):
                    i = gi % 2
                    off = 0
                    for (c0, n) in groups[gi][1]:
                        DMA("pool", slab[i][:, :, off:off + n], Wd["w_in"][l][:, c0:c0 + n].rearrange("(kc p) n -> p kc n", p=128), [], [b_slab[i]])
                        off += n

                def with_pre(stages, pre):
                    s0 = stages[0]

                    def s0p():
                        pre()
                        s0()
                    return [s0p] + stages[1:]

                u = 0
                for gi, (kind, cr) in enumerate(groups):
                    i = gi % 2
                    for tt in range(NT):
                        if kind == "qa":
                            stages = qk_unit(u, i, tt, 4, gqa, b_gqa, qaT[gi * 512:(gi + 1) * 512, :], 4, False)
                        elif kind == "ka":
                            stages = qk_unit(u, i, tt, 2, gka, b_gka, kaT, 2, True)
                        elif kind == "qi":
                            stages = qi_unit(u, i, tt, qiT[(gi - 3) * 512:(gi - 2) * 512, :])
                        else:
                            stages = misc_unit(u, i, tt)
                        if tt == 0:
                            def pre(gi=gi):
                                if gi == 0:
                                    slab_load(0)
                                if gi + 1 < len(groups):
                                    slab_load(gi + 1)
                            stages = with_pre(stages, pre)
                        units.append(stages)
                        u += 1
                run_skewed(units)
                slab_n[0] = len(groups)
                lat = sbuf(st, "lat", [128, 4, 512], F32); b_lat = Buf()
                latb = [sbuf(st, "latb%d" % i, [128, 512], BF16) for i in range(2)]; b_latb = [Buf(), Buf()]
                lsq = [sbuf(st, "lsq%d" % i, [128, 512], F32) for i in range(2)]; b_lsq = [Buf(), Buf()]
                lrs = sbuf(st, "lrs", [128, 512], F32); b_lrs = Buf()
                glatT = sbuf(st, "glatT", [128, 8], F32); b_glat = Buf()
                rows_to_cols(st, Wd["g_mq_lat"][l].rearrange("(k p) -> k p", p=128), 4, glatT[:, 0:4], b_glat, 6, "gmq")
                rows_to_cols(st, Wd["g_mkv_lat"][l].rearrange("(k p) -> k p", p=128), 2, glatT[:, 4:6], b_glat, 6, "gmkv")
                for (c0, nm, dst, goff) in ((2640, 4, mqlT, 0), (3152, 2, mkvlT, 4)):
                    i, _ = load_slab([(c0, nm * 128)])
                    for ch in range(NCH):
                        cs = slice(ch * 512, (ch + 1) * 512)
                        for m in range(nm):
                            bk = m % 4
                            for kc in range(KC):
                                MM(PB[bk][:], slab[i][:, kc, m * 128:(m + 1) * 128], hT[:, kc, cs], kc == 0, kc == KC - 1, [b_slab[i], b_hT], [BK[bk]])
                            CP("dve", lat[:, m, :], PB[bk][:], [BK[bk]], [b_lat])
                            ACT(lsq[m % 2][:], PB[bk][:], AF.Square, [BK[bk]], [b_lsq[m % 2]])
                            MM(PB[4][:], ones_f[:], lsq[m % 2][:], m == 0, m == nm - 1, [b_1f, b_lsq[m % 2]], [BK[4]])
                        CP("dve", lrs[:], PB[4][:], [BK[4]], [b_lrs])
                        RSTD(lrs[:], 1.0 / (nm * 128), [b_lrs])
                        for m in range(nm):
                            STT("dve", latb[m % 2][:], lat[:, m, :], glatT[:, goff + m:goff + m + 1], lrs[:], ALU.mult, ALU.mult, [b_lat, b_glat, b_lrs], [b_latb[m % 2]])
                            DMA("sp", dst[m * 128:(m + 1) * 128, cs], latb[m % 2][:], [b_latb[m % 2]], [])
                sg = [sbuf(st, "sg%d" % i, [128, 512], F32) for i in range(2)]; b_sg = [Buf(), Buf()]
                n = 0
                for (c0, dst) in ((3472, sgaT), (5520, sgbT)):
                    for s4 in range(4):
                        i, _ = load_slab([(c0 + s4 * 512, 512)])
                        for ch in range(NCH):
                            cs = slice(ch * 512, (ch + 1) * 512)
                            for m in range(4):
                                bk = n % 4
                                for kc in range(KC):
                                    MM(PB[bk][:], slab[i][:, kc, m * 128:(m + 1) * 128], hT[:, kc, cs], kc == 0, kc == KC - 1, [b_slab[i], b_hT], [BK[bk]])
                                ACT(sg[n % 2][:], PB[bk][:], AF.Sigmoid, [BK[bk]], [b_sg[n % 2]])
                                r0 = s4 * 512 + m * 128
                                DMA("sp", dst[r0:r0 + 128, cs], sg[n % 2][:], [b_sg[n % 2]], [])
                                n += 1
            P.barrier()

            with ExitStack() as st:
                mqlS = sbuf(st, "mqlS", [128, 4, S], BF16); b_mqlS = Buf()
                mkvlS = sbuf(st, "mkvlS", [128, 2, S], BF16); b_mkvlS = Buf()
                wq = sbuf(st, "wq", [128, 4, 1536], BF16); b_wq = Buf()
                wkv = sbuf(st, "wkv", [128, 2, 2048], BF16); b_wkv = Buf()
                DMA("sp", mqlS[:], mqlT.rearrange("(kc p) s -> p kc s", p=128), [], [b_mqlS])
                DMA("sp", mkvlS[:], mkvlT.rearrange("(kc p) s -> p kc s", p=128), [], [b_mkvlS])
                DMA("pool", wq[:], Wd["w_mq_up"][l].rearrange("(kc p) n -> p kc n", p=128), [], [b_wq])
                DMA("pool", wkv[:], Wd["w_mkv_up"][l].rearrange("(kc p) n -> p kc n", p=128), [], [b_wkv])
                gqm, b_gqm = bcast_load(st, "gqm", Wd["g_qm"][l:l + 1, :], 192)
                gkm, b_gkm = bcast_load(st, "gkm", Wd["g_km"][l:l + 1, :], 192)
                NB = 8
                qn = [sbuf(st, "m_qn%d" % i, [128, 512], F32) for i in range(NB)]; b_qn = [Buf() for _ in range(NB)]
                qb = [sbuf(st, "m_qb%d" % i, [128, 384], BF16) for i in range(NB)]; b_qb = [Buf() for _ in range(NB)]
                kb_ = [sbuf(st, "m_kb%d" % i, [128, 384], BF16) for i in range(NB)]; b_kb = [Buf() for _ in range(NB)]
                kr = [sbuf(st, "m_kr%d" % i, [128, 128], F32) for i in range(NB)]; b_kr = [Buf() for _ in range(NB)]
                vb = [sbuf(st, "m_vb%d" % i, [128, 256], BF16) for i in range(NB)]; b_vb = [Buf() for _ in range(NB)]
                ssq = [sbuf(st, "m_ssq%d" % i, [128, 4], F32) for i in range(NB)]; b_ssq = [Buf() for _ in range(NB)]
                junk = [sbuf(st, "m_junk%d" % i, [128, 192], F32) for i in range(2)]; b_junk = [Buf(), Buf()]
                rts = [(sbuf(st, "m_rt_a%d" % i, [128, 64], F32), sbuf(st, "m_rt_b%d" % i, [128, 64], F32), Buf()) for i in range(4)]
                stg = [sbuf(st, "m_stg%d" % i, [128, 2, 128], BF16) for i in range(4)]; b_stg = [Buf() for _ in range(4)]
                stgr = [sbuf(st, "m_stgr%d" % i, [64, 2, 128], BF16) for i in range(4)]; b_stgr = [Buf() for _ in range(4)]
                mkr_t = [sbuf(st, "m_mkr%d" % i, [128, 64], F32) for i in range(3)]; b_mkr = [Buf() for _ in range(3)]
                ssr = [sbuf(st, "m_ssr%d" % i, [128, 1], F32) for i in range(3)]; b_ssr = [Buf() for _ in range(3)]

                def mq_unit(u, tt, g):
                    b = u % NB; bk = 6 + u % 2; tb = 4 + u % 2; sg = u % 4; rt = rts[u % 4]; jk = u % 2
                    ts_ = slice(tt * 128, (tt + 1) * 128)

                    def s0():
                        for kc in range(4):
                            MM(PB[bk][:, 0:384], mqlS[:, kc, ts_], wq[:, kc, g * 384:(g + 1) * 384], kc == 0, kc == 3, [b_mqlS, b_wq], [BK[bk]])

                    def s1():
                        CP("act", qn[b][:, 0:384], PB[bk][:, 0:384], [BK[bk]], [b_qn[b]])

                    def s2():
                        for h in range(2):
                            ACT(junk[jk][:, 0:192], qn[b][:, h * 192:(h + 1) * 192], AF.Square, [b_qn[b]], [b_junk[jk], b_ssq[b]], accum_out=ssq[b][:, h:h + 1])

                    def s3():
                        pass

                    def s4():
                        ACT(ssq[b][:, 0:2], ssq[b][:, 0:2], AF.Sqrt, [b_ssq[b], b_eps], [b_ssq[b]], scale=1.0 / 192, bias=eps_t[:, 0:1])

                    def s5():
                        RECIP(ssq[b][:, 0:2], ssq[b][:, 0:2], [b_ssq[b]], [b_ssq[b]])

                    def s6():
                        for h in range(2):
                            hs = slice(h * 192, (h + 1) * 192)
                            STT("dve", qn[b][:, hs], qn[b][:, hs], ssq[b][:, h:h + 1], gqm[:], ALU.mult, ALU.mult, [b_qn[b], b_ssq[b], b_gqm], [b_qn[b]])

                    def s7():
                        CP("pool", qb[b][:], qn[b][:, 0:384], [b_qn[b]], [b_qb[b]])
                        s3_ = qn[b][:, 0:384].rearrange("p (h d) -> p h d", h=2)[:, :, 128:192]
                        d3_ = qb[b][:].rearrange("p (h d) -> p h d", h=2)[:, :, 128:192]
                        rope(rt, s3_, d3_, 2, 32, tab[:, tt, 80:112], tab[:, tt, 24:56], [b_qn[b]], [b_qb[b]])

                    def s8():
                        for h in range(2):
                            TR(PBb[tb][:, h * 128:(h + 1) * 128], qb[b][:, h * 192:h * 192 + 128], identb[:], [b_qb[b], b_idb], [BK[tb]])
                            TR(PBb[tb][0:64, 256 + h * 128:256 + (h + 1) * 128], qb[b][:, h * 192 + 128:(h + 1) * 192], identb[:], [b_qb[b], b_idb], [BK[tb]])

                    def s9():
                        CP("act", stg[sg][:, :, :], PBb[tb][:, 0:256].rearrange("p (j t) -> p j t", j=2), [BK[tb]], [b_stg[sg]])
                        CP("act", stgr[sg][:, :, :], PBb[tb][0:64, 256:512].rearrange("p (j t) -> p j t", j=2), [BK[tb]], [b_stgr[sg]])
                        DMA("sp", mqTn[g * 256:(g + 1) * 256, ts_].rearrange("(j p) t -> p j t", p=128), stg[sg][:, :, :], [b_stg[sg]], [])
                        DMA("sp", mqTr[g * 128:(g + 1) * 128, ts_].rearrange("(j p) t -> p j t", p=64), stgr[sg][:, :, :], [b_stgr[sg]], [])
                    return [s0, s1, s2, s3, s4, s5, s6, s7, s8, s9]

                def mkv_unit(u, tt, g):
                    b = u % NB; bk = 6 + u % 2; tb = 4 + u % 2; sg = u % 4; rt = rts[u % 4]; jk = u % 2; tr3 = tt % 3
                    ts_ = slice(tt * 128, (tt + 1) * 128)

                    def s0():
                        for kc in range(2):
                            MM(PB[bk][:], mkvlS[:, kc, ts_], wkv[:, kc, g * 512:(g + 1) * 512], kc == 0, kc == 1, [b_mkvlS, b_wkv], [BK[bk]])
                        if g == 0:
                            DMA("sp", mkr_t[tr3][:], mkr[ts_, :], [], [b_mkr[tr3]])

                    def s1():
                        CP("act", qn[b][:], PB[bk][:], [BK[bk]], [b_qn[b]])

                    def s2():
                        if g == 0:
                            ACT(junk[jk][:, 0:64], mkr_t[tr3][:], AF.Square, [b_mkr[tr3]], [b_junk[jk], b_ssr[tr3]], accum_out=ssr[tr3][:, 0:1])
                        for h in range(2):
                            ACT(junk[jk][:, 0:128], qn[b][:, h * 256:h * 256 + 128], AF.Square, [b_qn[b]], [b_junk[jk], b_ssq[b]], accum_out=ssq[b][:, h:h + 1])

                    def s3():
                        TS("dve", ssq[b][:, 0:2], ssq[b][:, 0:2], ssr[tr3][:, 0:1], None, ALU.add, None, [b_ssq[b], b_ssr[tr3]], [b_ssq[b]])

                    def s4():
                        ACT(ssq[b][:, 0:2], ssq[b][:, 0:2], AF.Sqrt, [b_ssq[b], b_eps], [b_ssq[b]], scale=1.0 / 192, bias=eps_t[:, 0:1])

                    def s5():
                        RECIP(ssq[b][:, 0:2], ssq[b][:, 0:2], [b_ssq[b]], [b_ssq[b]])

                    def s6():
                        for h in range(2):
                            STT("dve", kb_[b][:, h * 128:(h + 1) * 128], qn[b][:, h * 256:h * 256 + 128], ssq[b][:, h:h + 1], gkm[:, 0:128], ALU.mult, ALU.mult, [b_qn[b], b_ssq[b], b_gkm], [b_kb[b]])
                            STT("dve", kr[b][:, h * 64:(h + 1) * 64], mkr_t[tr3][:], ssq[b][:, h:h + 1], gkm[:, 128:192], ALU.mult, ALU.mult, [b_mkr[tr3], b_ssq[b], b_gkm], [b_kr[b]])

                    def s7():
                        s3_ = kr[b][:].rearrange("p (h d) -> p h d", h=2)
                        d3_ = kb_[b][:, 256:384].rearrange("p (h d) -> p h d", h=2)
                        rope(rt, s3_, d3_, 2, 32, tab[:, tt, 80:112], tab[:, tt, 24:56], [b_kr[b]], [b_kb[b]])
                        CP("pool", vb[b][:].rearrange("p (h d) -> p h d", h=2), qn[b][:].rearrange("p (h d) -> p h d", h=2)[:, :, 128:256], [b_qn[b]], [b_vb[b]])

                    def s8():
                        for h in range(2):
                            TR(PBb[tb][:, h * 128:(h + 1) * 128], kb_[b][:, h * 128:(h + 1) * 128], identb[:], [b_kb[b], b_idb], [BK[tb]])
                            TR(PBb[tb][0:64, 256 + h * 128:256 + (h + 1) * 128], kb_[b][:, 256 + h * 64:256 + (h + 1) * 64], identb[:], [b_kb[b], b_idb], [BK[tb]])
                        DMA("sp", mv[ts_, g * 256:(g + 1) * 256], vb[b][:], [b_vb[b]], [])

                    def s9():
                        CP("act", stg[sg][:, :, :], PBb[tb][:, 0:256].rearrange("p (j t) -> p j t", j=2), [BK[tb]], [b_stg[sg]])
                        CP("act", stgr[sg][:, :, :], PBb[tb][0:64, 256:512].rearrange("p (j t) -> p j t", j=2), [BK[tb]], [b_stgr[sg]])
                        DMA("sp", mkTn[g * 256:(g + 1) * 256, ts_].rearrange("(j p) t -> p j t", p=128), stg[sg][:, :, :], [b_stg[sg]], [])
                        DMA("sp", mkTr[g * 128:(g + 1) * 128, ts_].rearrange("(j p) t -> p j t", p=64), stgr[sg][:, :, :], [b_stgr[sg]], [])
                    return [s0, s1, s2, s3, s4, s5, s6, s7, s8, s9]

                units = []
                u = 0
                for tt in range(NT):
                    for g in range(4):
                        units.append(mq_unit(u, tt, g)); u += 1
                    for g in range(4):
                        units.append(mkv_unit(u, tt, g)); u += 1
                p3 = make_p3(st, list(range(NT)), dict(score=[0, 1], acc=[2], tr=[3]))
                interleave(p3, skewed_steps(units))
            P.barrier()

            with ExitStack() as st:
                kaS = sbuf(st, "kaS", [128, 2, S], BF16); b_kaS = Buf()
                vaS = sbuf(st, "vaS", [128, NT, 256], BF16); b_vaS = Buf()
                DMA("sp", kaS[:], kaT.rearrange("(g p) s -> p g s", p=128), [], [b_kaS])
                DMA("sp", vaS[:], va.rearrange("(t p) c -> p t c", p=128), [], [b_vaS])
                mkn = sbuf(st, "mkn", [128, 8, S], BF16); b_mkn = Buf()
                mkrS = sbuf(st, "mkrS", [64, 8, S], BF16); b_mkrS = Buf()
                mvS = sbuf(st, "mvS", [128, NT, 1024], BF16); b_mvS = Buf()
                DMA("sp", mkn[:], mkTn.rearrange("(h p) s -> p h s", p=128), [], [b_mkn])
                DMA("sp", mkrS[:], mkTr.rearrange("(h p) s -> p h s", p=64), [], [b_mkrS])
                DMA("sp", mvS[:], mv.rearrange("(t p) c -> p t c", p=128), [], [b_mvS])
                qT_ = [sbuf(st, "a_q%d" % i, [128, 512], BF16) for i in range(3)]; b_q = [Buf(), Buf(), Buf()]
                qR_ = [sbuf(st, "a_qr%d" % i, [64, 512], BF16) for i in range(3)]; b_qr = [Buf(), Buf(), Buf()]
                mts = [sbuf(st, "a_mts%d" % i, [128, NT, 512], BF16) for i in range(2)]; b_mts = [Buf(), Buf()]
                pT = [sbuf(st, "a_pT%d" % i, [128, 512], BF16) for i in range(4)]; b_pT = [Buf() for _ in range(4)]
                rd_ = sbuf(st, "a_rd", [128, 512], F32); b_rd = Buf()
                ob = [sbuf(st, "a_ob%d" % i, [128, 512], BF16) for i in range(2)]; b_ob = [Buf(), Buf()]
                hcount = [0]
                icount = [0]
                heads_all = [(qc_, h_) for qc_ in range(NCH) for h_ in range(16)]

                def load_q(g):
                    if g >= len(heads_all):
                        return
                    qc_, h_ = heads_all[g]
                    cs_ = slice(qc_ * 512, (qc_ + 1) * 512)
                    isA_ = h_ < 8
                    hh_ = h_ if isA_ else h_ - 8
                    i_ = g % 3
                    if isA_:
                        DMA("sp", qT_[i_][:], qaT[hh_ * 128:(hh_ + 1) * 128, cs_], [], [b_q[i_]])
                    else:
                        DMA("sp", qT_[i_][:], mqTn[hh_ * 128:(hh_ + 1) * 128, cs_], [], [b_q[i_]])
                        DMA("sp", qR_[i_][:], mqTr[hh_ * 64:(hh_ + 1) * 64, cs_], [], [b_qr[i_]])
                load_q(0)
                for qc in range(NCH):
                    cs = slice(qc * 512, (qc + 1) * 512)
                    nkb = 4 * qc + 4
                    mi = qc % 2
                    if qc == 0:
                        DMA("sp", mts[0][:, 0:4, :], MT[0:512, 0:512].rearrange("(k p) t -> p k t", p=128), [], [b_mts[0]])
                    if qc + 1 < NCH:
                        nkb2 = 4 * (qc + 1) + 4
                        DMA("sp", mts[(qc + 1) % 2][:, 0:nkb2, :], MT[0:nkb2 * 128, (qc + 1) * 512:(qc + 2) * 512].rearrange("(k p) t -> p k t", p=128), [], [b_mts[(qc + 1) % 2]])
                    items = []
                    for h in range(16):
                        hq = hcount[0]
                        hcount[0] += 1
                        for kb in range(nkb):
                            items.append((h, kb, hq, icount[0]))
                            icount[0] += 1

                    def a_scores(it, qc=qc, cs=cs):
                        h, kb, hq, n = it
                        isA = h < 8
                        hh = h if isA else h - 8
                        i = hq % 3
                        if kb == 0:
                            load_q(hq + 1)
                        q0 = max(0, kb - 4 * qc) * 128
                        ks = slice(kb * 128, (kb + 1) * 128)
                        bk = n % 4
                        if isA:
                            MM(PB[bk][:, q0:512], kaS[:, hh // 4, ks], qT_[i][:, q0:512], True, True, [b_kaS, b_q[i]], [BK[bk]])
                        else:
                            MM(PB[bk][:, q0:512], mkn[:, hh, ks], qT_[i][:, q0:512], True, False, [b_mkn, b_q[i]], [BK[bk]])
                            MM(PB[bk][:, q0:512], mkrS[:, hh, ks], qR_[i][:, q0:512], False, True, [b_mkrS, b_qr[i]], [BK[bk]])

                    def a_rest(it, qc=qc, cs=cs, nkb=nkb, mi=mi):
                        h, kb, hq, n = it
                        isA = h < 8
                        hh = h if isA else h - 8
                        scale = float(128 ** -0.5) if isA else float(192 ** -0.5)
                        q0 = max(0, kb - 4 * qc) * 128
                        bk = n % 4
                        j = n % 4
                        bo = 4 + (hq % 2) * 2
                        ACT(pT[j][:, q0:512], PB[bk][:, q0:512], AF.Exp, [BK[bk]], [b_pT[j]], scale=scale)
                        if isA:
                            TT("dve" if n % 2 else "pool", pT[j][:, q0:512], pT[j][:, q0:512], mts[mi][:, kb, q0:512], ALU.mult, [b_pT[j], b_mts[mi]], [b_pT[j]])
                            vv = vaS[:, kb, (hh // 4) * 128:(hh // 4 + 1) * 128]
                            b_v = b_vaS
                        else:
                            if kb >= 4 * qc:
                                TT("dve" if n % 2 else "pool", pT[j][:, q0:q0 + 128], pT[j][:, q0:q0 + 128], tri[:], ALU.mult, [b_pT[j], b_tri], [b_pT[j]])
                            vv = mvS[:, kb, hh * 128:(hh + 1) * 128]
                            b_v = b_mvS
                        MM(PB[bo][:, q0:512], vv, pT[j][:, q0:512], kb == 0, kb == nkb - 1, [b_v, b_pT[j]], [BK[bo]])
                        MM(PB[bo + 1][:, q0:512], ones_b[:], pT[j][:, q0:512], kb == 0, kb == nkb - 1, [b_1b, b_pT[j]], [BK[bo + 1]])
                        if kb == nkb - 1:
                            i2 = hq % 2
                            RECIP(rd_[:], PB[bo + 1][:], [BK[bo + 1]], [b_rd])
                            TT("dve", ob[i2][:], PB[bo][:], rd_[:], ALU.mult, [BK[bo], b_rd], [b_ob[i2]])
                            dst = oaT if isA else obT
                            DMA("sp", dst[hh * 128:(hh + 1) * 128, cs], ob[i2][:], [b_ob[i2]], [])

                    DEPTH_PF = 2
                    for k in range(min(DEPTH_PF, len(items))):
                        a_scores(items[k])
                    for k in range(len(items)):
                        if k + DEPTH_PF < len(items):
                            a_scores(items[k + DEPTH_PF])
                        a_rest(items[k])
            P.barrier()

            with ExitStack() as st:
                mT_ = sbuf(st, "mT_", [128, KC, S], BF16); b_mT = Buf()
                with ExitStack() as st1:
                    oaS = sbuf(st1, "oaS", [128, 8, S], BF16); b_oaS = Buf()
                    obS = sbuf(st1, "obS", [128, 8, S], BF16); b_obS = Buf()
                    DMA("sp", oaS[:], oaT.rearrange("(k p) s -> p k s", p=128), [], [b_oaS])
                    DMA("sp", obS[:], obT.rearrange("(k p) s -> p k s", p=128), [], [b_obS])
                    wpa = [sbuf(st1, "wpa%d" % i, [128, 8, 128], BF16) for i in range(2)]; b_wpa = [Buf(), Buf()]
                    wpb = [sbuf(st1, "wpb%d" % i, [128, 8, 128], BF16) for i in range(2)]; b_wpb = [Buf(), Buf()]
                    ga = [sbuf(st1, "ga%d" % i, [128, 512], F32) for i in range(4)]; b_ga = [Buf() for _ in range(4)]
                    gb = [sbuf(st1, "gb%d" % i, [128, 512], F32) for i in range(4)]; b_gb = [Buf() for _ in range(4)]
                    t1 = [sbuf(st1, "p5t%d" % i, [128, 512], F32) for i in range(4)]; b_t1 = [Buf() for _ in range(4)]
                    n = 0
                    for m in range(KC):
                        i = m % 2
                        ms = slice(m * 128, (m + 1) * 128)
                        DMA("pool", wpa[i][:], Wd["w_pa"][l][:, ms].rearrange("(k p) n -> p k n", p=128), [], [b_wpa[i]])
                        DMA("pool", wpb[i][:], Wd["w_pb"][l][:, ms].rearrange("(k p) n -> p k n", p=128), [], [b_wpb[i]])
                        for ch in range(NCH):
                            cs = slice(ch * 512, (ch + 1) * 512)
                            ba = (n % 4) * 2
                            for k in range(8):
                                MM(PB[ba][:], wpa[i][:, k, :], oaS[:, k, cs], k == 0, k == 7, [b_wpa[i], b_oaS], [BK[ba]])
                            for k in range(8):
                                MM(PB[ba + 1][:], wpb[i][:, k, :], obS[:, k, cs], k == 0, k == 7, [b_wpb[i], b_obS], [BK[ba + 1]])
                            j = n % 4
                            DMA("sp", ga[j][:], sgaT[ms, cs], [], [b_ga[j]])
                            DMA("sp", gb[j][:], sgbT[ms, cs], [], [b_gb[j]])
                            TT("dve", t1[j][:], PB[ba][:], ga[j][:], ALU.mult, [BK[ba], b_ga[j]], [b_t1[j]])
                            TT("dve", ga[j][:], PB[ba + 1][:], gb[j][:], ALU.mult, [BK[ba + 1], b_gb[j], b_ga[j]], [b_ga[j]])
                            TT("pool", mT_[:, m, cs], t1[j][:], ga[j][:], ALU.add, [b_t1[j], b_ga[j]], [b_mT])
                            n += 1
                P.barrier()
                with ExitStack() as st1:
                    wo = [sbuf(st1, "wo%d" % i, [128, KC, 128], BF16) for i in range(2)]; b_wo = [Buf(), Buf()]
                    xm = [sbuf(st1, "xm%d" % i, [128, S], F32) for i in range(3)]; b_xm = [Buf(), Buf(), Buf()]
                    n = 0
                    DMA("sp", xm[0][:], xT[0:128, :], [], [b_xm[0]])
                    for m in range(KC):
                        i = m % 2
                        ms = slice(m * 128, (m + 1) * 128)
                        DMA("pool", wo[i][:], Wd["w_o"][l][:, ms].rearrange("(k p) n -> p k n", p=128), [], [b_wo[i]])
                        if m + 1 < KC:
                            DMA("sp", xm[(m + 1) % 3][:], xT[(m + 1) * 128:(m + 2) * 128, :], [], [b_xm[(m + 1) % 3]])
                        i3 = m % 3
                        for ch in range(NCH):
                            cs = slice(ch * 512, (ch + 1) * 512)
                            bk = n % 4
                            for k in range(KC):
                                MM(PB[bk][:], wo[i][:, k, :], mT_[:, k, cs], k == 0, k == KC - 1, [b_wo[i], b_mT], [BK[bk]])
                            STT("dve", xm[i3][:, cs], PB[bk][:], modT[:, l, 32 + m:33 + m], xm[i3][:, cs], ALU.mult, ALU.add, [BK[bk], b_mod, b_xm[i3]], [b_xm[i3]])
                            n += 1
                        DMA("sp", xT[ms, :], xm[i3][:], [b_xm[i3]], [])
            P.barrier()

            with ExitStack() as st:
                hT = sbuf(st, "h2T", [128, KC, S], BF16); b_hT = Buf()
                with ExitStack() as st1:
                    norm_phase(st1, hT, b_hT, AB[:, l, 16:32], modT[:, l, 48:64])
                P.barrier()
                wcT = sbuf(st, "wcT", [128, 3, 88], F32); b_wc = Buf()
                bcT = sbuf(st, "bcT", [128, 88], F32); b_bc = Buf()
                for t in range(3):
                    rows_to_cols(st, Wd["w_conv"][l][t].rearrange("(k p) -> k p", p=128), 88, wcT[:, t, :], b_wc, 7, "wc%d" % t)
                rows_to_cols(st, Wd["b_conv"][l].rearrange("(k p) -> k p", p=128), 88, bcT[:], b_bc, 7, "bc")
                wu = [sbuf(st, "wu%d" % i, [128, KC, 256], BF16) for i in range(2)]; b_wu = [Buf(), Buf()]
                ug = [sbuf(st, "ug%d" % i, [128, S + 2], F32) for i in range(2)]; b_ug = [Buf(), Buf()]
                uv = [sbuf(st, "uv%d" % i, [128, S + 2], F32) for i in range(2)]; b_uv = [Buf(), Buf()]
                cg = [sbuf(st, "cg%d" % i, [128, S], F32) for i in range(2)]; b_cg = [Buf(), Buf()]
                cv = [sbuf(st, "cv%d" % i, [128, S], F32) for i in range(2)]; b_cv = [Buf(), Buf()]
                ab = [sbuf(st, "ab%d" % i, [128, S], BF16) for i in range(2)]; b_ab = [Buf(), Buf()]
                for i in range(2):
                    MS("pool", ug[i][:, 0:2], 0.0, [b_ug[i]])
                    MS("pool", uv[i][:, 0:2], 0.0, [b_uv[i]])
                n = 0
                for j in range(44):
                    i = j % 2
                    DMA("pool", wu[i][:, :, 0:128], Wd["w_up"][l][:, j * 128:(j + 1) * 128].rearrange("(k p) n -> p k n", p=128), [], [b_wu[i]])
                    DMA("pool", wu[i][:, :, 128:256], Wd["w_up"][l][:, DFF + j * 128:DFF + (j + 1) * 128].rearrange("(k p) n -> p k n", p=128), [], [b_wu[i]])
                    for ch in range(NCH):
                        cs2 = slice(2 + ch * 512, 2 + (ch + 1) * 512)
                        cs = slice(ch * 512, (ch + 1) * 512)
                        bg = (n % 4) * 2
                        for k in range(KC):
                            MM(PB[bg][:], wu[i][:, k, 0:128], hT[:, k, cs], k == 0, k == KC - 1, [b_wu[i], b_hT], [BK[bg]])
                        for k in range(KC):
                            MM(PB[bg + 1][:], wu[i][:, k, 128:256], hT[:, k, cs], k == 0, k == KC - 1, [b_wu[i], b_hT], [BK[bg + 1]])
                        CP("act", ug[i][:, cs2], PB[bg][:], [BK[bg]], [b_ug[i]])
                        CP("dve", uv[i][:, cs2], PB[bg + 1][:], [BK[bg + 1]], [b_uv[i]])
                        n += 1
                    for (u_, c_, b_u, b_c, col, e1, e2) in ((ug[i], cg[i], b_ug[i], b_cg[i], j, "pool", "dve"), (uv[i], cv[i], b_uv[i], b_cv[i], 44 + j, "dve", "pool")):
                        ACT(c_[:], u_[:, 2:S + 2], AF.Identity, [b_u, b_wc, b_bc], [b_c], scale=wcT[:, 2, col:col + 1], bias=bcT[:, col:col + 1])
                        STT(e1, c_[:], u_[:, 1:S + 1], wcT[:, 1, col:col + 1], c_[:], ALU.mult, ALU.add, [b_u, b_wc, b_c], [b_c])
                        STT(e2, c_[:], u_[:, 0:S], wcT[:, 0, col:col + 1], c_[:], ALU.mult, ALU.add, [b_u, b_wc, b_c], [b_c])
                    ACT(cg[i][:], cg[i][:], AF.Silu, [b_cg[i]], [b_cg[i]])
                    TT("dve", ab[i][:], cg[i][:], cv[i][:], ALU.mult, [b_cg[i], b_cv[i]], [b_ab[i]])
                    DMA("sp", actT[j * 128:(j + 1) * 128, :], ab[i][:], [b_ab[i]], [])
            P.barrier()

            with ExitStack() as st:
                TC = min(1024, S)
                aS = sbuf(st, "aS", [128, 44, TC], BF16); b_aS = Buf()
                wd = [sbuf(st, "wd%d" % i, [128, 44, 128], BF16) for i in range(2)]; b_wd = [Buf(), Buf()]
                xm = [sbuf(st, "xd%d" % i, [128, TC], F32) for i in range(3)]; b_xm = [Buf(), Buf(), Buf()]
                its = [(c0, m) for c0 in range(0, S, TC) for m in range(KC)]

                def p8_load(n_):
                    if n_ < len(its):
                        c0_, m_ = its[n_]
                        DMA("sp", xm[n_ % 3][:], xT[m_ * 128:(m_ + 1) * 128, c0_:c0_ + TC], [], [b_xm[n_ % 3]])
                p8_load(0)
                for n, (c0, m) in enumerate(its):
                    if m == 0:
                        DMA("sp", aS[:], actT[:, c0:c0 + TC].rearrange("(k p) s -> p k s", p=128), [], [b_aS])
                    i = n % 2
                    i3 = n % 3
                    ms = slice(m * 128, (m + 1) * 128)
                    DMA("pool", wd[i][:], Wd["w_down"][l][:, ms].rearrange("(k p) n -> p k n", p=128), [], [b_wd[i]])
                    p8_load(n + 1)
                    for s0 in range(0, TC, 512):
                        bk = (n * 2 + s0 // 512) % 8
                        for k in range(44):
                            MM(PB[bk][:], wd[i][:, k, :], aS[:, k, s0:s0 + 512], k == 0, k == 43, [b_wd[i], b_aS], [BK[bk]])
                        STT("dve", xm[i3][:, s0:s0 + 512], PB[bk][:], modT[:, l, 80 + m:81 + m], xm[i3][:, s0:s0 + 512], ALU.mult, ALU.add, [BK[bk], b_mod, b_xm[i3]], [b_xm[i3]])
                    DMA("sp", xT[ms, c0:c0 + TC], xm[i3][:], [b_xm[i3]], [])
            P.barrier()

        with ExitStack() as st:
            xc = [sbuf(st, "f_xc%d" % i, [128, KC, 128], F32) for i in range(2)]; b_xc = [Buf(), Buf()]
            ot = [sbuf(st, "f_ot%d" % i, [128, D], F32) for i in range(2)]; b_ot = [Buf(), Buf()]
            for tt in range(NT):
                i = tt % 2
                DMA("sp", xc[i][:], xTv[:, :, tt * 128:(tt + 1) * 128], [], [b_xc[i]])
                for g4 in range(4):
                    bk = (tt * 4 + g4) % 8
                    for j in range(4):
                        kc = g4 * 4 + j
                        TR(PB[bk][:, j * 128:(j + 1) * 128], xc[i][:, kc, :], identf[:], [b_xc[i], b_idf], [BK[bk]])
                    CP("act" if g4 % 2 else "dve", ot[i][:, g4 * 512:(g4 + 1) * 512], PB[bk][:], [BK[bk]], [b_ot[i]])
                DMA("sp", out[tt * 128:(tt + 1) * 128, :], ot[i][:], [b_ot[i]], [])
        for k, ap in dbg_out.items():
            src = {"xT": xT, "qaT": qaT, "kaT": kaT, "va": va, "qiT": qiT, "kiT2": kiT2, "wis": wis, "mkr": mkr, "mqlT": mqlT,
                   "mkvlT": mkvlT, "sgaT": sgaT, "sgbT": sgbT, "mqTn": mqTn, "mqTr": mqTr, "mkTn": mkTn, "mkTr": mkTr, "mv": mv,
                   "MT": MT, "oaT": oaT, "obT": obT, "actT": actT}[k]
            P.dma("sp", lambda e, ap=ap, src=src: e.dma_start(out=ap, in_=src), (), ())
        P.barrier()
        with nc.Block() as block:
            P.emit(block)
    return nc


_NC_CACHE = {}


def kernel(**inputs):
    S = inputs["x"].shape[1]
    B = inputs["x"].shape[0]
    L = inputs["w_in"].shape[0]
    n_sel = min(256, S // 4)
    key = (S, L, n_sel)
    if key not in _NC_CACHE:
        _NC_CACHE[key] = build(S, L, n_sel)
    nc = _NC_CACHE[key]
    invf = inv_freq_table()
    shared = {k: np.ascontiguousarray(np.asarray(inputs[k], dtype=np.float32)) for k in W_SHAPES}
    in_maps = []
    for b in range(B):
        m = dict(shared)
        m["x"] = np.ascontiguousarray(np.asarray(inputs["x"][b], dtype=np.float32))
        m["c"] = np.ascontiguousarray(np.asarray(inputs["c"][b], dtype=np.float32))
        m["positions"] = np.ascontiguousarray(np.asarray(inputs["positions"][b], dtype=np.int32))
        m["invf"] = invf
        in_maps.append(m)
    res = run_bass_kernel_spmd(nc, in_maps, core_ids=list(range(B)))
    return np.stack([np.asarray(r["out"], dtype=np.float32) for r in res.results], axis=0)
```

```python
from contextlib import ExitStack
import numpy as np
import concourse.bass as bass
import concourse.mybir as mybir
from concourse.bass_utils import run_bass_kernel_spmd

F32 = mybir.dt.float32
BF16 = mybir.dt.bfloat16
I32 = mybir.dt.int32
AF = mybir.ActivationFunctionType
ALU = mybir.AluOpType

ENGS = ("pe", "act", "dve", "pool", "sp")
NRING = 12

D = 2048
KC = 16
IN_DIM = 7568
DFF = 5632
EPS = 1e-6
PI = float(np.pi)
TWO_PI = float(2 * np.pi)


class Buf:
    __slots__ = ("w", "r", "excl")

    def __init__(self, excl=False):
        self.w = None
        self.r = {}
        self.excl = excl


class Prog:
    def __init__(self, nc, stack):
        self.nc = nc
        self.q = {e: [] for e in ENGS}
        self.cnt = {e: 0 for e in ENGS}
        self.dcnt = {e: 0 for e in ENGS}
        self.seen = {e: {} for e in ENGS}
        self.sem = {}
        for e in ENGS:
            self.sem[("c", e)] = stack.enter_context(nc.semaphore("c_" + e))
        for e in ("sp", "pool", "act"):
            for r in range(NRING):
                self.sem[("d", e, r)] = stack.enter_context(nc.semaphore("d_%s_%d" % (e, r)))

    def _need(self, eng, waits, tok):
        if tok is None:
            return
        key, val = tok
        if eng == "pe" and key == ("c", "pe"):
            return
        if self.seen[eng].get(key, 0) >= val:
            return
        self.seen[eng][key] = val
        waits.append((key, val))

    def _deps(self, eng, reads, writes):
        waits = []
        own = ("c", eng)
        for b in reads:
            self._need(eng, waits, b.w)
            if b.excl:
                for k, v in b.r.items():
                    if k != own:
                        self._need(eng, waits, (k, v))
        for b in writes:
            self._need(eng, waits, b.w)
            for k, v in b.r.items():
                self._need(eng, waits, (k, v))
        return waits

    def _mark(self, tok, reads, writes):
        k, v = tok
        for b in reads:
            if b.r.get(k, 0) < v:
                b.r[k] = v
        for b in writes:
            b.w = tok
            b.r = {}

    def op(self, eng, fn, reads=(), writes=()):
        waits = self._deps(eng, reads, writes)
        self.cnt[eng] += 1
        tok = (("c", eng), self.cnt[eng])
        self.q[eng].append((fn, waits, tok[0], 1))
        self._mark(tok, reads, writes)

    def dma(self, eng, fn, reads=(), writes=()):
        waits = self._deps(eng, reads, writes)
        j = self.dcnt[eng]
        self.dcnt[eng] += 1
        slot = j % NRING
        use = j // NRING
        key = ("d", eng, slot)
        if use > 0:
            self._need(eng, waits, (key, 16 * use))
        tok = (key, 16 * (use + 1))
        self.q[eng].append((fn, waits, key, 16))
        self._mark(tok, reads, writes)

    def wait_all(self, eng):
        waits = []
        for e in ENGS:
            if self.cnt[e] > 0:
                self._need(eng, waits, (("c", e), self.cnt[e]))
        for e in ("sp", "pool", "act"):
            j = self.dcnt[e]
            for slot in range(NRING):
                if j > slot:
                    uses = (j - slot + NRING - 1) // NRING
                    self._need(eng, waits, (("d", e, slot), 16 * uses))
        self.q[eng].append((None, waits, None, 0))

    def barrier(self):
        for e in ENGS:
            self.wait_all(e)

    def _replay(self, name, e):
        sem = self.sem
        for fn, waits, key, inc in self.q[name]:
            for k, v in waits:
                e.wait_ge(sem[k], v)
            if fn is not None:
                fn(e).then_inc(sem[key], inc)

    def emit(self, block):
        P = self

        @block.tensor
        def _(e):
            P._replay("pe", e)

        @block.scalar
        def _(e):
            P._replay("act", e)

        @block.vector
        def _(e):
            P._replay("dve", e)

        @block.gpsimd
        def _(e):
            P._replay("pool", e)

        @block.sync
        def _(e):
            P._replay("sp", e)


W_SHAPES = {
    "g_attn": (D,), "g_ffn": (D,), "w_ada": (D, 6 * D), "b_ada": (6 * D,), "w_in": (D, IN_DIM),
    "g_qa": (128,), "g_ka": (128,), "g_mq_lat": (512,), "w_mq_up": (512, 1536), "g_mkv_lat": (256,),
    "w_mkv_up": (256, 2048), "g_qm": (192,), "g_km": (192,), "w_pa": (1024, D), "w_pb": (1024, D),
    "w_o": (D, D), "w_up": (D, 2 * DFF), "w_conv": (3, 2 * DFF), "b_conv": (2 * DFF,), "w_down": (DFF, D),
}


def inv_freq_table():
    def inv(rot):
        return (500000.0 ** (-np.arange(0, rot, 2, dtype=np.float32) / np.float32(rot))).astype(np.float32)
    t = np.concatenate([inv(32), inv(16), inv(64)]).astype(np.float32)
    return np.ascontiguousarray(np.broadcast_to(t[None, :], (128, 56))).astype(np.float32)


def build(S, L, n_sel, dbg=()):
    NT = S // 128
    NCH = S // 512
    nc = bass.Bass("TRN2", target_bir_lowering=False)

    def din(name, shape, dt=F32):
        return nc.dram_tensor(name, list(shape), dt, kind="ExternalInput").ap()

    def dscr(name, shape, dt):
        return nc.dram_tensor(name, list(shape), dt, kind="Internal").ap()

    x_in = din("x", (S, D))
    c_in = din("c", (D,))
    pos_in = din("positions", (S,), I32)
    invf_in = din("invf", (128, 56))
    Wd = {k: din(k, (L,) + v) for k, v in W_SHAPES.items()}
    out = nc.dram_tensor("out", [S, D], F32, kind="ExternalOutput").ap()
    dbg_out = {k: nc.dram_tensor("dbg_" + k, list(shp), dt, kind="ExternalOutput").ap() for k, (shp, dt) in dbg}

    xT = dscr("xT", (D, S), F32)
    qaT = dscr("qaT", (1024, S), BF16)
    kaT = dscr("kaT", (256, S), BF16)
    va = dscr("va", (S, 256), BF16)
    qiT = dscr("qiT", (1024, S), BF16)
    kiT2 = dscr("kiT2", (128, S), BF16)
    wis = dscr("wis", (S, 16), F32)
    mkr = dscr("mkr", (S, 64), F32)
    mqlT = dscr("mqlT", (512, S), BF16)
    mkvlT = dscr("mkvlT", (256, S), BF16)
    sgaT = dscr("sgaT", (D, S), F32)
    sgbT = dscr("sgbT", (D, S), F32)
    mqTn = dscr("mqTn", (1024, S), BF16)
    mqTr = dscr("mqTr", (512, S), BF16)
    mkTn = dscr("mkTn", (1024, S), BF16)
    mkTr = dscr("mkTr", (512, S), BF16)
    mv = dscr("mv", (S, 1024), BF16)
    MT = dscr("MT", (S, S), BF16)
    oaT = dscr("oaT", (1024, S), BF16)
    obT = dscr("obT", (1024, S), BF16)
    actT = dscr("actT", (DFF, S), BF16)

    with ExitStack() as st0:
        P = Prog(nc, st0)

        def MM(o, lhsT, rhs, start, stop, rd, wr):
            P.op("pe", lambda e: e.matmul(o, lhsT=lhsT, rhs=rhs, start=start, stop=stop), rd, wr)

        def TR(o, in_, ident, rd, wr):
            P.op("pe", lambda e: e.transpose(out=o, in_=in_, identity=ident), rd, wr)

        def ACT(o, in_, func, rd, wr, **kw):
            P.op("act", lambda e: e.activation(out=o, in_=in_, func=func, **kw), rd, wr)

        def TS(eng, o, in0, s1, s2, op0, op1, rd, wr):
            if s2 is None:
                P.op(eng, lambda e: e.tensor_scalar(out=o, in0=in0, scalar1=s1, scalar2=None, op0=op0), rd, wr)
            else:
                P.op(eng, lambda e: e.tensor_scalar(out=o, in0=in0, scalar1=s1, scalar2=s2, op0=op0, op1=op1), rd, wr)

        def TT(eng, o, in0, in1, op, rd, wr):
            P.op(eng, lambda e: e.tensor_tensor(out=o, in0=in0, in1=in1, op=op), rd, wr)

        def STT(eng, o, in0, scalar, in1, op0, op1, rd, wr):
            eng = "dve"
            P.op(eng, lambda e: e.scalar_tensor_tensor(out=o, in0=in0, scalar=scalar, in1=in1, op0=op0, op1=op1), rd, wr)

        def CP(eng, o, in_, rd, wr):
            if eng == "act":
                P.op("act", lambda e: e.copy(out=o, in_=in_), rd, wr)
            else:
                P.op(eng, lambda e: e.tensor_copy(out=o, in_=in_), rd, wr)

        def MS(eng, o, val, wr):
            P.op(eng, lambda e: e.memset(o, val), (), wr)

        def DMA(eng, o, in_, rd, wr):
            P.dma(eng, lambda e: e.dma_start(out=o, in_=in_), rd, wr)

        def MAX8(o, in_, rd, wr):
            P.op("dve", lambda e: e.max(out=o, in_=in_), rd, wr)

        def MREP(o, rep, vals, rd, wr):
            P.op("dve", lambda e: e.match_replace(out=o, in_to_replace=rep, in_values=vals, imm_value=-1e30), rd, wr)

        def RECIP(o, in_, rd, wr):
            P.op("dve", lambda e: e.reciprocal(out=o, in_=in_), rd, wr)

        def RSTD(t, inv_n, rd_wr):
            TS("dve", t, t, inv_n, EPS, ALU.mult, ALU.add, rd_wr, rd_wr)
            ACT(t, t, AF.Sqrt, rd_wr, rd_wr)
            P.op("dve", lambda e: e.reciprocal(out=t, in_=t), rd_wr, rd_wr)

        def skewed_steps(units):
            ns = max(len(x) for x in units)
            steps = []
            for t in range(len(units) + ns - 1):
                def step(t=t):
                    for k in reversed(range(ns)):
                        u = t - k
                        if 0 <= u < len(units) and k < len(units[u]):
                            units[u][k]()
                steps.append(step)
            return steps

        def interleave(A, B):
            tot = sum(w for _, w in A) or 1.0
            j = 0
            acc = 0.0
            for fn, w in A:
                fn()
                acc += w
                target = int(round(acc / tot * len(B)))
                while j < target:
                    B[j]()
                    j += 1
            while j < len(B):
                B[j]()
                j += 1

        def run_skewed(units):
            ns = max(len(x) for x in units)
            for t in range(len(units) + ns - 1):
                for k in reversed(range(ns)):
                    u = t - k
                    if 0 <= u < len(units) and k < len(units[u]):
                        units[u][k]()

        PB = [st0.enter_context(nc.psum_tensor("pb%d" % i, [128, 512], F32)) for i in range(8)]
        PBb = [b[:].bitcast(BF16) for b in PB]
        BK = [Buf(excl=True) for _ in range(8)]

        uid = [0]

        def sbuf(st, name, shape, dt):
            uid[0] += 1
            return st.enter_context(nc.sbuf_tensor("%s_u%d" % (name, uid[0]), list(shape), dt))

        identf = sbuf(st0, "identf", [128, 128], F32); b_idf = Buf()
        identb = sbuf(st0, "identb", [128, 128], BF16); b_idb = Buf()
        ones_f = sbuf(st0, "ones_f", [128, 128], F32); b_1f = Buf()
        ones_b = sbuf(st0, "ones_b", [128, 128], BF16); b_1b = Buf()
        tri = sbuf(st0, "tri", [128, 128], BF16); b_tri = Buf()
        trif = sbuf(st0, "trif", [128, 128], F32); b_trif = Buf()
        negm = sbuf(st0, "negm", [128, 128], F32); b_negm = Buf()
        tab = sbuf(st0, "tab", [128, NT, 112], F32); b_tab = Buf()
        modT = sbuf(st0, "modT", [128, L, 96], F32); b_mod = Buf()
        gT = sbuf(st0, "gT", [128, L, 32], F32); b_gT = Buf()
        AB = sbuf(st0, "AB", [128, L, 32], F32); b_AB = Buf()
        eps_t = sbuf(st0, "eps_t", [128, 1], F32); b_eps = Buf()
        CONST = [b_idf, b_idb, b_1f, b_1b, b_tri, b_negm, b_tab, b_mod, b_AB]

        MS("pool", eps_t[:], EPS, [b_eps])
        MS("pool", identf[:], 1.0, [b_idf])
        P.op("pool", lambda e: e.affine_select(out=identf[:], in_=identf[:], pattern=[[-1, 128]], compare_op=ALU.is_equal, fill=0.0, base=0, channel_multiplier=1), [b_idf], [b_idf])
        CP("dve", identb[:], identf[:], [b_idf], [b_idb])
        MS("pool", ones_f[:], 1.0, [b_1f])
        MS("pool", ones_b[:], 1.0, [b_1b])
        MS("pool", trif[:], 1.0, [b_trif])
        P.op("pool", lambda e: e.affine_select(out=trif[:], in_=trif[:], pattern=[[1, 128]], compare_op=ALU.is_ge, fill=0.0, base=0, channel_multiplier=-1), [b_trif], [b_trif])
        CP("dve", tri[:], trif[:], [b_trif], [b_tri])
        MS("pool", negm[:], 0.0, [b_negm])
        P.op("pool", lambda e: e.affine_select(out=negm[:], in_=negm[:], pattern=[[-1, 128]], compare_op=ALU.is_ge, fill=-1e30, base=0, channel_multiplier=1), [b_negm], [b_negm])

        def rows_to_cols(st, src_rows_ap, nrows, dst, dst_buf, bank, nm):
            rt = sbuf(st, "rt_" + nm, [128, 128], F32); b_rt = Buf()
            DMA("sp", rt[0:nrows, :], src_rows_ap, [], [b_rt])
            TR(PB[bank][:, 0:nrows], rt[0:nrows, :], identf[0:nrows, 0:nrows], [b_rt, b_idf], [BK[bank]])
            CP("dve", dst, PB[bank][:, 0:nrows], [BK[bank]], [dst_buf])

        with ExitStack() as st:
            xt = [sbuf(st, "xt%d" % i, [128, D], F32) for i in range(2)]; b_xt = [Buf(), Buf()]
            xs = [sbuf(st, "xs%d" % i, [128, KC, 128], F32) for i in range(2)]; b_xs = [Buf(), Buf()]
            xTv = xT.rearrange("(kc p) s -> p kc s", p=128)
            for tt in range(NT):
                i = tt % 2
                DMA("sp", xt[i][:], x_in[tt * 128:(tt + 1) * 128, :], [], [b_xt[i]])
                for g4 in range(4):
                    bk = (tt * 4 + g4) % 8
                    for j in range(4):
                        kc = g4 * 4 + j
                        TR(PB[bk][:, j * 128:(j + 1) * 128], xt[i][:, kc * 128:(kc + 1) * 128], identf[:], [b_xt[i], b_idf], [BK[bk]])
                    CP("act" if g4 % 2 else "dve", xs[i][:, g4 * 4:(g4 + 1) * 4, :], PB[bk][:].rearrange("p (j t) -> p j t", j=4), [BK[bk]], [b_xs[i]])
                DMA("sp", xTv[:, :, tt * 128:(tt + 1) * 128], xs[i][:], [b_xs[i]], [])

            posr = sbuf(st, "posr", [16, 128], I32); b_posr = Buf()
            posf = sbuf(st, "posf", [16, 128], F32); b_posf = Buf()
            posT = sbuf(st, "posT", [128, 16], F32); b_posT = Buf()
            invf = sbuf(st, "invf", [128, 56], F32); b_invf = Buf()
            kf = sbuf(st, "kf", [128, NT, 56], F32); b_kf = Buf()
            ki_ = sbuf(st, "ki_", [128, NT, 56], I32); b_ki = Buf()
            DMA("sp", posr[0:NT, :], pos_in.rearrange("(t p) -> t p", p=128), [], [b_posr])
            DMA("sp", invf[:], invf_in[:, :], [], [b_invf])
            CP("dve", posf[0:NT, :], posr[0:NT, :], [b_posr], [b_posf])
            TR(PB[0][:, 0:NT], posf[0:NT, :], identf[0:NT, 0:NT], [b_posf, b_idf], [BK[0]])
            CP("dve", posT[:, 0:NT], PB[0][:, 0:NT], [BK[0]], [b_posT])
            sinv = tab[:, :, 0:56]
            cosv = tab[:, :, 56:112]
            for tt in range(NT):
                TS("dve", tab[:, tt, 0:56], invf[:], posT[:, tt:tt + 1], None, ALU.mult, None, [b_invf, b_posT], [b_tab])
            C1 = 6.28125
            C2 = float(2 * np.pi - 6.28125)
            RW = [b_tab, b_kf, b_ki]
            TS("dve", kf[:], sinv, float(1.0 / (2 * np.pi)), None, ALU.mult, None, RW, RW)
            CP("dve", ki_[:], kf[:], RW, RW)
            CP("dve", kf[:], ki_[:], RW, RW)
            STT("dve", sinv, kf[:], -C1, sinv, ALU.mult, ALU.add, RW, RW)
            STT("dve", sinv, kf[:], -C2, sinv, ALU.mult, ALU.add, RW, RW)
            TS("dve", kf[:], sinv, PI, -TWO_PI, ALU.is_gt, ALU.mult, RW, RW)
            TT("dve", sinv, sinv, kf[:], ALU.add, RW, RW)
            TS("dve", kf[:], sinv, -PI, TWO_PI, ALU.is_lt, ALU.mult, RW, RW)
            TT("dve", sinv, sinv, kf[:], ALU.add, RW, RW)
            TS("dve", cosv, sinv, PI / 2, None, ALU.add, None, RW, RW)
            TS("dve", kf[:], cosv, PI, -TWO_PI, ALU.is_gt, ALU.mult, RW, RW)
            TT("dve", cosv, cosv, kf[:], ALU.add, RW, RW)
            ACT(tab[:], tab[:], AF.Sin, RW, RW)

            cT = sbuf(st, "cT", [128, 16], F32); b_cT = Buf()
            cTb = sbuf(st, "cTb", [128, 16], BF16); b_cTb = Buf()
            rows_to_cols(st, c_in.rearrange("(k p) -> k p", p=128), 16, cT[:], b_cT, 1, "c")
            ACT(cT[:], cT[:], AF.Silu, [b_cT], [b_cT])
            CP("dve", cTb[:], cT[:], [b_cT], [b_cTb])
            for l in range(L):
                rows_to_cols(st, Wd["g_attn"][l].rearrange("(k p) -> k p", p=128), 16, gT[:, l, 0:16], b_gT, 2, "ga%d" % l)
                rows_to_cols(st, Wd["g_ffn"][l].rearrange("(k p) -> k p", p=128), 16, gT[:, l, 16:32], b_gT, 3, "gf%d" % l)
            wsl = [sbuf(st, "adasl%d" % i, [128, KC, 512], BF16) for i in range(2)]; b_wsl = [Buf(), Buf()]
            badaT = sbuf(st, "badaT", [128, 96], F32); b_bada = Buf()
            for l in range(L):
                rows_to_cols(st, Wd["b_ada"][l].rearrange("(k p) -> k p", p=128), 96, badaT[:], b_bada, 4, "ba%d" % l)
                bk = 5 + (l % 2)
                for sl in range(24):
                    i = sl % 2
                    DMA("pool", wsl[i][:], Wd["w_ada"][l][:, sl * 512:(sl + 1) * 512].rearrange("(kc p) n -> p kc n", p=128), [], [b_wsl[i]])
                    for jj in range(4):
                        j = sl * 4 + jj
                        for kc in range(KC):
                            MM(PB[bk][:, j:j + 1], wsl[i][:, kc, jj * 128:(jj + 1) * 128], cTb[:, kc:kc + 1], kc == 0, kc == KC - 1, [b_wsl[i], b_cTb], [BK[bk]])
                TT("dve", modT[:, l, :], PB[bk][:, 0:96], badaT[:], ALU.add, [BK[bk], b_bada], [b_mod])
                STT("dve", AB[:, l, 0:16], modT[:, l, 16:32], 1.0, gT[:, l, 0:16], ALU.add, ALU.mult, [b_mod, b_gT], [b_AB])
                STT("dve", AB[:, l, 16:32], modT[:, l, 64:80], 1.0, gT[:, l, 16:32], ALU.add, ALU.mult, [b_mod, b_gT], [b_AB])
        P.barrier()

        xTv = xT.rearrange("(kc p) s -> p kc s", p=128)

        def norm_phase(st, hT, b_hT, A_ap, B_ap):
            xc = [sbuf(st, "n_xc%d" % i, [128, KC, 512], F32) for i in range(2)]; b_xc = [Buf(), Buf()]
            sq = [sbuf(st, "n_sq%d" % i, [128, 512], F32) for i in range(2)]; b_sq = [Buf(), Buf()]
            rs = sbuf(st, "n_rs", [128, 512], F32); b_rs = Buf()
            tmp = [sbuf(st, "n_tmp%d" % i, [128, 512], F32) for i in range(2)]; b_tmp = [Buf(), Buf()]
            for ch in range(NCH):
                i = ch % 2
                cs = slice(ch * 512, (ch + 1) * 512)
                DMA("sp", xc[i][:], xTv[:, :, cs], [], [b_xc[i]])
                bk = ch % 2
                for kc in range(KC):
                    j = kc % 2
                    ACT(sq[j][:], xc[i][:, kc, :], AF.Square, [b_xc[i]], [b_sq[j]])
                    MM(PB[bk][:], ones_f[:], sq[j][:], kc == 0, kc == KC - 1, [b_1f, b_sq[j]], [BK[bk]])
                CP("dve", rs[:], PB[bk][:], [BK[bk]], [b_rs])
                RSTD(rs[:], 1.0 / D, [b_rs])
                for kc in range(KC):
                    j = kc % 2
                    TT("dve" if kc % 2 else "pool", tmp[j][:], xc[i][:, kc, :], rs[:], ALU.mult, [b_xc[i], b_rs], [b_tmp[j]])
                    ACT(hT[:, kc, cs], tmp[j][:], AF.Identity, [b_tmp[j], b_AB, b_mod], [b_hT], scale=A_ap[:, kc:kc + 1], bias=B_ap[:, kc:kc + 1])

        def bcast_load(st, name, src_row_ap, n):
            t = sbuf(st, name, [128, n], F32)
            b = Buf()
            DMA("sp", t[:], src_row_ap.to_broadcast([128, n]), [], [b])
            return t, b

        def rope(st_tiles, src3, dst3, nh, half, cos_ap, sin_ap, rd, wr, eng="pool"):
            ta, tb_, b_t = st_tiles
            cosb = cos_ap.unsqueeze(1).to_broadcast([128, nh, half])
            sinb = sin_ap.unsqueeze(1).to_broadcast([128, nh, half])
            t1 = src3[:, :, 0:half]
            t2 = src3[:, :, half:2 * half]
            av = ta[:, 0:nh * half].rearrange("p (h d) -> p h d", h=nh)
            bv = tb_[:, 0:nh * half].rearrange("p (h d) -> p h d", h=nh)
            TT(eng, av, t1, cosb, ALU.mult, rd + [b_tab], [b_t])
            TT(eng, bv, t2, sinb, ALU.mult, rd + [b_tab], [b_t])
            TT(eng, dst3[:, :, 0:half], av, bv, ALU.subtract, [b_t], wr)
            TT(eng, av, t2, cosb, ALU.mult, rd + [b_tab], [b_t] + wr)
            TT(eng, bv, t1, sinb, ALU.mult, rd + [b_tab], [b_t])
            TT(eng, dst3[:, :, half:2 * half], av, bv, ALU.add, [b_t], wr)

        def make_p3(st, qt_list, banks):
            kiS = sbuf(st, "kiS", [128, S], BF16); b_kiS = Buf()
            qit = [sbuf(st, "qit%d" % i, [128, 8, 128], BF16) for i in range(2)]; b_qit = [Buf(), Buf()]
            sc = [sbuf(st, "sc%d" % i, [128, S], F32) for i in range(2)]; b_sc = [Buf(), Buf()]
            wk = sbuf(st, "wk", [128, S], F32); b_wk = Buf()
            rl = [sbuf(st, "rl%d" % i, [128, 512], BF16) for i in range(3)]; b_rl = [Buf(), Buf(), Buf()]
            wv = [sbuf(st, "wv%d" % i, [128, 16], F32) for i in range(2)]; b_wv = [Buf(), Buf()]
            dg = [sbuf(st, "dg%d" % i, [128, 16, 128], BF16) for i in range(2)]; b_dg = [Buf(), Buf()]
            mx = sbuf(st, "mx", [128, 8], F32); b_mx = Buf()
            Mb = [sbuf(st, "Mb%d" % i, [128, S], BF16) for i in range(2)]; b_Mb = [Buf(), Buf()]
            mstg = [sbuf(st, "mstg%d" % i, [128, 4, 128], BF16) for i in range(2)]; b_mstg = [Buf(), Buf()]
            qiTv = qiT.rearrange("(j p) s -> p j s", p=128)
            cnt = {"s": 0, "a": 0, "t": 0, "r": 0}
            items = []

            def first():
                DMA("sp", kiS[:], kiT2[:, :], [], [b_kiS])
            items.append((first, 0.1))

            def tile_items(qt, i):
                W = (qt + 1) * 128
                qs = slice(qt * 128, (qt + 1) * 128)
                out = []

                def prep():
                    DMA("sp", wv[i][:], wis[qs, :], [], [b_wv[i]])
                    DMA("sp", qit[i][:], qiTv[:, :, qs], [], [b_qit[i]])
                    for h in range(16):
                        TS("pool", dg[i][:, h, :], identb[:], wv[i][:, h:h + 1], None, ALU.mult, None, [b_idb, b_wv[i]], [b_dg[i]])
                out.append((prep, 0.5))

                def chunk(k0, kw):
                    def f():
                        ab = banks["acc"][cnt["a"] % len(banks["acc"])]
                        cnt["a"] += 1
                        sbs = []

                        def dots(h):
                            sb = banks["score"][cnt["s"] % len(banks["score"])]
                            cnt["s"] += 1
                            pr = slice((h % 2) * 64, (h % 2) * 64 + 64)
                            MM(PB[sb][:, 0:kw], qit[i][pr, h // 2, :], kiS[pr, k0:k0 + kw], True, True, [b_qit[i], b_kiS], [BK[sb]])
                            sbs.append(sb)
                        dots(0)
                        for h in range(16):
                            if h + 1 < 16:
                                dots(h + 1)
                            r = cnt["r"] % 3
                            cnt["r"] += 1
                            ACT(rl[r][:, 0:kw], PB[sbs[h]][:, 0:kw], AF.Relu, [BK[sbs[h]]], [b_rl[r]])
                            MM(PB[ab][:, 0:kw], dg[i][:, h, :], rl[r][:, 0:kw], h == 0, h == 15, [b_dg[i], b_rl[r]], [BK[ab]])
                        CP("act", sc[i][:, k0:k0 + kw], PB[ab][:, 0:kw], [BK[ab]], [b_sc[i]])
                    return f
                for k0 in range(0, W, 512):
                    out.append((chunk(k0, min(512, W - k0)), 2.0 * min(512, W - k0) / 512))

                def causal():
                    TT("dve", sc[i][:, qt * 128:W], sc[i][:, qt * 128:W], negm[:], ALU.add, [b_sc[i], b_negm], [b_sc[i]])
                out.append((causal, 0.2))
                if W > n_sel:
                    rounds = n_sel // 8

                    def topk(r0, r1):
                        def f():
                            for r in range(r0, r1):
                                src = sc[i] if r == 0 else wk
                                MAX8(mx[:], src[:, 0:W], [b_sc[i], b_wk], [b_mx])
                                if r < rounds - 1:
                                    MREP(wk[:, 0:W], mx[:], src[:, 0:W], [b_sc[i], b_mx, b_wk], [b_wk])
                        return f
                    for r0 in range(0, rounds, 4):
                        out.append((topk(r0, min(rounds, r0 + 4)), (min(rounds, r0 + 4) - r0) * 2.0 * W / 1000.0))

                    def mask():
                        TS("dve", Mb[i][:, 0:W], sc[i][:, 0:W], mx[:, 7:8], None, ALU.is_ge, None, [b_sc[i], b_mx], [b_Mb[i]])
                else:
                    def mask():
                        TS("dve", Mb[i][:, 0:W], sc[i][:, 0:W], -1e29, None, ALU.is_ge, None, [b_sc[i]], [b_Mb[i]])
                out.append((mask, W / 1000.0))

                def trans(kb0, nb):
                    def f():
                        bk = banks["tr"][cnt["t"] % len(banks["tr"])]
                        sgi = cnt["t"] % 2
                        cnt["t"] += 1
                        for j in range(nb):
                            TR(PBb[bk][:, j * 128:(j + 1) * 128], Mb[i][:, (kb0 + j) * 128:(kb0 + j + 1) * 128], identb[:], [b_Mb[i], b_idb], [BK[bk]])
                        CP("act", mstg[sgi][:, 0:nb, :], PBb[bk][:, 0:nb * 128].rearrange("p (j t) -> p j t", j=nb), [BK[bk]], [b_mstg[sgi]])
                        DMA("sp", MT[kb0 * 128:(kb0 + nb) * 128, qs].rearrange("(j p) t -> p j t", p=128), mstg[sgi][:, 0:nb, :], [b_mstg[sgi]], [])
                    return f
                for kb0 in range(0, qt + 1, 4):
                    out.append((trans(kb0, min(4, qt + 1 - kb0)), 0.5))
                nfront = 1 + len(range(0, W, 512))
                ntr = len(range(0, qt + 1, 4))
                return out[:nfront], out[nfront:len(out) - ntr], out[len(out) - ntr:]

            fb = [tile_items(qt, n_ % 2) for n_, qt in enumerate(qt_list)]
            if fb:
                items.extend(fb[0][0])
            for n_ in range(len(fb)):
                if n_ + 1 < len(fb):
                    items.extend(fb[n_ + 1][0])
                items.extend(fb[n_][1])
                if n_ >= 1:
                    items.extend(fb[n_ - 1][2])
            if fb:
                items.extend(fb[-1][2])
            return items

        for l in range(L):
            with ExitStack() as st:
                hT = sbuf(st, "hT", [128, KC, S], BF16); b_hT = Buf()
                with ExitStack() as st1:
                    norm_phase(st1, hT, b_hT, AB[:, l, 0:16], modT[:, l, 0:16])
                P.barrier()
                slab = [sbuf(st, "slab%d" % i, [128, KC, 512], BF16) for i in range(2)]; b_slab = [Buf(), Buf()]
                st_tm = ExitStack()
                gqa, b_gqa = bcast_load(st_tm, "gqa", Wd["g_qa"][l:l + 1, :], 128)
                gka, b_gka = bcast_load(st_tm, "gka", Wd["g_ka"][l:l + 1, :], 128)
                NB = 8
                qn = [sbuf(st_tm, "qn%d" % i, [128, 512], F32) for i in range(NB)]; b_qn = [Buf() for _ in range(NB)]
                qb = [sbuf(st_tm, "qb%d" % i, [128, 512], BF16) for i in range(NB)]; b_qb = [Buf() for _ in range(NB)]
                ssq = [sbuf(st_tm, "ssq%d" % i, [128, 8], F32) for i in range(NB)]; b_ssq = [Buf() for _ in range(NB)]
                junk = [sbuf(st_tm, "junk%d" % i, [128, 128], F32) for i in range(2)]; b_junk = [Buf(), Buf()]
                rts = [(sbuf(st_tm, "rt_a%d" % i, [128, 64], F32), sbuf(st_tm, "rt_b%d" % i, [128, 64], F32), Buf()) for i in range(4)]
                stg = [sbuf(st_tm, "stg%d" % i, [128, 4, 128], BF16) for i in range(4)]; b_stg = [Buf() for _ in range(4)]
                ki2 = [sbuf(st_tm, "ki2_%d" % i, [128, 128], BF16) for i in range(4)]; b_ki2 = [Buf() for _ in range(4)]
                slab_n = [0]

                def load_slab(col_ranges):
                    i = slab_n[0] % 2
                    slab_n[0] += 1
                    off = 0
                    for (c0, n) in col_ranges:
                        DMA("pool", slab[i][:, :, off:off + n], Wd["w_in"][l][:, c0:c0 + n].rearrange("(kc p) n -> p kc n", p=128), [], [b_slab[i]])
                        off += n
                    return i, off

                def tok_mm(i, ncols, tt, bk):
                    for kc in range(KC):
                        MM(PB[bk][:, 0:ncols], hT[:, kc, tt * 128:(tt + 1) * 128], slab[i][:, kc, 0:ncols], kc == 0, kc == KC - 1, [b_hT, b_slab[i]], [BK[bk]])

                units = []

                def qk_unit(u, i, tt, nh, g_t, b_g, dst_rows, nblk, has_va):
                    b = u % NB; bk = u % 3; tb = 4 + u % 4; sg = u % 4; rt = rts[u % 4]; jk = u % 2
                    ts_ = slice(tt * 128, (tt + 1) * 128)

                    def s0():
                        tok_mm(i, 512, tt, bk)

                    def s1():
                        CP("act", qn[b][:], PB[bk][:], [BK[bk]], [b_qn[b]])

                    def s2():
                        for h in range(nh):
                            ACT(junk[jk][:], qn[b][:, h * 128:(h + 1) * 128], AF.Square, [b_qn[b]], [b_junk[jk], b_ssq[b]], accum_out=ssq[b][:, h:h + 1])

                    def s3():
                        ACT(ssq[b][:, 0:nh], ssq[b][:, 0:nh], AF.Sqrt, [b_ssq[b], b_eps], [b_ssq[b]], scale=1.0 / 128, bias=eps_t[:, 0:1])

                    def s4():
                        RECIP(ssq[b][:, 0:nh], ssq[b][:, 0:nh], [b_ssq[b]], [b_ssq[b]])

                    def s5():
                        for h in range(nh):
                            hs = slice(h * 128, (h + 1) * 128)
                            STT("dve", qn[b][:, hs], qn[b][:, hs], ssq[b][:, h:h + 1], g_t[:], ALU.mult, ALU.mult, [b_qn[b], b_ssq[b], b_g], [b_qn[b]])

                    def s6():
                        CP("pool", qb[b][:], qn[b][:], [b_qn[b]], [b_qb[b]])
                        s3_ = qn[b][:, 0:nh * 128].rearrange("p (h d) -> p h d", h=nh)[:, :, 0:32]
                        d3_ = qb[b][:, 0:nh * 128].rearrange("p (h d) -> p h d", h=nh)[:, :, 0:32]
                        rope(rt, s3_, d3_, nh, 16, tab[:, tt, 56:72], tab[:, tt, 0:16], [b_qn[b]], [b_qb[b]])

                    def s7():
                        for j in range(nblk):
                            TR(PBb[tb][:, j * 128:(j + 1) * 128], qb[b][:, j * 128:(j + 1) * 128], identb[:], [b_qb[b], b_idb], [BK[tb]])

                    def s8():
                        CP("act", stg[sg][:, 0:nblk, :], PBb[tb][:, 0:nblk * 128].rearrange("p (j t) -> p j t", j=nblk), [BK[tb]], [b_stg[sg]])
                        DMA("sp", dst_rows[0:nblk * 128, ts_].rearrange("(j p) t -> p j t", p=128), stg[sg][:, 0:nblk, :], [b_stg[sg]], [])
                        if has_va:
                            DMA("sp", va[ts_, :], qb[b][:, 256:512], [b_qb[b]], [])
                    return [s0, s1, s2, s3, s4, s5, s6, s7, s8]

                def qi_unit(u, i, tt, dst_rows):
                    b = u % NB; bk = u % 3; tb = 4 + u % 4; sg = u % 4; rt = rts[u % 4]
                    ts_ = slice(tt * 128, (tt + 1) * 128)

                    def s0():
                        tok_mm(i, 512, tt, bk)

                    def s1():
                        CP("act", qn[b][:], PB[bk][:], [BK[bk]], [b_qn[b]])

                    def s2():
                        CP("pool", qb[b][:], qn[b][:], [b_qn[b]], [b_qb[b]])
                        s3_ = qn[b][:].rearrange("p (h d) -> p h d", h=8)[:, :, 0:16]
                        d3_ = qb[b][:].rearrange("p (h d) -> p h d", h=8)[:, :, 0:16]
                        rope(rt, s3_, d3_, 8, 8, tab[:, tt, 72:80], tab[:, tt, 16:24], [b_qn[b]], [b_qb[b]])

                    def s3():
                        for j in range(4):
                            TR(PBb[tb][:, j * 128:(j + 1) * 128], qb[b][:, j * 128:(j + 1) * 128], identb[:], [b_qb[b], b_idb], [BK[tb]])

                    def s4():
                        CP("act", stg[sg][:, 0:4, :], PBb[tb][:, 0:512].rearrange("p (j t) -> p j t", j=4), [BK[tb]], [b_stg[sg]])
                        DMA("sp", dst_rows[0:512, ts_].rearrange("(j p) t -> p j t", p=128), stg[sg][:, 0:4, :], [b_stg[sg]], [])
                    nop = lambda: None
                    return [s0, s1, nop, nop, nop, nop, s2, s3, s4]

                def misc_unit(u, i, tt):
                    b = u % NB; bk = u % 3; tb = 4 + u % 4; sg = u % 4; rt = rts[u % 4]; kk = u % 4
                    ts_ = slice(tt * 128, (tt + 1) * 128)

                    def s0():
                        tok_mm(i, 144, tt, bk)

                    def s1():
                        CP("act", qn[b][:, 0:144], PB[bk][:, 0:144], [BK[bk]], [b_qn[b]])

                    def s2():
                        CP("pool", ki2[kk][:, 0:64], qn[b][:, 0:64], [b_qn[b]], [b_ki2[kk]])
                        s3_ = qn[b][:, 0:64].rearrange("p (h d) -> p h d", h=1)[:, :, 0:16]
                        d3_ = ki2[kk][:, 0:64].rearrange("p (h d) -> p h d", h=1)[:, :, 0:16]
                        rope(rt, s3_, d3_, 1, 8, tab[:, tt, 72:80], tab[:, tt, 16:24], [b_qn[b]], [b_ki2[kk]])
                        CP("pool", ki2[kk][:, 64:128], ki2[kk][:, 0:64], [b_ki2[kk]], [b_ki2[kk]])
                        TS("pool", qn[b][:, 64:80], qn[b][:, 64:80], float(1024 ** -0.5), None, ALU.mult, None, [b_qn[b]], [b_qn[b]])

                    def s3():
                        TR(PBb[tb][:, 0:128], ki2[kk][:], identb[:], [b_ki2[kk], b_idb], [BK[tb]])
                        DMA("sp", wis[ts_, :], qn[b][:, 64:80], [b_qn[b]], [])
                        DMA("sp", mkr[ts_, :], qn[b][:, 80:144], [b_qn[b]], [])

                    def s4():
                        CP("act", stg[sg][:, 0, :], PBb[tb][:, 0:128], [BK[tb]], [b_stg[sg]])
                        DMA("sp", kiT2[:, ts_], stg[sg][:, 0, :], [b_stg[sg]], [])
                    nop = lambda: None
                    return [s0, s1, nop, nop, nop, nop, s2, s3, s4]

                groups = [("qa", [(0, 512)]), ("qa", [(512, 512)]), ("ka", [(1024, 512)]), ("qi", [(1536, 512)]), ("qi", [(2048, 512)]), ("misc", [(2560, 80), (3408, 64)])]

                def slab_load(gi):
                    i = gi % 2
                    off = 0
                    for (c0, n) in groups[gi][1]:
                        DMA("pool", slab[i][:, :, off:off + n], Wd["w_in"][l][:, c0:c0 + n].rearrange("(kc p) n -> p kc n", p=128), [], [b_slab[i]])
                        off += n

                def with_pre(stages, pre):
                    s0 = stages[0]

                    def s0p():
                        pre()
                        s0()
                    return [s0p] + stages[1:]

                u = 0
                for gi, (kind, cr) in enumerate(groups):
                    i = gi % 2
                    for tt in range(NT):
                        if kind == "qa":
                            stages = qk_unit(u, i, tt, 4, gqa, b_gqa, qaT[gi * 512:(gi + 1) * 512, :], 4, False)
                        elif kind == "ka":
                            stages = qk_unit(u, i, tt, 2, gka, b_gka, kaT, 2, True)
                        elif kind == "qi":
                            stages = qi_unit(u, i, tt, qiT[(gi - 3) * 512:(gi - 2) * 512, :])
                        else:
                            stages = misc_unit(u, i, tt)
                        if tt == 0:
                            def pre(gi=gi):
                                if gi == 0:
                                    slab_load(0)
                                if gi + 1 < len(groups):
                                    slab_load(gi + 1)
                            stages = with_pre(stages, pre)
                        units.append(stages)
                        u += 1
                run_skewed(units)
                slab_n[0] = len(groups)
                P.barrier()
                st_tm.close()
                st_f = ExitStack()
                lat = sbuf(st_f, "lat", [128, 4, 512], F32); b_lat = Buf()
                latb = [sbuf(st_f, "latb%d" % i, [128, 512], BF16) for i in range(2)]; b_latb = [Buf(), Buf()]
                lsq = [sbuf(st_f, "lsq%d" % i, [128, 512], F32) for i in range(2)]; b_lsq = [Buf(), Buf()]
                lrs = sbuf(st_f, "lrs", [128, 512], F32); b_lrs = Buf()
                glatT = sbuf(st_f, "glatT", [128, 8], F32); b_glat = Buf()
                rows_to_cols(st_f, Wd["g_mq_lat"][l].rearrange("(k p) -> k p", p=128), 4, glatT[:, 0:4], b_glat, 6, "gmq")
                rows_to_cols(st_f, Wd["g_mkv_lat"][l].rearrange("(k p) -> k p", p=128), 2, glatT[:, 4:6], b_glat, 6, "gmkv")
                for (c0, nm, dst, goff) in ((2640, 4, mqlT, 0), (3152, 2, mkvlT, 4)):
                    i, _ = load_slab([(c0, nm * 128)])
                    for ch in range(NCH):
                        cs = slice(ch * 512, (ch + 1) * 512)
                        for m in range(nm):
                            bk = m % 4
                            for kc in range(KC):
                                MM(PB[bk][:], slab[i][:, kc, m * 128:(m + 1) * 128], hT[:, kc, cs], kc == 0, kc == KC - 1, [b_slab[i], b_hT], [BK[bk]])
                            CP("dve", lat[:, m, :], PB[bk][:], [BK[bk]], [b_lat])
                            ACT(lsq[m % 2][:], PB[bk][:], AF.Square, [BK[bk]], [b_lsq[m % 2]])
                            MM(PB[4][:], ones_f[:], lsq[m % 2][:], m == 0, m == nm - 1, [b_1f, b_lsq[m % 2]], [BK[4]])
                        CP("dve", lrs[:], PB[4][:], [BK[4]], [b_lrs])
                        RSTD(lrs[:], 1.0 / (nm * 128), [b_lrs])
                        for m in range(nm):
                            STT("dve", latb[m % 2][:], lat[:, m, :], glatT[:, goff + m:goff + m + 1], lrs[:], ALU.mult, ALU.mult, [b_lat, b_glat, b_lrs], [b_latb[m % 2]])
                            DMA("sp", dst[m * 128:(m + 1) * 128, cs], latb[m % 2][:], [b_latb[m % 2]], [])
                P.barrier()
                st_f.close()
                sg = [sbuf(st, "sg%d" % i, [128, 512], F32) for i in range(2)]; b_sg = [Buf(), Buf()]
                gslabs = [(c0 + s4 * 512, dst, s4) for (c0, dst) in ((3472, sgaT), (5520, sgbT)) for s4 in range(4)]
                gslab_i = {}

                def gate_slab(gn):
                    if gn < len(gslabs) and gn not in gslab_i:
                        gslab_i[gn] = load_slab([(gslabs[gn][0], 512)])[0]
                gate_items = []
                gcnt = [0]

                def gate_item(gn, ch, m):
                    def f():
                        if ch == 0 and m == 0:
                            gate_slab(gn)
                            gate_slab(gn + 1)
                        i = gslab_i[gn]
                        c0_, dst, s4 = gslabs[gn]
                        n = gcnt[0]
                        gcnt[0] += 1
                        cs = slice(ch * 512, (ch + 1) * 512)
                        bk = 6 + n % 2
                        for kc in range(KC):
                            MM(PB[bk][:], slab[i][:, kc, m * 128:(m + 1) * 128], hT[:, kc, cs], kc == 0, kc == KC - 1, [b_slab[i], b_hT], [BK[bk]])
                        ACT(sg[n % 2][:], PB[bk][:], AF.Sigmoid, [BK[bk]], [b_sg[n % 2]])
                        r0 = s4 * 512 + m * 128
                        DMA("sp", dst[r0:r0 + 128, cs], sg[n % 2][:], [b_sg[n % 2]], [])
                    return f
                for gn in range(len(gslabs)):
                    for ch in range(NCH):
                        for m in range(4):
                            gate_items.append(gate_item(gn, ch, m))
                p3 = make_p3(st, list(range(NT)), dict(score=[0, 1], acc=[2, 3], tr=[4, 5]))
                interleave(p3, gate_items)
            P.barrier()

            with ExitStack() as st:
                mqlS = sbuf(st, "mqlS", [128, 4, S], BF16); b_mqlS = Buf()
                mkvlS = sbuf(st, "mkvlS", [128, 2, S], BF16); b_mkvlS = Buf()
                wq = sbuf(st, "wq", [128, 4, 1536], BF16); b_wq = Buf()
                wkv = sbuf(st, "wkv", [128, 2, 2048], BF16); b_wkv = Buf()
                DMA("sp", mqlS[:], mqlT.rearrange("(kc p) s -> p kc s", p=128), [], [b_mqlS])
                DMA("sp", mkvlS[:], mkvlT.rearrange("(kc p) s -> p kc s", p=128), [], [b_mkvlS])
                DMA("pool", wq[:], Wd["w_mq_up"][l].rearrange("(kc p) n -> p kc n", p=128), [], [b_wq])
                DMA("pool", wkv[:], Wd["w_mkv_up"][l].rearrange("(kc p) n -> p kc n", p=128), [], [b_wkv])
                gqm, b_gqm = bcast_load(st, "gqm", Wd["g_qm"][l:l + 1, :], 192)
                gkm, b_gkm = bcast_load(st, "gkm", Wd["g_km"][l:l + 1, :], 192)
                NB = 8
                qn = [sbuf(st, "m_qn%d" % i, [128, 512], F32) for i in range(NB)]; b_qn = [Buf() for _ in range(NB)]
                qb = [sbuf(st, "m_qb%d" % i, [128, 384], BF16) for i in range(NB)]; b_qb = [Buf() for _ in range(NB)]
                kb_ = [sbuf(st, "m_kb%d" % i, [128, 384], BF16) for i in range(NB)]; b_kb = [Buf() for _ in range(NB)]
                kr = [sbuf(st, "m_kr%d" % i, [128, 128], F32) for i in range(NB)]; b_kr = [Buf() for _ in range(NB)]
                vb = [sbuf(st, "m_vb%d" % i, [128, 256], BF16) for i in range(NB)]; b_vb = [Buf() for _ in range(NB)]
                ssq = [sbuf(st, "m_ssq%d" % i, [128, 4], F32) for i in range(NB)]; b_ssq = [Buf() for _ in range(NB)]
                junk = [sbuf(st, "m_junk%d" % i, [128, 192], F32) for i in range(2)]; b_junk = [Buf(), Buf()]
                rts = [(sbuf(st, "m_rt_a%d" % i, [128, 64], F32), sbuf(st, "m_rt_b%d" % i, [128, 64], F32), Buf()) for i in range(4)]
                stg = [sbuf(st, "m_stg%d" % i, [128, 2, 128], BF16) for i in range(4)]; b_stg = [Buf() for _ in range(4)]
                stgr = [sbuf(st, "m_stgr%d" % i, [64, 2, 128], BF16) for i in range(4)]; b_stgr = [Buf() for _ in range(4)]
                mkr_t = [sbuf(st, "m_mkr%d" % i, [128, 64], F32) for i in range(3)]; b_mkr = [Buf() for _ in range(3)]
                ssr = [sbuf(st, "m_ssr%d" % i, [128, 1], F32) for i in range(3)]; b_ssr = [Buf() for _ in range(3)]

                def mq_unit(u, tt, g):
                    b = u % NB; bk = 6 + u % 2; tb = 4 + u % 2; sg = u % 4; rt = rts[u % 4]; jk = u % 2
                    ts_ = slice(tt * 128, (tt + 1) * 128)

                    def s0():
                        for kc in range(4):
                            MM(PB[bk][:, 0:384], mqlS[:, kc, ts_], wq[:, kc, g * 384:(g + 1) * 384], kc == 0, kc == 3, [b_mqlS, b_wq], [BK[bk]])

                    def s1():
                        CP("act", qn[b][:, 0:384], PB[bk][:, 0:384], [BK[bk]], [b_qn[b]])

                    def s2():
                        for h in range(2):
                            ACT(junk[jk][:, 0:192], qn[b][:, h * 192:(h + 1) * 192], AF.Square, [b_qn[b]], [b_junk[jk], b_ssq[b]], accum_out=ssq[b][:, h:h + 1])

                    def s3():
                        pass

                    def s4():
                        ACT(ssq[b][:, 0:2], ssq[b][:, 0:2], AF.Sqrt, [b_ssq[b], b_eps], [b_ssq[b]], scale=1.0 / 192, bias=eps_t[:, 0:1])

                    def s5():
                        RECIP(ssq[b][:, 0:2], ssq[b][:, 0:2], [b_ssq[b]], [b_ssq[b]])

                    def s6():
                        for h in range(2):
                            hs = slice(h * 192, (h + 1) * 192)
                            STT("dve", qn[b][:, hs], qn[b][:, hs], ssq[b][:, h:h + 1], gqm[:], ALU.mult, ALU.mult, [b_qn[b], b_ssq[b], b_gqm], [b_qn[b]])

                    def s7():
                        CP("pool", qb[b][:], qn[b][:, 0:384], [b_qn[b]], [b_qb[b]])
                        s3_ = qn[b][:, 0:384].rearrange("p (h d) -> p h d", h=2)[:, :, 128:192]
                        d3_ = qb[b][:].rearrange("p (h d) -> p h d", h=2)[:, :, 128:192]
                        rope(rt, s3_, d3_, 2, 32, tab[:, tt, 80:112], tab[:, tt, 24:56], [b_qn[b]], [b_qb[b]], eng="dve")

                    def s8():
                        for h in range(2):
                            TR(PBb[tb][:, h * 128:(h + 1) * 128], qb[b][:, h * 192:h * 192 + 128], identb[:], [b_qb[b], b_idb], [BK[tb]])
                            TR(PBb[tb][0:64, 256 + h * 128:256 + (h + 1) * 128], qb[b][:, h * 192 + 128:(h + 1) * 192], identb[:], [b_qb[b], b_idb], [BK[tb]])

                    def s9():
                        CP("act", stg[sg][:, :, :], PBb[tb][:, 0:256].rearrange("p (j t) -> p j t", j=2), [BK[tb]], [b_stg[sg]])
                        CP("act", stgr[sg][:, :, :], PBb[tb][0:64, 256:512].rearrange("p (j t) -> p j t", j=2), [BK[tb]], [b_stgr[sg]])
                        DMA("sp", mqTn[g * 256:(g + 1) * 256, ts_].rearrange("(j p) t -> p j t", p=128), stg[sg][:, :, :], [b_stg[sg]], [])
                        DMA("sp", mqTr[g * 128:(g + 1) * 128, ts_].rearrange("(j p) t -> p j t", p=64), stgr[sg][:, :, :], [b_stgr[sg]], [])
                    return [s0, s1, s2, s3, s4, s5, s6, s7, s8, s9]

                def mkv_unit(u, tt, g):
                    b = u % NB; bk = 6 + u % 2; tb = 4 + u % 2; sg = u % 4; rt = rts[u % 4]; jk = u % 2; tr3 = tt % 3
                    ts_ = slice(tt * 128, (tt + 1) * 128)

                    def s0():
                        for kc in range(2):
                            MM(PB[bk][:], mkvlS[:, kc, ts_], wkv[:, kc, g * 512:(g + 1) * 512], kc == 0, kc == 1, [b_mkvlS, b_wkv], [BK[bk]])
                        if g == 0:
                            DMA("sp", mkr_t[tr3][:], mkr[ts_, :], [], [b_mkr[tr3]])

                    def s1():
                        CP("act", qn[b][:], PB[bk][:], [BK[bk]], [b_qn[b]])

                    def s2():
                        if g == 0:
                            ACT(junk[jk][:, 0:64], mkr_t[tr3][:], AF.Square, [b_mkr[tr3]], [b_junk[jk], b_ssr[tr3]], accum_out=ssr[tr3][:, 0:1])
                        for h in range(2):
                            ACT(junk[jk][:, 0:128], qn[b][:, h * 256:h * 256 + 128], AF.Square, [b_qn[b]], [b_junk[jk], b_ssq[b]], accum_out=ssq[b][:, h:h + 1])

                    def s3():
                        TS("dve", ssq[b][:, 0:2], ssq[b][:, 0:2], ssr[tr3][:, 0:1], None, ALU.add, None, [b_ssq[b], b_ssr[tr3]], [b_ssq[b]])

                    def s4():
                        ACT(ssq[b][:, 0:2], ssq[b][:, 0:2], AF.Sqrt, [b_ssq[b], b_eps], [b_ssq[b]], scale=1.0 / 192, bias=eps_t[:, 0:1])

                    def s5():
                        RECIP(ssq[b][:, 0:2], ssq[b][:, 0:2], [b_ssq[b]], [b_ssq[b]])

                    def s6():
                        for h in range(2):
                            STT("dve", kb_[b][:, h * 128:(h + 1) * 128], qn[b][:, h * 256:h * 256 + 128], ssq[b][:, h:h + 1], gkm[:, 0:128], ALU.mult, ALU.mult, [b_qn[b], b_ssq[b], b_gkm], [b_kb[b]])
                            STT("dve", kr[b][:, h * 64:(h + 1) * 64], mkr_t[tr3][:], ssq[b][:, h:h + 1], gkm[:, 128:192], ALU.mult, ALU.mult, [b_mkr[tr3], b_ssq[b], b_gkm], [b_kr[b]])

                    def s7():
                        s3_ = kr[b][:].rearrange("p (h d) -> p h d", h=2)
                        d3_ = kb_[b][:, 256:384].rearrange("p (h d) -> p h d", h=2)
                        rope(rt, s3_, d3_, 2, 32, tab[:, tt, 80:112], tab[:, tt, 24:56], [b_kr[b]], [b_kb[b]], eng="dve")
                        CP("pool", vb[b][:].rearrange("p (h d) -> p h d", h=2), qn[b][:].rearrange("p (h d) -> p h d", h=2)[:, :, 128:256], [b_qn[b]], [b_vb[b]])

                    def s8():
                        for h in range(2):
                            TR(PBb[tb][:, h * 128:(h + 1) * 128], kb_[b][:, h * 128:(h + 1) * 128], identb[:], [b_kb[b], b_idb], [BK[tb]])
                            TR(PBb[tb][0:64, 256 + h * 128:256 + (h + 1) * 128], kb_[b][:, 256 + h * 64:256 + (h + 1) * 64], identb[:], [b_kb[b], b_idb], [BK[tb]])
                        DMA("sp", mv[ts_, g * 256:(g + 1) * 256], vb[b][:], [b_vb[b]], [])

                    def s9():
                        CP("act", stg[sg][:, :, :], PBb[tb][:, 0:256].rearrange("p (j t) -> p j t", j=2), [BK[tb]], [b_stg[sg]])
                        CP("act", stgr[sg][:, :, :], PBb[tb][0:64, 256:512].rearrange("p (j t) -> p j t", j=2), [BK[tb]], [b_stgr[sg]])
                        DMA("sp", mkTn[g * 256:(g + 1) * 256, ts_].rearrange("(j p) t -> p j t", p=128), stg[sg][:, :, :], [b_stg[sg]], [])
                        DMA("sp", mkTr[g * 128:(g + 1) * 128, ts_].rearrange("(j p) t -> p j t", p=64), stgr[sg][:, :, :], [b_stgr[sg]], [])
                    return [s0, s1, s2, s3, s4, s5, s6, s7, s8, s9]

                units = []
                u = 0
                for tt in range(NT):
                    for g in range(4):
                        units.append(mq_unit(u, tt, g)); u += 1
                    for g in range(4):
                        units.append(mkv_unit(u, tt, g)); u += 1
                run_skewed(units)
            P.barrier()

            with ExitStack() as st:
                kaS = sbuf(st, "kaS", [128, 2, S], BF16); b_kaS = Buf()
                vaS = sbuf(st, "vaS", [128, NT, 256], BF16); b_vaS = Buf()
                DMA("sp", kaS[:], kaT.rearrange("(g p) s -> p g s", p=128), [], [b_kaS])
                DMA("sp", vaS[:], va.rearrange("(t p) c -> p t c", p=128), [], [b_vaS])
                mkn = sbuf(st, "mkn", [128, 8, S], BF16); b_mkn = Buf()
                mkrS = sbuf(st, "mkrS", [64, 8, S], BF16); b_mkrS = Buf()
                mvS = sbuf(st, "mvS", [128, NT, 1024], BF16); b_mvS = Buf()
                DMA("sp", mkn[:], mkTn.rearrange("(h p) s -> p h s", p=128), [], [b_mkn])
                DMA("sp", mkrS[:], mkTr.rearrange("(h p) s -> p h s", p=64), [], [b_mkrS])
                DMA("sp", mvS[:], mv.rearrange("(t p) c -> p t c", p=128), [], [b_mvS])
                qT_ = [sbuf(st, "a_q%d" % i, [128, 512], BF16) for i in range(3)]; b_q = [Buf(), Buf(), Buf()]
                qR_ = [sbuf(st, "a_qr%d" % i, [64, 512], BF16) for i in range(3)]; b_qr = [Buf(), Buf(), Buf()]
                mts = [sbuf(st, "a_mts%d" % i, [128, NT, 512], BF16) for i in range(2)]; b_mts = [Buf(), Buf()]
                pT = [sbuf(st, "a_pT%d" % i, [128, 512], BF16) for i in range(4)]; b_pT = [Buf() for _ in range(4)]
                rd_ = sbuf(st, "a_rd", [128, 512], F32); b_rd = Buf()
                ob = [sbuf(st, "a_ob%d" % i, [128, 512], BF16) for i in range(2)]; b_ob = [Buf(), Buf()]
                hcount = [0]
                icount = [0]
                heads_all = [(qc_, h_) for qc_ in range(NCH) for h_ in range(16)]

                def load_q(g):
                    if g >= len(heads_all):
                        return
                    qc_, h_ = heads_all[g]
                    cs_ = slice(qc_ * 512, (qc_ + 1) * 512)
                    isA_ = h_ < 8
                    hh_ = h_ if isA_ else h_ - 8
                    i_ = g % 3
                    if isA_:
                        DMA("sp", qT_[i_][:], qaT[hh_ * 128:(hh_ + 1) * 128, cs_], [], [b_q[i_]])
                    else:
                        DMA("sp", qT_[i_][:], mqTn[hh_ * 128:(hh_ + 1) * 128, cs_], [], [b_q[i_]])
                        DMA("sp", qR_[i_][:], mqTr[hh_ * 64:(hh_ + 1) * 64, cs_], [], [b_qr[i_]])
                load_q(0)
                for qc in range(NCH):
                    cs = slice(qc * 512, (qc + 1) * 512)
                    nkb = 4 * qc + 4
                    mi = qc % 2
                    if qc == 0:
                        DMA("sp", mts[0][:, 0:4, :], MT[0:512, 0:512].rearrange("(k p) t -> p k t", p=128), [], [b_mts[0]])
                    if qc + 1 < NCH:
                        nkb2 = 4 * (qc + 1) + 4
                        DMA("sp", mts[(qc + 1) % 2][:, 0:nkb2, :], MT[0:nkb2 * 128, (qc + 1) * 512:(qc + 2) * 512].rearrange("(k p) t -> p k t", p=128), [], [b_mts[(qc + 1) % 2]])
                    items = []
                    for h in range(16):
                        hq = hcount[0]
                        hcount[0] += 1
                        for kb in range(nkb):
                            items.append((h, kb, hq, icount[0]))
                            icount[0] += 1

                    def a_scores(it, qc=qc, cs=cs):
                        h, kb, hq, n = it
                        isA = h < 8
                        hh = h if isA else h - 8
                        i = hq % 3
                        if kb == 0:
                            load_q(hq + 1)
                        q0 = max(0, kb - 4 * qc) * 128
                        ks = slice(kb * 128, (kb + 1) * 128)
                        bk = n % 4
                        if isA:
                            MM(PB[bk][:, q0:512], kaS[:, hh // 4, ks], qT_[i][:, q0:512], True, True, [b_kaS, b_q[i]], [BK[bk]])
                        else:
                            MM(PB[bk][:, q0:512], mkn[:, hh, ks], qT_[i][:, q0:512], True, False, [b_mkn, b_q[i]], [BK[bk]])
                            MM(PB[bk][:, q0:512], mkrS[:, hh, ks], qR_[i][:, q0:512], False, True, [b_mkrS, b_qr[i]], [BK[bk]])

                    def a_rest(it, qc=qc, cs=cs, nkb=nkb, mi=mi):
                        h, kb, hq, n = it
                        isA = h < 8
                        hh = h if isA else h - 8
                        scale = float(128 ** -0.5) if isA else float(192 ** -0.5)
                        q0 = max(0, kb - 4 * qc) * 128
                        bk = n % 4
                        j = n % 4
                        bo = 4 + (hq % 2) * 2
                        ACT(pT[j][:, q0:512], PB[bk][:, q0:512], AF.Exp, [BK[bk]], [b_pT[j]], scale=scale)
                        if isA:
                            TT("dve" if n % 2 else "pool", pT[j][:, q0:512], pT[j][:, q0:512], mts[mi][:, kb, q0:512], ALU.mult, [b_pT[j], b_mts[mi]], [b_pT[j]])
                            vv = vaS[:, kb, (hh // 4) * 128:(hh // 4 + 1) * 128]
                            b_v = b_vaS
                        else:
                            if kb >= 4 * qc:
                                TT("dve" if n % 2 else "pool", pT[j][:, q0:q0 + 128], pT[j][:, q0:q0 + 128], tri[:], ALU.mult, [b_pT[j], b_tri], [b_pT[j]])
                            vv = mvS[:, kb, hh * 128:(hh + 1) * 128]
                            b_v = b_mvS
                        MM(PB[bo][:, q0:512], vv, pT[j][:, q0:512], kb == 0, kb == nkb - 1, [b_v, b_pT[j]], [BK[bo]])
                        MM(PB[bo + 1][:, q0:512], ones_b[:], pT[j][:, q0:512], kb == 0, kb == nkb - 1, [b_1b, b_pT[j]], [BK[bo + 1]])
                        if kb == nkb - 1:
                            i2 = hq % 2
                            RECIP(rd_[:], PB[bo + 1][:], [BK[bo + 1]], [b_rd])
                            TT("dve", ob[i2][:], PB[bo][:], rd_[:], ALU.mult, [BK[bo], b_rd], [b_ob[i2]])
                            dst = oaT if isA else obT
                            DMA("sp", dst[hh * 128:(hh + 1) * 128, cs], ob[i2][:], [b_ob[i2]], [])

                    DEPTH_PF = 2
                    for k in range(min(DEPTH_PF, len(items))):
                        a_scores(items[k])
                    for k in range(len(items)):
                        if k + DEPTH_PF < len(items):
                            a_scores(items[k + DEPTH_PF])
                        a_rest(items[k])
            P.barrier()

            with ExitStack() as st:
                mT_ = sbuf(st, "mT_", [128, KC, S], BF16); b_mT = Buf()
                with ExitStack() as st1:
                    oaS = sbuf(st1, "oaS", [128, 8, S], BF16); b_oaS = Buf()
                    obS = sbuf(st1, "obS", [128, 8, S], BF16); b_obS = Buf()
                    DMA("sp", oaS[:], oaT.rearrange("(k p) s -> p k s", p=128), [], [b_oaS])
                    DMA("sp", obS[:], obT.rearrange("(k p) s -> p k s", p=128), [], [b_obS])
                    wpa = [sbuf(st1, "wpa%d" % i, [128, 8, 128], BF16) for i in range(2)]; b_wpa = [Buf(), Buf()]
                    wpb = [sbuf(st1, "wpb%d" % i, [128, 8, 128], BF16) for i in range(2)]; b_wpb = [Buf(), Buf()]
                    ga = [sbuf(st1, "ga%d" % i, [128, 512], F32) for i in range(4)]; b_ga = [Buf() for _ in range(4)]
                    gb = [sbuf(st1, "gb%d" % i, [128, 512], F32) for i in range(4)]; b_gb = [Buf() for _ in range(4)]
                    t1 = [sbuf(st1, "p5t%d" % i, [128, 512], F32) for i in range(4)]; b_t1 = [Buf() for _ in range(4)]
                    n = 0
                    for m in range(KC):
                        i = m % 2
                        ms = slice(m * 128, (m + 1) * 128)
                        DMA("pool", wpa[i][:], Wd["w_pa"][l][:, ms].rearrange("(k p) n -> p k n", p=128), [], [b_wpa[i]])
                        DMA("pool", wpb[i][:], Wd["w_pb"][l][:, ms].rearrange("(k p) n -> p k n", p=128), [], [b_wpb[i]])
                        for ch in range(NCH):
                            cs = slice(ch * 512, (ch + 1) * 512)
                            ba = (n % 4) * 2
                            for k in range(8):
                                MM(PB[ba][:], wpa[i][:, k, :], oaS[:, k, cs], k == 0, k == 7, [b_wpa[i], b_oaS], [BK[ba]])
                            for k in range(8):
                                MM(PB[ba + 1][:], wpb[i][:, k, :], obS[:, k, cs], k == 0, k == 7, [b_wpb[i], b_obS], [BK[ba + 1]])
                            j = n % 4
                            DMA("sp", ga[j][:], sgaT[ms, cs], [], [b_ga[j]])
                            DMA("sp", gb[j][:], sgbT[ms, cs], [], [b_gb[j]])
                            TT("dve", t1[j][:], PB[ba][:], ga[j][:], ALU.mult, [BK[ba], b_ga[j]], [b_t1[j]])
                            TT("dve", ga[j][:], PB[ba + 1][:], gb[j][:], ALU.mult, [BK[ba + 1], b_gb[j], b_ga[j]], [b_ga[j]])
                            TT("pool", mT_[:, m, cs], t1[j][:], ga[j][:], ALU.add, [b_t1[j], b_ga[j]], [b_mT])
                            n += 1
                P.barrier()
                with ExitStack() as st1:
                    wo = [sbuf(st1, "wo%d" % i, [128, KC, 128], BF16) for i in range(2)]; b_wo = [Buf(), Buf()]
                    xm = [sbuf(st1, "xm%d" % i, [128, S], F32) for i in range(3)]; b_xm = [Buf(), Buf(), Buf()]
                    n = 0
                    DMA("sp", xm[0][:], xT[0:128, :], [], [b_xm[0]])
                    for m in range(KC):
                        i = m % 2
                        ms = slice(m * 128, (m + 1) * 128)
                        DMA("pool", wo[i][:], Wd["w_o"][l][:, ms].rearrange("(k p) n -> p k n", p=128), [], [b_wo[i]])
                        if m + 1 < KC:
                            DMA("sp", xm[(m + 1) % 3][:], xT[(m + 1) * 128:(m + 2) * 128, :], [], [b_xm[(m + 1) % 3]])
                        i3 = m % 3
                        for ch in range(NCH):
                            cs = slice(ch * 512, (ch + 1) * 512)
                            bk = n % 4
                            for k in range(KC):
                                MM(PB[bk][:], wo[i][:, k, :], mT_[:, k, cs], k == 0, k == KC - 1, [b_wo[i], b_mT], [BK[bk]])
                            STT("dve", xm[i3][:, cs], PB[bk][:], modT[:, l, 32 + m:33 + m], xm[i3][:, cs], ALU.mult, ALU.add, [BK[bk], b_mod, b_xm[i3]], [b_xm[i3]])
                            n += 1
                        DMA("sp", xT[ms, :], xm[i3][:], [b_xm[i3]], [])
            P.barrier()

            with ExitStack() as st:
                hT = sbuf(st, "h2T", [128, KC, S], BF16); b_hT = Buf()
                with ExitStack() as st1:
                    norm_phase(st1, hT, b_hT, AB[:, l, 16:32], modT[:, l, 48:64])
                P.barrier()
                wcT = sbuf(st, "wcT", [128, 3, 88], F32); b_wc = Buf()
                bcT = sbuf(st, "bcT", [128, 88], F32); b_bc = Buf()
                for t in range(3):
                    rows_to_cols(st, Wd["w_conv"][l][t].rearrange("(k p) -> k p", p=128), 88, wcT[:, t, :], b_wc, 7, "wc%d" % t)
                rows_to_cols(st, Wd["b_conv"][l].rearrange("(k p) -> k p", p=128), 88, bcT[:], b_bc, 7, "bc")
                wu = [sbuf(st, "wu%d" % i, [128, KC, 256], BF16) for i in range(2)]; b_wu = [Buf(), Buf()]
                ug = [sbuf(st, "ug%d" % i, [128, S + 2], F32) for i in range(2)]; b_ug = [Buf(), Buf()]
                uv = [sbuf(st, "uv%d" % i, [128, S + 2], F32) for i in range(2)]; b_uv = [Buf(), Buf()]
                cg = [sbuf(st, "cg%d" % i, [128, S], F32) for i in range(2)]; b_cg = [Buf(), Buf()]
                cv = [sbuf(st, "cv%d" % i, [128, S], F32) for i in range(2)]; b_cv = [Buf(), Buf()]
                ab = [sbuf(st, "ab%d" % i, [128, S], BF16) for i in range(2)]; b_ab = [Buf(), Buf()]
                for i in range(2):
                    MS("pool", ug[i][:, 0:2], 0.0, [b_ug[i]])
                    MS("pool", uv[i][:, 0:2], 0.0, [b_uv[i]])
                n = 0
                for j in range(44):
                    i = j % 2
                    DMA("pool", wu[i][:, :, 0:128], Wd["w_up"][l][:, j * 128:(j + 1) * 128].rearrange("(k p) n -> p k n", p=128), [], [b_wu[i]])
                    DMA("pool", wu[i][:, :, 128:256], Wd["w_up"][l][:, DFF + j * 128:DFF + (j + 1) * 128].rearrange("(k p) n -> p k n", p=128), [], [b_wu[i]])
                    for ch in range(NCH):
                        cs2 = slice(2 + ch * 512, 2 + (ch + 1) * 512)
                        cs = slice(ch * 512, (ch + 1) * 512)
                        bg = (n % 4) * 2
                        for k in range(KC):
                            MM(PB[bg][:], wu[i][:, k, 0:128], hT[:, k, cs], k == 0, k == KC - 1, [b_wu[i], b_hT], [BK[bg]])
                        for k in range(KC):
                            MM(PB[bg + 1][:], wu[i][:, k, 128:256], hT[:, k, cs], k == 0, k == KC - 1, [b_wu[i], b_hT], [BK[bg + 1]])
                        CP("act", ug[i][:, cs2], PB[bg][:], [BK[bg]], [b_ug[i]])
                        CP("dve", uv[i][:, cs2], PB[bg + 1][:], [BK[bg + 1]], [b_uv[i]])
                        n += 1
                    for (u_, c_, b_u, b_c, col, e1, e2) in ((ug[i], cg[i], b_ug[i], b_cg[i], j, "pool", "dve"), (uv[i], cv[i], b_uv[i], b_cv[i], 44 + j, "dve", "pool")):
                        ACT(c_[:], u_[:, 2:S + 2], AF.Identity, [b_u, b_wc, b_bc], [b_c], scale=wcT[:, 2, col:col + 1], bias=bcT[:, col:col + 1])
                        STT(e1, c_[:], u_[:, 1:S + 1], wcT[:, 1, col:col + 1], c_[:], ALU.mult, ALU.add, [b_u, b_wc, b_c], [b_c])
                        STT(e2, c_[:], u_[:, 0:S], wcT[:, 0, col:col + 1], c_[:], ALU.mult, ALU.add, [b_u, b_wc, b_c], [b_c])
                    ACT(cg[i][:], cg[i][:], AF.Silu, [b_cg[i]], [b_cg[i]])
                    TT("dve", ab[i][:], cg[i][:], cv[i][:], ALU.mult, [b_cg[i], b_cv[i]], [b_ab[i]])
                    DMA("sp", actT[j * 128:(j + 1) * 128, :], ab[i][:], [b_ab[i]], [])
            P.barrier()

            with ExitStack() as st:
                TC = min(1024, S)
                aS = sbuf(st, "aS", [128, 44, TC], BF16); b_aS = Buf()
                wd = [sbuf(st, "wd%d" % i, [128, 44, 128], BF16) for i in range(2)]; b_wd = [Buf(), Buf()]
                xm = [sbuf(st, "xd%d" % i, [128, TC], F32) for i in range(3)]; b_xm = [Buf(), Buf(), Buf()]
                its = [(c0, m) for c0 in range(0, S, TC) for m in range(KC)]

                def p8_load(n_):
                    if n_ < len(its):
                        c0_, m_ = its[n_]
                        DMA("sp", xm[n_ % 3][:], xT[m_ * 128:(m_ + 1) * 128, c0_:c0_ + TC], [], [b_xm[n_ % 3]])
                p8_load(0)
                for n, (c0, m) in enumerate(its):
                    if m == 0:
                        DMA("sp", aS[:], actT[:, c0:c0 + TC].rearrange("(k p) s -> p k s", p=128), [], [b_aS])
                    i = n % 2
                    i3 = n % 3
                    ms = slice(m * 128, (m + 1) * 128)
                    DMA("pool", wd[i][:], Wd["w_down"][l][:, ms].rearrange("(k p) n -> p k n", p=128), [], [b_wd[i]])
                    p8_load(n + 1)
                    for s0 in range(0, TC, 512):
                        bk = (n * 2 + s0 // 512) % 8
                        for k in range(44):
                            MM(PB[bk][:], wd[i][:, k, :], aS[:, k, s0:s0 + 512], k == 0, k == 43, [b_wd[i], b_aS], [BK[bk]])
                        STT("dve", xm[i3][:, s0:s0 + 512], PB[bk][:], modT[:, l, 80 + m:81 + m], xm[i3][:, s0:s0 + 512], ALU.mult, ALU.add, [BK[bk], b_mod, b_xm[i3]], [b_xm[i3]])
                    DMA("sp", xT[ms, c0:c0 + TC], xm[i3][:], [b_xm[i3]], [])
            P.barrier()

        with ExitStack() as st:
            xc = [sbuf(st, "f_xc%d" % i, [128, KC, 128], F32) for i in range(2)]; b_xc = [Buf(), Buf()]
            ot = [sbuf(st, "f_ot%d" % i, [128, D], F32) for i in range(2)]; b_ot = [Buf(), Buf()]
            for tt in range(NT):
                i = tt % 2
                DMA("sp", xc[i][:], xTv[:, :, tt * 128:(tt + 1) * 128], [], [b_xc[i]])
                for g4 in range(4):
                    bk = (tt * 4 + g4) % 8
                    for j in range(4):
                        kc = g4 * 4 + j
                        TR(PB[bk][:, j * 128:(j + 1) * 128], xc[i][:, kc, :], identf[:], [b_xc[i], b_idf], [BK[bk]])
                    CP("act" if g4 % 2 else "dve", ot[i][:, g4 * 512:(g4 + 1) * 512], PB[bk][:], [BK[bk]], [b_ot[i]])
                DMA("sp", out[tt * 128:(tt + 1) * 128, :], ot[i][:], [b_ot[i]], [])
        for k, ap in dbg_out.items():
            src = {"xT": xT, "qaT": qaT, "kaT": kaT, "va": va, "qiT": qiT, "kiT2": kiT2, "wis": wis, "mkr": mkr, "mqlT": mqlT,
                   "mkvlT": mkvlT, "sgaT": sgaT, "sgbT": sgbT, "mqTn": mqTn, "mqTr": mqTr, "mkTn": mkTn, "mkTr": mkTr, "mv": mv,
                   "MT": MT, "oaT": oaT, "obT": obT, "actT": actT}[k]
            P.dma("sp", lambda e, ap=ap, src=src: e.dma_start(out=ap, in_=src), (), ())
        P.barrier()
        with nc.Block() as block:
            P.emit(block)
    return nc


_NC_CACHE = {}


def kernel(**inputs):
    S = inputs["x"].shape[1]
    B = inputs["x"].shape[0]
    L = inputs["w_in"].shape[0]
    n_sel = min(256, S // 4)
    key = (S, L, n_sel)
    if key not in _NC_CACHE:
        _NC_CACHE[key] = build(S, L, n_sel)
    nc = _NC_CACHE[key]
    invf = inv_freq_table()
    shared = {k: np.ascontiguousarray(np.asarray(inputs[k], dtype=np.float32)) for k in W_SHAPES}
    in_maps = []
    for b in range(B):
        m = dict(shared)
        m["x"] = np.ascontiguousarray(np.asarray(inputs["x"][b], dtype=np.float32))
        m["c"] = np.ascontiguousarray(np.asarray(inputs["c"][b], dtype=np.float32))
        m["positions"] = np.ascontiguousarray(np.asarray(inputs["positions"][b], dtype=np.int32))
        m["invf"] = invf
        in_maps.append(m)
    res = run_bass_kernel_spmd(nc, in_maps, core_ids=list(range(B)))
    return np.stack([np.asarray(r["out"], dtype=np.float32) for r in res.results], axis=0)
```

```python
from contextlib import ExitStack
import numpy as np
import concourse.bass as bass
import concourse.mybir as mybir
from concourse.bass_utils import run_bass_kernel_spmd

F32 = mybir.dt.float32
BF16 = mybir.dt.bfloat16
I32 = mybir.dt.int32
AF = mybir.ActivationFunctionType
ALU = mybir.AluOpType

ENGS = ("pe", "act", "dve", "pool", "sp")
NRING = 12

D = 2048
KC = 16
IN_DIM = 7568
DFF = 5632
EPS = 1e-6
PI = float(np.pi)
TWO_PI = float(2 * np.pi)


class Buf:
    __slots__ = ("w", "r", "excl")

    def __init__(self, excl=False):
        self.w = None
        self.r = {}
        self.excl = excl


class Prog:
    def __init__(self, nc, stack):
        self.nc = nc
        self.q = {e: [] for e in ENGS}
        self.cnt = {e: 0 for e in ENGS}
        self.dcnt = {e: 0 for e in ENGS}
        self.seen = {e: {} for e in ENGS}
        self.sem = {}
        for e in ENGS:
            self.sem[("c", e)] = stack.enter_context(nc.semaphore("c_" + e))
        for e in ("sp", "pool", "act"):
            for r in range(NRING):
                self.sem[("d", e, r)] = stack.enter_context(nc.semaphore("d_%s_%d" % (e, r)))

    def _need(self, eng, waits, tok):
        if tok is None:
            return
        key, val = tok
        if eng == "pe" and key == ("c", "pe"):
            return
        if self.seen[eng].get(key, 0) >= val:
            return
        self.seen[eng][key] = val
        waits.append((key, val))

    def _deps(self, eng, reads, writes):
        waits = []
        own = ("c", eng)
        for b in reads:
            self._need(eng, waits, b.w)
            if b.excl:
                for k, v in b.r.items():
                    if k != own:
                        self._need(eng, waits, (k, v))
        for b in writes:
            self._need(eng, waits, b.w)
            for k, v in b.r.items():
                self._need(eng, waits, (k, v))
        return waits

    def _mark(self, tok, reads, writes):
        k, v = tok
        for b in reads:
            if b.r.get(k, 0) < v:
                b.r[k] = v
        for b in writes:
            b.w = tok
            b.r = {}

    def op(self, eng, fn, reads=(), writes=()):
        waits = self._deps(eng, reads, writes)
        self.cnt[eng] += 1
        tok = (("c", eng), self.cnt[eng])
        self.q[eng].append((fn, waits, tok[0], 1))
        self._mark(tok, reads, writes)

    def dma(self, eng, fn, reads=(), writes=()):
        waits = self._deps(eng, reads, writes)
        j = self.dcnt[eng]
        self.dcnt[eng] += 1
        slot = j % NRING
        use = j // NRING
        key = ("d", eng, slot)
        if use > 0:
            self._need(eng, waits, (key, 16 * use))
        tok = (key, 16 * (use + 1))
        self.q[eng].append((fn, waits, key, 16))
        self._mark(tok, reads, writes)

    def wait_all(self, eng):
        waits = []
        for e in ENGS:
            if self.cnt[e] > 0:
                self._need(eng, waits, (("c", e), self.cnt[e]))
        for e in ("sp", "pool", "act"):
            j = self.dcnt[e]
            for slot in range(NRING):
                if j > slot:
                    uses = (j - slot + NRING - 1) // NRING
                    self._need(eng, waits, (("d", e, slot), 16 * uses))
        self.q[eng].append((None, waits, None, 0))

    def barrier(self):
        for e in ENGS:
            self.wait_all(e)

    def _replay(self, name, e):
        sem = self.sem
        for fn, waits, key, inc in self.q[name]:
            for k, v in waits:
                e.wait_ge(sem[k], v)
            if fn is not None:
                fn(e).then_inc(sem[key], inc)

    def emit(self, block):
        P = self

        @block.tensor
        def _(e):
            P._replay("pe", e)

        @block.scalar
        def _(e):
            P._replay("act", e)

        @block.vector
        def _(e):
            P._replay("dve", e)

        @block.gpsimd
        def _(e):
            P._replay("pool", e)

        @block.sync
        def _(e):
            P._replay("sp", e)


W_SHAPES = {
    "g_attn": (D,), "g_ffn": (D,), "w_ada": (D, 6 * D), "b_ada": (6 * D,), "w_in": (D, IN_DIM),
    "g_qa": (128,), "g_ka": (128,), "g_mq_lat": (512,), "w_mq_up": (512, 1536), "g_mkv_lat": (256,),
    "w_mkv_up": (256, 2048), "g_qm": (192,), "g_km": (192,), "w_pa": (1024, D), "w_pb": (1024, D),
    "w_o": (D, D), "w_up": (D, 2 * DFF), "w_conv": (3, 2 * DFF), "b_conv": (2 * DFF,), "w_down": (DFF, D),
}


def inv_freq_table():
    def inv(rot):
        return (500000.0 ** (-np.arange(0, rot, 2, dtype=np.float32) / np.float32(rot))).astype(np.float32)
    t = np.concatenate([inv(32), inv(16), inv(64)]).astype(np.float32)
    return np.ascontiguousarray(np.broadcast_to(t[None, :], (128, 56))).astype(np.float32)


def build(S, L, n_sel, dbg=()):
    NT = S // 128
    NCH = S // 512
    nc = bass.Bass("TRN2", target_bir_lowering=False)

    def din(name, shape, dt=F32):
        return nc.dram_tensor(name, list(shape), dt, kind="ExternalInput").ap()

    def dscr(name, shape, dt):
        return nc.dram_tensor(name, list(shape), dt, kind="Internal").ap()

    x_in = din("x", (S, D))
    c_in = din("c", (D,))
    pos_in = din("positions", (S,), I32)
    invf_in = din("invf", (128, 56))
    Wd = {k: din(k, (L,) + v) for k, v in W_SHAPES.items()}
    out = nc.dram_tensor("out", [S, D], F32, kind="ExternalOutput").ap()
    dbg_out = {k: nc.dram_tensor("dbg_" + k, list(shp), dt, kind="ExternalOutput").ap() for k, (shp, dt) in dbg}

    xT = dscr("xT", (D, S), F32)
    qaT = dscr("qaT", (1024, S), BF16)
    kaT = dscr("kaT", (256, S), BF16)
    va = dscr("va", (S, 256), BF16)
    qiT = dscr("qiT", (1024, S), BF16)
    kiT2 = dscr("kiT2", (128, S), BF16)
    wis = dscr("wis", (S, 16), F32)
    mkr = dscr("mkr", (S, 64), F32)
    mqlT = dscr("mqlT", (512, S), BF16)
    mkvlT = dscr("mkvlT", (256, S), BF16)
    sgaT = dscr("sgaT", (D, S), F32)
    sgbT = dscr("sgbT", (D, S), F32)
    mqTn = dscr("mqTn", (1024, S), BF16)
    mqTr = dscr("mqTr", (512, S), BF16)
    mkTn = dscr("mkTn", (1024, S), BF16)
    mkTr = dscr("mkTr", (512, S), BF16)
    mv = dscr("mv", (S, 1024), BF16)
    MT = dscr("MT", (S, S), BF16)
    oaT = dscr("oaT", (1024, S), BF16)
    obT = dscr("obT", (1024, S), BF16)
    actT = dscr("actT", (DFF, S), BF16)

    with ExitStack() as st0:
        P = Prog(nc, st0)

        def MM(o, lhsT, rhs, start, stop, rd, wr):
            P.op("pe", lambda e: e.matmul(o, lhsT=lhsT, rhs=rhs, start=start, stop=stop), rd, wr)

        def TR(o, in_, ident, rd, wr):
            P.op("pe", lambda e: e.transpose(out=o, in_=in_, identity=ident), rd, wr)

        def ACT(o, in_, func, rd, wr, **kw):
            P.op("act", lambda e: e.activation(out=o, in_=in_, func=func, **kw), rd, wr)

        def TS(eng, o, in0, s1, s2, op0, op1, rd, wr):
            if s2 is None:
                P.op(eng, lambda e: e.tensor_scalar(out=o, in0=in0, scalar1=s1, scalar2=None, op0=op0), rd, wr)
            else:
                P.op(eng, lambda e: e.tensor_scalar(out=o, in0=in0, scalar1=s1, scalar2=s2, op0=op0, op1=op1), rd, wr)

        def TT(eng, o, in0, in1, op, rd, wr):
            P.op(eng, lambda e: e.tensor_tensor(out=o, in0=in0, in1=in1, op=op), rd, wr)

        def STT(eng, o, in0, scalar, in1, op0, op1, rd, wr):
            eng = "dve"
            P.op(eng, lambda e: e.scalar_tensor_tensor(out=o, in0=in0, scalar=scalar, in1=in1, op0=op0, op1=op1), rd, wr)

        def CP(eng, o, in_, rd, wr):
            if eng == "act":
                P.op("act", lambda e: e.copy(out=o, in_=in_), rd, wr)
            else:
                P.op(eng, lambda e: e.tensor_copy(out=o, in_=in_), rd, wr)

        def MS(eng, o, val, wr):
            P.op(eng, lambda e: e.memset(o, val), (), wr)

        def DMA(eng, o, in_, rd, wr):
            P.dma(eng, lambda e: e.dma_start(out=o, in_=in_), rd, wr)

        def MAX8(o, in_, rd, wr):
            P.op("dve", lambda e: e.max(out=o, in_=in_), rd, wr)

        def MREP(o, rep, vals, rd, wr):
            P.op("dve", lambda e: e.match_replace(out=o, in_to_replace=rep, in_values=vals, imm_value=-1e30), rd, wr)

        def RECIP(o, in_, rd, wr):
            P.op("dve", lambda e: e.reciprocal(out=o, in_=in_), rd, wr)

        def RSTD(t, inv_n, rd_wr):
            TS("dve", t, t, inv_n, EPS, ALU.mult, ALU.add, rd_wr, rd_wr)
            ACT(t, t, AF.Sqrt, rd_wr, rd_wr)
            P.op("dve", lambda e: e.reciprocal(out=t, in_=t), rd_wr, rd_wr)

        def skewed_steps(units):
            ns = max(len(x) for x in units)
            steps = []
            for t in range(len(units) + ns - 1):
                def step(t=t):
                    for k in reversed(range(ns)):
                        u = t - k
                        if 0 <= u < len(units) and k < len(units[u]):
                            units[u][k]()
                steps.append(step)
            return steps

        def interleave(A, B):
            tot = sum(w for _, w in A) or 1.0
            j = 0
            acc = 0.0
            for fn, w in A:
                fn()
                acc += w
                target = int(round(acc / tot * len(B)))
                while j < target:
                    B[j]()
                    j += 1
            while j < len(B):
                B[j]()
                j += 1

        def run_skewed(units):
            ns = max(len(x) for x in units)
            for t in range(len(units) + ns - 1):
                for k in reversed(range(ns)):
                    u = t - k
                    if 0 <= u < len(units) and k < len(units[u]):
                        units[u][k]()

        PB = [st0.enter_context(nc.psum_tensor("pb%d" % i, [128, 512], F32)) for i in range(8)]
        PBb = [b[:].bitcast(BF16) for b in PB]
        BK = [Buf(excl=True) for _ in range(8)]

        uid = [0]

        def sbuf(st, name, shape, dt):
            uid[0] += 1
            return st.enter_context(nc.sbuf_tensor("%s_u%d" % (name, uid[0]), list(shape), dt))

        identf = sbuf(st0, "identf", [128, 128], F32); b_idf = Buf()
        identb = sbuf(st0, "identb", [128, 128], BF16); b_idb = Buf()
        ones_f = sbuf(st0, "ones_f", [128, 128], F32); b_1f = Buf()
        ones_b = sbuf(st0, "ones_b", [128, 128], BF16); b_1b = Buf()
        tri = sbuf(st0, "tri", [128, 128], BF16); b_tri = Buf()
        trif = sbuf(st0, "trif", [128, 128], F32); b_trif = Buf()
        negm = sbuf(st0, "negm", [128, 128], F32); b_negm = Buf()
        tab = sbuf(st0, "tab", [128, NT, 112], F32); b_tab = Buf()
        modT = sbuf(st0, "modT", [128, L, 96], F32); b_mod = Buf()
        gT = sbuf(st0, "gT", [128, L, 32], F32); b_gT = Buf()
        AB = sbuf(st0, "AB", [128, L, 32], F32); b_AB = Buf()
        eps_t = sbuf(st0, "eps_t", [128, 1], F32); b_eps = Buf()
        cTb = sbuf(st0, "cTb", [128, 16], BF16); b_cTb = Buf()
        CONST = [b_idf, b_idb, b_1f, b_1b, b_tri, b_negm, b_tab, b_mod, b_AB]

        MS("pool", eps_t[:], EPS, [b_eps])
        MS("pool", identf[:], 1.0, [b_idf])
        P.op("pool", lambda e: e.affine_select(out=identf[:], in_=identf[:], pattern=[[-1, 128]], compare_op=ALU.is_equal, fill=0.0, base=0, channel_multiplier=1), [b_idf], [b_idf])
        CP("dve", identb[:], identf[:], [b_idf], [b_idb])
        MS("pool", ones_f[:], 1.0, [b_1f])
        MS("pool", ones_b[:], 1.0, [b_1b])
        MS("pool", trif[:], 1.0, [b_trif])
        P.op("pool", lambda e: e.affine_select(out=trif[:], in_=trif[:], pattern=[[1, 128]], compare_op=ALU.is_ge, fill=0.0, base=0, channel_multiplier=-1), [b_trif], [b_trif])
        CP("dve", tri[:], trif[:], [b_trif], [b_tri])
        MS("pool", negm[:], 0.0, [b_negm])
        P.op("pool", lambda e: e.affine_select(out=negm[:], in_=negm[:], pattern=[[-1, 128]], compare_op=ALU.is_ge, fill=-1e30, base=0, channel_multiplier=1), [b_negm], [b_negm])

        def rows_to_cols(st, src_rows_ap, nrows, dst, dst_buf, bank, nm):
            rt = sbuf(st, "rt_" + nm, [128, 128], F32); b_rt = Buf()
            DMA("sp", rt[0:nrows, :], src_rows_ap, [], [b_rt])
            TR(PB[bank][:, 0:nrows], rt[0:nrows, :], identf[0:nrows, 0:nrows], [b_rt, b_idf], [BK[bank]])
            CP("dve", dst, PB[bank][:, 0:nrows], [BK[bank]], [dst_buf])

        def mod_items(st, l2, bank_acc, bank_rc):
            wsl = [sbuf(st, "adasl%d" % i, [128, KC, 512], BF16) for i in range(2)]; b_wsl = [Buf(), Buf()]
            badaT = sbuf(st, "badaT", [128, 96], F32); b_bada = Buf()
            rtb = sbuf(st, "rt_bada", [128, 128], F32); b_rtb = Buf()
            items = []

            def slab_dma(sl):
                if sl < 24:
                    DMA("pool", wsl[sl % 2][:], Wd["w_ada"][l2][:, sl * 512:(sl + 1) * 512].rearrange("(kc p) n -> p kc n", p=128), [], [b_wsl[sl % 2]])

            def mk(sl):
                def f():
                    if sl == 0:
                        slab_dma(0)
                    slab_dma(sl + 1)
                    i = sl % 2
                    for jj in range(4):
                        j = sl * 4 + jj
                        for kc in range(KC):
                            MM(PB[bank_acc][:, j:j + 1], wsl[i][:, kc, jj * 128:(jj + 1) * 128], cTb[:, kc:kc + 1], kc == 0, kc == KC - 1, [b_wsl[i], b_cTb], [BK[bank_acc]])
                return f
            for sl in range(24):
                items.append(mk(sl))

            def fin():
                DMA("sp", rtb[0:96, :], Wd["b_ada"][l2].rearrange("(k p) -> k p", p=128), [], [b_rtb])
                TR(PB[bank_rc][:, 0:96], rtb[0:96, :], identf[0:96, 0:96], [b_rtb, b_idf], [BK[bank_rc]])
                CP("act", badaT[:], PB[bank_rc][:, 0:96], [BK[bank_rc]], [b_bada])
                TT("dve", modT[:, l2, :], PB[bank_acc][:, 0:96], badaT[:], ALU.add, [BK[bank_acc], b_bada], [b_mod])
                STT("dve", AB[:, l2, 0:16], modT[:, l2, 16:32], 1.0, gT[:, l2, 0:16], ALU.add, ALU.mult, [b_mod, b_gT], [b_AB])
                STT("dve", AB[:, l2, 16:32], modT[:, l2, 64:80], 1.0, gT[:, l2, 16:32], ALU.add, ALU.mult, [b_mod, b_gT], [b_AB])
            items.append(fin)
            return items

        with ExitStack() as st:
            xt = [sbuf(st, "xt%d" % i, [128, D], F32) for i in range(2)]; b_xt = [Buf(), Buf()]
            xs = [sbuf(st, "xs%d" % i, [128, KC, 128], F32) for i in range(2)]; b_xs = [Buf(), Buf()]
            xTv = xT.rearrange("(kc p) s -> p kc s", p=128)
            for tt in range(NT):
                i = tt % 2
                DMA("sp", xt[i][:], x_in[tt * 128:(tt + 1) * 128, :], [], [b_xt[i]])
                for g4 in range(4):
                    bk = (tt * 4 + g4) % 8
                    for j in range(4):
                        kc = g4 * 4 + j
                        TR(PB[bk][:, j * 128:(j + 1) * 128], xt[i][:, kc * 128:(kc + 1) * 128], identf[:], [b_xt[i], b_idf], [BK[bk]])
                    CP("act" if g4 % 2 else "dve", xs[i][:, g4 * 4:(g4 + 1) * 4, :], PB[bk][:].rearrange("p (j t) -> p j t", j=4), [BK[bk]], [b_xs[i]])
                DMA("sp", xTv[:, :, tt * 128:(tt + 1) * 128], xs[i][:], [b_xs[i]], [])

            posr = sbuf(st, "posr", [16, 128], I32); b_posr = Buf()
            posf = sbuf(st, "posf", [16, 128], F32); b_posf = Buf()
            posT = sbuf(st, "posT", [128, 16], F32); b_posT = Buf()
            invf = sbuf(st, "invf", [128, 56], F32); b_invf = Buf()
            kf = sbuf(st, "kf", [128, NT, 56], F32); b_kf = Buf()
            ki_ = sbuf(st, "ki_", [128, NT, 56], I32); b_ki = Buf()
            DMA("sp", posr[0:NT, :], pos_in.rearrange("(t p) -> t p", p=128), [], [b_posr])
            DMA("sp", invf[:], invf_in[:, :], [], [b_invf])
            CP("dve", posf[0:NT, :], posr[0:NT, :], [b_posr], [b_posf])
            TR(PB[0][:, 0:NT], posf[0:NT, :], identf[0:NT, 0:NT], [b_posf, b_idf], [BK[0]])
            CP("dve", posT[:, 0:NT], PB[0][:, 0:NT], [BK[0]], [b_posT])
            sinv = tab[:, :, 0:56]
            cosv = tab[:, :, 56:112]
            for tt in range(NT):
                TS("dve", tab[:, tt, 0:56], invf[:], posT[:, tt:tt + 1], None, ALU.mult, None, [b_invf, b_posT], [b_tab])
            C1 = 6.28125
            C2 = float(2 * np.pi - 6.28125)
            RW = [b_tab, b_kf, b_ki]
            TS("dve", kf[:], sinv, float(1.0 / (2 * np.pi)), None, ALU.mult, None, RW, RW)
            CP("dve", ki_[:], kf[:], RW, RW)
            CP("dve", kf[:], ki_[:], RW, RW)
            STT("dve", sinv, kf[:], -C1, sinv, ALU.mult, ALU.add, RW, RW)
            STT("dve", sinv, kf[:], -C2, sinv, ALU.mult, ALU.add, RW, RW)
            TS("dve", kf[:], sinv, PI, -TWO_PI, ALU.is_gt, ALU.mult, RW, RW)
            TT("dve", sinv, sinv, kf[:], ALU.add, RW, RW)
            TS("dve", kf[:], sinv, -PI, TWO_PI, ALU.is_lt, ALU.mult, RW, RW)
            TT("dve", sinv, sinv, kf[:], ALU.add, RW, RW)
            TS("dve", cosv, sinv, PI / 2, None, ALU.add, None, RW, RW)
            TS("dve", kf[:], cosv, PI, -TWO_PI, ALU.is_gt, ALU.mult, RW, RW)
            TT("dve", cosv, cosv, kf[:], ALU.add, RW, RW)
            ACT(tab[:], tab[:], AF.Sin, RW, RW)

            cT = sbuf(st, "cT", [128, 16], F32); b_cT = Buf()
            rows_to_cols(st, c_in.rearrange("(k p) -> k p", p=128), 16, cT[:], b_cT, 1, "c")
            ACT(cT[:], cT[:], AF.Silu, [b_cT], [b_cT])
            CP("dve", cTb[:], cT[:], [b_cT], [b_cTb])
            for l in range(L):
                rows_to_cols(st, Wd["g_attn"][l].rearrange("(k p) -> k p", p=128), 16, gT[:, l, 0:16], b_gT, 2, "ga%d" % l)
                rows_to_cols(st, Wd["g_ffn"][l].rearrange("(k p) -> k p", p=128), 16, gT[:, l, 16:32], b_gT, 3, "gf%d" % l)
            for it_ in mod_items(st, 0, 5, 4):
                it_()
        P.barrier()

        xTv = xT.rearrange("(kc p) s -> p kc s", p=128)

        def norm_phase(st, hT, b_hT, A_ap, B_ap):
            xc = [sbuf(st, "n_xc%d" % i, [128, KC, 512], F32) for i in range(2)]; b_xc = [Buf(), Buf()]
            sq = [sbuf(st, "n_sq%d" % i, [128, 512], F32) for i in range(2)]; b_sq = [Buf(), Buf()]
            rs = sbuf(st, "n_rs", [128, 512], F32); b_rs = Buf()
            tmp = [sbuf(st, "n_tmp%d" % i, [128, 512], F32) for i in range(2)]; b_tmp = [Buf(), Buf()]
            for ch in range(NCH):
                i = ch % 2
                cs = slice(ch * 512, (ch + 1) * 512)
                DMA("sp", xc[i][:], xTv[:, :, cs], [], [b_xc[i]])
                bk = ch % 2
                for kc in range(KC):
                    j = kc % 2
                    ACT(sq[j][:], xc[i][:, kc, :], AF.Square, [b_xc[i]], [b_sq[j]])
                    MM(PB[bk][:], ones_f[:], sq[j][:], kc == 0, kc == KC - 1, [b_1f, b_sq[j]], [BK[bk]])
                CP("dve", rs[:], PB[bk][:], [BK[bk]], [b_rs])
                RSTD(rs[:], 1.0 / D, [b_rs])
                for kc in range(KC):
                    j = kc % 2
                    TT("dve" if kc % 2 else "pool", tmp[j][:], xc[i][:, kc, :], rs[:], ALU.mult, [b_xc[i], b_rs], [b_tmp[j]])
                    ACT(hT[:, kc, cs], tmp[j][:], AF.Identity, [b_tmp[j], b_AB, b_mod], [b_hT], scale=A_ap[:, kc:kc + 1], bias=B_ap[:, kc:kc + 1])

        def bcast_load(st, name, src_row_ap, n):
            t = sbuf(st, name, [128, n], F32)
            b = Buf()
            DMA("sp", t[:], src_row_ap.to_broadcast([128, n]), [], [b])
            return t, b

        def rope(st_tiles, src3, dst3, nh, half, cos_ap, sin_ap, rd, wr, eng="pool"):
            ta, tb_, b_t = st_tiles
            cosb = cos_ap.unsqueeze(1).to_broadcast([128, nh, half])
            sinb = sin_ap.unsqueeze(1).to_broadcast([128, nh, half])
            t1 = src3[:, :, 0:half]
            t2 = src3[:, :, half:2 * half]
            av = ta[:, 0:nh * half].rearrange("p (h d) -> p h d", h=nh)
            bv = tb_[:, 0:nh * half].rearrange("p (h d) -> p h d", h=nh)
            TT(eng, av, t1, cosb, ALU.mult, rd + [b_tab], [b_t])
            TT(eng, bv, t2, sinb, ALU.mult, rd + [b_tab], [b_t])
            TT(eng, dst3[:, :, 0:half], av, bv, ALU.subtract, [b_t], wr)
            TT(eng, av, t2, cosb, ALU.mult, rd + [b_tab], [b_t] + wr)
            TT(eng, bv, t1, sinb, ALU.mult, rd + [b_tab], [b_t])
            TT(eng, dst3[:, :, half:2 * half], av, bv, ALU.add, [b_t], wr)

        def make_p3(st, qt_list, banks):
            kiS = sbuf(st, "kiS", [128, S], BF16); b_kiS = Buf()
            qit = [sbuf(st, "qit%d" % i, [128, 8, 128], BF16) for i in range(2)]; b_qit = [Buf(), Buf()]
            sc = [sbuf(st, "sc%d" % i, [128, S], F32) for i in range(2)]; b_sc = [Buf(), Buf()]
            wk = sbuf(st, "wk", [128, S], F32); b_wk = Buf()
            rl = [sbuf(st, "rl%d" % i, [128, 512], BF16) for i in range(3)]; b_rl = [Buf(), Buf(), Buf()]
            wv = [sbuf(st, "wv%d" % i, [128, 16], F32) for i in range(2)]; b_wv = [Buf(), Buf()]
            dg = [sbuf(st, "dg%d" % i, [128, 16, 128], BF16) for i in range(2)]; b_dg = [Buf(), Buf()]
            mx = sbuf(st, "mx", [128, 8], F32); b_mx = Buf()
            Mb = [sbuf(st, "Mb%d" % i, [128, S], BF16) for i in range(2)]; b_Mb = [Buf(), Buf()]
            mstg = [sbuf(st, "mstg%d" % i, [128, 4, 128], BF16) for i in range(2)]; b_mstg = [Buf(), Buf()]
            qiTv = qiT.rearrange("(j p) s -> p j s", p=128)
            cnt = {"s": 0, "a": 0, "t": 0, "r": 0}
            items = []

            def first():
                DMA("sp", kiS[:], kiT2[:, :], [], [b_kiS])
            items.append((first, 0.1))

            def tile_items(qt, i):
                W = (qt + 1) * 128
                qs = slice(qt * 128, (qt + 1) * 128)
                out = []

                def prep():
                    DMA("sp", wv[i][:], wis[qs, :], [], [b_wv[i]])
                    DMA("sp", qit[i][:], qiTv[:, :, qs], [], [b_qit[i]])
                    for h in range(16):
                        TS("pool", dg[i][:, h, :], identb[:], wv[i][:, h:h + 1], None, ALU.mult, None, [b_idb, b_wv[i]], [b_dg[i]])
                out.append((prep, 0.5))

                def chunk(k0, kw):
                    def f():
                        ab = banks["acc"][cnt["a"] % len(banks["acc"])]
                        cnt["a"] += 1
                        sbs = []

                        def dots(h):
                            sb = banks["score"][cnt["s"] % len(banks["score"])]
                            cnt["s"] += 1
                            pr = slice((h % 2) * 64, (h % 2) * 64 + 64)
                            MM(PB[sb][:, 0:kw], qit[i][pr, h // 2, :], kiS[pr, k0:k0 + kw], True, True, [b_qit[i], b_kiS], [BK[sb]])
                            sbs.append(sb)
                        dots(0)
                        for h in range(16):
                            if h + 1 < 16:
                                dots(h + 1)
                            r = cnt["r"] % 3
                            cnt["r"] += 1
                            ACT(rl[r][:, 0:kw], PB[sbs[h]][:, 0:kw], AF.Relu, [BK[sbs[h]]], [b_rl[r]])
                            MM(PB[ab][:, 0:kw], dg[i][:, h, :], rl[r][:, 0:kw], h == 0, h == 15, [b_dg[i], b_rl[r]], [BK[ab]])
                        CP("act", sc[i][:, k0:k0 + kw], PB[ab][:, 0:kw], [BK[ab]], [b_sc[i]])
                    return f
                for k0 in range(0, W, 512):
                    out.append((chunk(k0, min(512, W - k0)), 2.0 * min(512, W - k0) / 512))

                def causal():
                    TT("dve", sc[i][:, qt * 128:W], sc[i][:, qt * 128:W], negm[:], ALU.add, [b_sc[i], b_negm], [b_sc[i]])
                out.append((causal, 0.2))
                if W > n_sel:
                    rounds = n_sel // 8

                    def topk(r0, r1):
                        def f():
                            for r in range(r0, r1):
                                src = sc[i] if r == 0 else wk
                                MAX8(mx[:], src[:, 0:W], [b_sc[i], b_wk], [b_mx])
                                if r < rounds - 1:
                                    MREP(wk[:, 0:W], mx[:], src[:, 0:W], [b_sc[i], b_mx, b_wk], [b_wk])
                        return f
                    for r0 in range(0, rounds, 4):
                        out.append((topk(r0, min(rounds, r0 + 4)), (min(rounds, r0 + 4) - r0) * 2.0 * W / 1000.0))

                    def mask():
                        TS("dve", Mb[i][:, 0:W], sc[i][:, 0:W], mx[:, 7:8], None, ALU.is_ge, None, [b_sc[i], b_mx], [b_Mb[i]])
                else:
                    def mask():
                        TS("dve", Mb[i][:, 0:W], sc[i][:, 0:W], -1e29, None, ALU.is_ge, None, [b_sc[i]], [b_Mb[i]])
                out.append((mask, W / 1000.0))

                def trans(kb0, nb):
                    def f():
                        bk = banks["tr"][cnt["t"] % len(banks["tr"])]
                        sgi = cnt["t"] % 2
                        cnt["t"] += 1
                        for j in range(nb):
                            TR(PBb[bk][:, j * 128:(j + 1) * 128], Mb[i][:, (kb0 + j) * 128:(kb0 + j + 1) * 128], identb[:], [b_Mb[i], b_idb], [BK[bk]])
                        CP("act", mstg[sgi][:, 0:nb, :], PBb[bk][:, 0:nb * 128].rearrange("p (j t) -> p j t", j=nb), [BK[bk]], [b_mstg[sgi]])
                        DMA("sp", MT[kb0 * 128:(kb0 + nb) * 128, qs].rearrange("(j p) t -> p j t", p=128), mstg[sgi][:, 0:nb, :], [b_mstg[sgi]], [])
                    return f
                for kb0 in range(0, qt + 1, 4):
                    out.append((trans(kb0, min(4, qt + 1 - kb0)), 0.5))
                nfront = 1 + len(range(0, W, 512))
                ntr = len(range(0, qt + 1, 4))
                return out[:nfront], out[nfront:len(out) - ntr], out[len(out) - ntr:]

            fb = [tile_items(qt, n_ % 2) for n_, qt in enumerate(qt_list)]
            if fb:
                items.extend(fb[0][0])
            for n_ in range(len(fb)):
                if n_ + 1 < len(fb):
                    items.extend(fb[n_ + 1][0])
                items.extend(fb[n_][1])
                if n_ >= 1:
                    items.extend(fb[n_ - 1][2])
            if fb:
                items.extend(fb[-1][2])
            return items

        for l in range(L):
            with ExitStack() as st:
                hT = sbuf(st, "hT", [128, KC, S], BF16); b_hT = Buf()
                with ExitStack() as st1:
                    norm_phase(st1, hT, b_hT, AB[:, l, 0:16], modT[:, l, 0:16])
                P.barrier()
                slab = [sbuf(st, "slab%d" % i, [128, KC, 512], BF16) for i in range(2)]; b_slab = [Buf(), Buf()]
                st_tm = ExitStack()
                gqa, b_gqa = bcast_load(st_tm, "gqa", Wd["g_qa"][l:l + 1, :], 128)
                gka, b_gka = bcast_load(st_tm, "gka", Wd["g_ka"][l:l + 1, :], 128)
                NB = 8
                qn = [sbuf(st_tm, "qn%d" % i, [128, 512], F32) for i in range(NB)]; b_qn = [Buf() for _ in range(NB)]
                qb = [sbuf(st_tm, "qb%d" % i, [128, 512], BF16) for i in range(NB)]; b_qb = [Buf() for _ in range(NB)]
                ssq = [sbuf(st_tm, "ssq%d" % i, [128, 8], F32) for i in range(NB)]; b_ssq = [Buf() for _ in range(NB)]
                junk = [sbuf(st_tm, "junk%d" % i, [128, 128], F32) for i in range(2)]; b_junk = [Buf(), Buf()]
                rts = [(sbuf(st_tm, "rt_a%d" % i, [128, 64], F32), sbuf(st_tm, "rt_b%d" % i, [128, 64], F32), Buf()) for i in range(4)]
                stg = [sbuf(st_tm, "stg%d" % i, [128, 4, 128], BF16) for i in range(4)]; b_stg = [Buf() for _ in range(4)]
                ki2 = [sbuf(st_tm, "ki2_%d" % i, [128, 128], BF16) for i in range(4)]; b_ki2 = [Buf() for _ in range(4)]
                slab_n = [0]

                def load_slab(col_ranges):
                    i = slab_n[0] % 2
                    slab_n[0] += 1
                    off = 0
                    for (c0, n) in col_ranges:
                        DMA("pool", slab[i][:, :, off:off + n], Wd["w_in"][l][:, c0:c0 + n].rearrange("(kc p) n -> p kc n", p=128), [], [b_slab[i]])
                        off += n
                    return i, off

                def tok_mm(i, ncols, tt, bk):
                    for kc in range(KC):
                        MM(PB[bk][:, 0:ncols], hT[:, kc, tt * 128:(tt + 1) * 128], slab[i][:, kc, 0:ncols], kc == 0, kc == KC - 1, [b_hT, b_slab[i]], [BK[bk]])

                units = []

                def qk_unit(u, i, tt, nh, g_t, b_g, dst_rows, nblk, has_va):
                    b = u % NB; bk = u % 3; tb = 4 + u % 4; sg = u % 4; rt = rts[u % 4]; jk = u % 2
                    ts_ = slice(tt * 128, (tt + 1) * 128)

                    def s0():
                        tok_mm(i, 512, tt, bk)

                    def s1():
                        CP("act", qn[b][:], PB[bk][:], [BK[bk]], [b_qn[b]])

                    def s2():
                        for h in range(nh):
                            ACT(junk[jk][:], qn[b][:, h * 128:(h + 1) * 128], AF.Square, [b_qn[b]], [b_junk[jk], b_ssq[b]], accum_out=ssq[b][:, h:h + 1])

                    def s3():
                        ACT(ssq[b][:, 0:nh], ssq[b][:, 0:nh], AF.Sqrt, [b_ssq[b], b_eps], [b_ssq[b]], scale=1.0 / 128, bias=eps_t[:, 0:1])

                    def s4():
                        RECIP(ssq[b][:, 0:nh], ssq[b][:, 0:nh], [b_ssq[b]], [b_ssq[b]])

                    def s5():
                        for h in range(nh):
                            hs = slice(h * 128, (h + 1) * 128)
                            STT("dve", qn[b][:, hs], qn[b][:, hs], ssq[b][:, h:h + 1], g_t[:], ALU.mult, ALU.mult, [b_qn[b], b_ssq[b], b_g], [b_qn[b]])

                    def s6():
                        CP("pool", qb[b][:], qn[b][:], [b_qn[b]], [b_qb[b]])
                        s3_ = qn[b][:, 0:nh * 128].rearrange("p (h d) -> p h d", h=nh)[:, :, 0:32]
                        d3_ = qb[b][:, 0:nh * 128].rearrange("p (h d) -> p h d", h=nh)[:, :, 0:32]
                        rope(rt, s3_, d3_, nh, 16, tab[:, tt, 56:72], tab[:, tt, 0:16], [b_qn[b]], [b_qb[b]])

                    def s7():
                        for j in range(nblk):
                            TR(PBb[tb][:, j * 128:(j + 1) * 128], qb[b][:, j * 128:(j + 1) * 128], identb[:], [b_qb[b], b_idb], [BK[tb]])

                    def s8():
                        CP("act", stg[sg][:, 0:nblk, :], PBb[tb][:, 0:nblk * 128].rearrange("p (j t) -> p j t", j=nblk), [BK[tb]], [b_stg[sg]])
                        DMA("sp", dst_rows[0:nblk * 128, ts_].rearrange("(j p) t -> p j t", p=128), stg[sg][:, 0:nblk, :], [b_stg[sg]], [])
                        if has_va:
                            DMA("sp", va[ts_, :], qb[b][:, 256:512], [b_qb[b]], [])
                    return [s0, s1, s2, s3, s4, s5, s6, s7, s8]

                def qi_unit(u, i, tt, dst_rows):
                    b = u % NB; bk = u % 3; tb = 4 + u % 4; sg = u % 4; rt = rts[u % 4]
                    ts_ = slice(tt * 128, (tt + 1) * 128)

                    def s0():
                        tok_mm(i, 512, tt, bk)

                    def s1():
                        CP("act", qn[b][:], PB[bk][:], [BK[bk]], [b_qn[b]])

                    def s2():
                        CP("pool", qb[b][:], qn[b][:], [b_qn[b]], [b_qb[b]])
                        s3_ = qn[b][:].rearrange("p (h d) -> p h d", h=8)[:, :, 0:16]
                        d3_ = qb[b][:].rearrange("p (h d) -> p h d", h=8)[:, :, 0:16]
                        rope(rt, s3_, d3_, 8, 8, tab[:, tt, 72:80], tab[:, tt, 16:24], [b_qn[b]], [b_qb[b]])

                    def s3():
                        for j in range(4):
                            TR(PBb[tb][:, j * 128:(j + 1) * 128], qb[b][:, j * 128:(j + 1) * 128], identb[:], [b_qb[b], b_idb], [BK[tb]])

                    def s4():
                        CP("act", stg[sg][:, 0:4, :], PBb[tb][:, 0:512].rearrange("p (j t) -> p j t", j=4), [BK[tb]], [b_stg[sg]])
                        DMA("sp", dst_rows[0:512, ts_].rearrange("(j p) t -> p j t", p=128), stg[sg][:, 0:4, :], [b_stg[sg]], [])
                    nop = lambda: None
                    return [s0, s1, nop, nop, nop, nop, s2, s3, s4]

                def misc_unit(u, i, tt):
                    b = u % NB; bk = u % 3; tb = 4 + u % 4; sg = u % 4; rt = rts[u % 4]; kk = u % 4
                    ts_ = slice(tt * 128, (tt + 1) * 128)

                    def s0():
                        tok_mm(i, 144, tt, bk)

                    def s1():
                        CP("act", qn[b][:, 0:144], PB[bk][:, 0:144], [BK[bk]], [b_qn[b]])

                    def s2():
                        CP("pool", ki2[kk][:, 0:64], qn[b][:, 0:64], [b_qn[b]], [b_ki2[kk]])
                        s3_ = qn[b][:, 0:64].rearrange("p (h d) -> p h d", h=1)[:, :, 0:16]
                        d3_ = ki2[kk][:, 0:64].rearrange("p (h d) -> p h d", h=1)[:, :, 0:16]
                        rope(rt, s3_, d3_, 1, 8, tab[:, tt, 72:80], tab[:, tt, 16:24], [b_qn[b]], [b_ki2[kk]])
                        CP("pool", ki2[kk][:, 64:128], ki2[kk][:, 0:64], [b_ki2[kk]], [b_ki2[kk]])
                        TS("pool", qn[b][:, 64:80], qn[b][:, 64:80], float(1024 ** -0.5), None, ALU.mult, None, [b_qn[b]], [b_qn[b]])

                    def s3():
                        TR(PBb[tb][:, 0:128], ki2[kk][:], identb[:], [b_ki2[kk], b_idb], [BK[tb]])
                        DMA("sp", wis[ts_, :], qn[b][:, 64:80], [b_qn[b]], [])
                        DMA("sp", mkr[ts_, :], qn[b][:, 80:144], [b_qn[b]], [])

                    def s4():
                        CP("act", stg[sg][:, 0, :], PBb[tb][:, 0:128], [BK[tb]], [b_stg[sg]])
                        DMA("sp", kiT2[:, ts_], stg[sg][:, 0, :], [b_stg[sg]], [])
                    nop = lambda: None
                    return [s0, s1, nop, nop, nop, nop, s2, s3, s4]

                groups = [("qa", [(0, 512)]), ("qa", [(512, 512)]), ("ka", [(1024, 512)]), ("qi", [(1536, 512)]), ("qi", [(2048, 512)]), ("misc", [(2560, 80), (3408, 64)])]

                def slab_load(gi):
                    i = gi % 2
                    off = 0
                    for (c0, n) in groups[gi][1]:
                        DMA("pool", slab[i][:, :, off:off + n], Wd["w_in"][l][:, c0:c0 + n].rearrange("(kc p) n -> p kc n", p=128), [], [b_slab[i]])
                        off += n

                def with_pre(stages, pre):
                    s0 = stages[0]

                    def s0p():
                        pre()
                        s0()
                    return [s0p] + stages[1:]

                u = 0
                for gi, (kind, cr) in enumerate(groups):
                    i = gi % 2
                    for tt in range(NT):
                        if kind == "qa":
                            stages = qk_unit(u, i, tt, 4, gqa, b_gqa, qaT[gi * 512:(gi + 1) * 512, :], 4, False)
                        elif kind == "ka":
                            stages = qk_unit(u, i, tt, 2, gka, b_gka, kaT, 2, True)
                        elif kind == "qi":
                            stages = qi_unit(u, i, tt, qiT[(gi - 3) * 512:(gi - 2) * 512, :])
                        else:
                            stages = misc_unit(u, i, tt)
                        if tt == 0:
                            def pre(gi=gi):
                                if gi == 0:
                                    slab_load(0)
                                if gi + 1 < len(groups):
                                    slab_load(gi + 1)
                            stages = with_pre(stages, pre)
                        units.append(stages)
                        u += 1
                run_skewed(units)
                slab_n[0] = len(groups)
                P.barrier()
                st_tm.close()
                st_f = ExitStack()
                lat = sbuf(st_f, "lat", [128, 4, 512], F32); b_lat = Buf()
                latb = [sbuf(st_f, "latb%d" % i, [128, 512], BF16) for i in range(2)]; b_latb = [Buf(), Buf()]
                lsq = [sbuf(st_f, "lsq%d" % i, [128, 512], F32) for i in range(2)]; b_lsq = [Buf(), Buf()]
                lrs = sbuf(st_f, "lrs", [128, 512], F32); b_lrs = Buf()
                glatT = sbuf(st_f, "glatT", [128, 8], F32); b_glat = Buf()
                rows_to_cols(st_f, Wd["g_mq_lat"][l].rearrange("(k p) -> k p", p=128), 4, glatT[:, 0:4], b_glat, 6, "gmq")
                rows_to_cols(st_f, Wd["g_mkv_lat"][l].rearrange("(k p) -> k p", p=128), 2, glatT[:, 4:6], b_glat, 6, "gmkv")
                for (c0, nm, dst, goff) in ((2640, 4, mqlT, 0), (3152, 2, mkvlT, 4)):
                    i, _ = load_slab([(c0, nm * 128)])
                    for ch in range(NCH):
                        cs = slice(ch * 512, (ch + 1) * 512)
                        for m in range(nm):
                            bk = m % 4
                            for kc in range(KC):
                                MM(PB[bk][:], slab[i][:, kc, m * 128:(m + 1) * 128], hT[:, kc, cs], kc == 0, kc == KC - 1, [b_slab[i], b_hT], [BK[bk]])
                            CP("dve", lat[:, m, :], PB[bk][:], [BK[bk]], [b_lat])
                            ACT(lsq[m % 2][:], PB[bk][:], AF.Square, [BK[bk]], [b_lsq[m % 2]])
                            MM(PB[4][:], ones_f[:], lsq[m % 2][:], m == 0, m == nm - 1, [b_1f, b_lsq[m % 2]], [BK[4]])
                        CP("dve", lrs[:], PB[4][:], [BK[4]], [b_lrs])
                        RSTD(lrs[:], 1.0 / (nm * 128), [b_lrs])
                        for m in range(nm):
                            STT("dve", latb[m % 2][:], lat[:, m, :], glatT[:, goff + m:goff + m + 1], lrs[:], ALU.mult, ALU.mult, [b_lat, b_glat, b_lrs], [b_latb[m % 2]])
                            DMA("sp", dst[m * 128:(m + 1) * 128, cs], latb[m % 2][:], [b_latb[m % 2]], [])
                P.barrier()
                st_f.close()
                sg = [sbuf(st, "sg%d" % i, [128, 512], F32) for i in range(2)]; b_sg = [Buf(), Buf()]
                gslabs = [(c0 + s4 * 512, dst, s4) for (c0, dst) in ((3472, sgaT), (5520, sgbT)) for s4 in range(4)]
                gslab_i = {}

                def gate_slab(gn):
                    if gn < len(gslabs) and gn not in gslab_i:
                        gslab_i[gn] = load_slab([(gslabs[gn][0], 512)])[0]
                gate_items = []
                gcnt = [0]

                def gate_item(gn, ch, m):
                    def f():
                        if ch == 0 and m == 0:
                            gate_slab(gn)
                            gate_slab(gn + 1)
                        i = gslab_i[gn]
                        c0_, dst, s4 = gslabs[gn]
                        n = gcnt[0]
                        gcnt[0] += 1
                        cs = slice(ch * 512, (ch + 1) * 512)
                        bk = 4 + n % 2
                        for kc in range(KC):
                            MM(PB[bk][:], slab[i][:, kc, m * 128:(m + 1) * 128], hT[:, kc, cs], kc == 0, kc == KC - 1, [b_slab[i], b_hT], [BK[bk]])
                        ACT(sg[n % 2][:], PB[bk][:], AF.Sigmoid, [BK[bk]], [b_sg[n % 2]])
                        r0 = s4 * 512 + m * 128
                        DMA("sp", dst[r0:r0 + 128, cs], sg[n % 2][:], [b_sg[n % 2]], [])
                    return f
                for gn in range(len(gslabs)):
                    for ch in range(NCH):
                        for m in range(4):
                            gate_items.append(gate_item(gn, ch, m))
                p3 = make_p3(st, list(range(NT)), dict(score=[0, 1], acc=[2], tr=[3]))
                side = gate_items
                if l + 1 < L:
                    mi_ = mod_items(st, l + 1, 6, 7)
                    side = []
                    step_ = max(1, len(gate_items) // len(mi_))
                    k_ = 0
                    for gi_, g_ in enumerate(gate_items):
                        side.append(g_)
                        if (gi_ + 1) % step_ == 0 and k_ < len(mi_):
                            side.append(mi_[k_]); k_ += 1
                    side.extend(mi_[k_:])
                interleave(p3, side)
            P.barrier()

            with ExitStack() as st:
                mqlS = sbuf(st, "mqlS", [128, 4, S], BF16); b_mqlS = Buf()
                mkvlS = sbuf(st, "mkvlS", [128, 2, S], BF16); b_mkvlS = Buf()
                wq = sbuf(st, "wq", [128, 4, 1536], BF16); b_wq = Buf()
                wkv = sbuf(st, "wkv", [128, 2, 2048], BF16); b_wkv = Buf()
                DMA("sp", mqlS[:], mqlT.rearrange("(kc p) s -> p kc s", p=128), [], [b_mqlS])
                DMA("sp", mkvlS[:], mkvlT.rearrange("(kc p) s -> p kc s", p=128), [], [b_mkvlS])
                DMA("pool", wq[:], Wd["w_mq_up"][l].rearrange("(kc p) n -> p kc n", p=128), [], [b_wq])
                DMA("pool", wkv[:], Wd["w_mkv_up"][l].rearrange("(kc p) n -> p kc n", p=128), [], [b_wkv])
                gqm, b_gqm = bcast_load(st, "gqm", Wd["g_qm"][l:l + 1, :], 192)
                gkm, b_gkm = bcast_load(st, "gkm", Wd["g_km"][l:l + 1, :], 192)
                NB = 8
                qn = [sbuf(st, "m_qn%d" % i, [128, 512], F32) for i in range(NB)]; b_qn = [Buf() for _ in range(NB)]
                qb = [sbuf(st, "m_qb%d" % i, [128, 384], BF16) for i in range(NB)]; b_qb = [Buf() for _ in range(NB)]
                kb_ = [sbuf(st, "m_kb%d" % i, [128, 384], BF16) for i in range(NB)]; b_kb = [Buf() for _ in range(NB)]
                kr = [sbuf(st, "m_kr%d" % i, [128, 128], F32) for i in range(NB)]; b_kr = [Buf() for _ in range(NB)]
                vb = [sbuf(st, "m_vb%d" % i, [128, 256], BF16) for i in range(NB)]; b_vb = [Buf() for _ in range(NB)]
                ssq = [sbuf(st, "m_ssq%d" % i, [128, 4], F32) for i in range(NB)]; b_ssq = [Buf() for _ in range(NB)]
                junk = [sbuf(st, "m_junk%d" % i, [128, 192], F32) for i in range(2)]; b_junk = [Buf(), Buf()]
                rts = [(sbuf(st, "m_rt_a%d" % i, [128, 64], F32), sbuf(st, "m_rt_b%d" % i, [128, 64], F32), Buf()) for i in range(4)]
                stg = [sbuf(st, "m_stg%d" % i, [128, 2, 128], BF16) for i in range(4)]; b_stg = [Buf() for _ in range(4)]
                stgr = [sbuf(st, "m_stgr%d" % i, [64, 2, 128], BF16) for i in range(4)]; b_stgr = [Buf() for _ in range(4)]
                mkr_t = [sbuf(st, "m_mkr%d" % i, [128, 64], F32) for i in range(3)]; b_mkr = [Buf() for _ in range(3)]
                ssr = [sbuf(st, "m_ssr%d" % i, [128, 1], F32) for i in range(3)]; b_ssr = [Buf() for _ in range(3)]

                def mq_unit(u, tt, g):
                    b = u % NB; bk = 6 + u % 2; tb = 4 + u % 2; sg = u % 4; rt = rts[u % 4]; jk = u % 2
                    ts_ = slice(tt * 128, (tt + 1) * 128)

                    def s0():
                        for kc in range(4):
                            MM(PB[bk][:, 0:384], mqlS[:, kc, ts_], wq[:, kc, g * 384:(g + 1) * 384], kc == 0, kc == 3, [b_mqlS, b_wq], [BK[bk]])

                    def s1():
                        CP("act", qn[b][:, 0:384], PB[bk][:, 0:384], [BK[bk]], [b_qn[b]])

                    def s2():
                        for h in range(2):
                            ACT(junk[jk][:, 0:192], qn[b][:, h * 192:(h + 1) * 192], AF.Square, [b_qn[b]], [b_junk[jk], b_ssq[b]], accum_out=ssq[b][:, h:h + 1])

                    def s3():
                        pass

                    def s4():
                        ACT(ssq[b][:, 0:2], ssq[b][:, 0:2], AF.Sqrt, [b_ssq[b], b_eps], [b_ssq[b]], scale=1.0 / 192, bias=eps_t[:, 0:1])

                    def s5():
                        RECIP(ssq[b][:, 0:2], ssq[b][:, 0:2], [b_ssq[b]], [b_ssq[b]])

                    def s6():
                        for h in range(2):
                            hs = slice(h * 192, (h + 1) * 192)
                            STT("dve", qn[b][:, hs], qn[b][:, hs], ssq[b][:, h:h + 1], gqm[:], ALU.mult, ALU.mult, [b_qn[b], b_ssq[b], b_gqm], [b_qn[b]])

                    def s7():
                        CP("pool", qb[b][:], qn[b][:, 0:384], [b_qn[b]], [b_qb[b]])
                        s3_ = qn[b][:, 0:384].rearrange("p (h d) -> p h d", h=2)[:, :, 128:192]
                        d3_ = qb[b][:].rearrange("p (h d) -> p h d", h=2)[:, :, 128:192]
                        rope(rt, s3_, d3_, 2, 32, tab[:, tt, 80:112], tab[:, tt, 24:56], [b_qn[b]], [b_qb[b]], eng="dve")

                    def s8():
                        for h in range(2):
                            TR(PBb[tb][:, h * 128:(h + 1) * 128], qb[b][:, h * 192:h * 192 + 128], identb[:], [b_qb[b], b_idb], [BK[tb]])
                            TR(PBb[tb][0:64, 256 + h * 128:256 + (h + 1) * 128], qb[b][:, h * 192 + 128:(h + 1) * 192], identb[:], [b_qb[b], b_idb], [BK[tb]])

                    def s9():
                        CP("act", stg[sg][:, :, :], PBb[tb][:, 0:256].rearrange("p (j t) -> p j t", j=2), [BK[tb]], [b_stg[sg]])
                        CP("act", stgr[sg][:, :, :], PBb[tb][0:64, 256:512].rearrange("p (j t) -> p j t", j=2), [BK[tb]], [b_stgr[sg]])
                        DMA("sp", mqTn[g * 256:(g + 1) * 256, ts_].rearrange("(j p) t -> p j t", p=128), stg[sg][:, :, :], [b_stg[sg]], [])
                        DMA("sp", mqTr[g * 128:(g + 1) * 128, ts_].rearrange("(j p) t -> p j t", p=64), stgr[sg][:, :, :], [b_stgr[sg]], [])
                    return [s0, s1, s2, s3, s4, s5, s6, s7, s8, s9]

                def mkv_unit(u, tt, g):
                    b = u % NB; bk = 6 + u % 2; tb = 4 + u % 2; sg = u % 4; rt = rts[u % 4]; jk = u % 2; tr3 = tt % 3
                    ts_ = slice(tt * 128, (tt + 1) * 128)

                    def s0():
                        for kc in range(2):
                            MM(PB[bk][:], mkvlS[:, kc, ts_], wkv[:, kc, g * 512:(g + 1) * 512], kc == 0, kc == 1, [b_mkvlS, b_wkv], [BK[bk]])
                        if g == 0:
                            DMA("sp", mkr_t[tr3][:], mkr[ts_, :], [], [b_mkr[tr3]])

                    def s1():
                        CP("act", qn[b][:], PB[bk][:], [BK[bk]], [b_qn[b]])

                    def s2():
                        if g == 0:
                            ACT(junk[jk][:, 0:64], mkr_t[tr3][:], AF.Square, [b_mkr[tr3]], [b_junk[jk], b_ssr[tr3]], accum_out=ssr[tr3][:, 0:1])
                        for h in range(2):
                            ACT(junk[jk][:, 0:128], qn[b][:, h * 256:h * 256 + 128], AF.Square, [b_qn[b]], [b_junk[jk], b_ssq[b]], accum_out=ssq[b][:, h:h + 1])

                    def s3():
                        TS("dve", ssq[b][:, 0:2], ssq[b][:, 0:2], ssr[tr3][:, 0:1], None, ALU.add, None, [b_ssq[b], b_ssr[tr3]], [b_ssq[b]])

                    def s4():
                        ACT(ssq[b][:, 0:2], ssq[b][:, 0:2], AF.Sqrt, [b_ssq[b], b_eps], [b_ssq[b]], scale=1.0 / 192, bias=eps_t[:, 0:1])

                    def s5():
                        RECIP(ssq[b][:, 0:2], ssq[b][:, 0:2], [b_ssq[b]], [b_ssq[b]])

                    def s6():
                        for h in range(2):
                            STT("dve", kb_[b][:, h * 128:(h + 1) * 128], qn[b][:, h * 256:h * 256 + 128], ssq[b][:, h:h + 1], gkm[:, 0:128], ALU.mult, ALU.mult, [b_qn[b], b_ssq[b], b_gkm], [b_kb[b]])
                            STT("dve", kr[b][:, h * 64:(h + 1) * 64], mkr_t[tr3][:], ssq[b][:, h:h + 1], gkm[:, 128:192], ALU.mult, ALU.mult, [b_mkr[tr3], b_ssq[b], b_gkm], [b_kr[b]])

                    def s7():
                        s3_ = kr[b][:].rearrange("p (h d) -> p h d", h=2)
                        d3_ = kb_[b][:, 256:384].rearrange("p (h d) -> p h d", h=2)
                        rope(rt, s3_, d3_, 2, 32, tab[:, tt, 80:112], tab[:, tt, 24:56], [b_kr[b]], [b_kb[b]], eng="dve")
                        CP("pool", vb[b][:].rearrange("p (h d) -> p h d", h=2), qn[b][:].rearrange("p (h d) -> p h d", h=2)[:, :, 128:256], [b_qn[b]], [b_vb[b]])

                    def s8():
                        for h in range(2):
                            TR(PBb[tb][:, h * 128:(h + 1) * 128], kb_[b][:, h * 128:(h + 1) * 128], identb[:], [b_kb[b], b_idb], [BK[tb]])
                            TR(PBb[tb][0:64, 256 + h * 128:256 + (h + 1) * 128], kb_[b][:, 256 + h * 64:256 + (h + 1) * 64], identb[:], [b_kb[b], b_idb], [BK[tb]])
                        DMA("sp", mv[ts_, g * 256:(g + 1) * 256], vb[b][:], [b_vb[b]], [])

                    def s9():
                        CP("act", stg[sg][:, :, :], PBb[tb][:, 0:256].rearrange("p (j t) -> p j t", j=2), [BK[tb]], [b_stg[sg]])
                        CP("act", stgr[sg][:, :, :], PBb[tb][0:64, 256:512].rearrange("p (j t) -> p j t", j=2), [BK[tb]], [b_stgr[sg]])
                        DMA("sp", mkTn[g * 256:(g + 1) * 256, ts_].rearrange("(j p) t -> p j t", p=128), stg[sg][:, :, :], [b_stg[sg]], [])
                        DMA("sp", mkTr[g * 128:(g + 1) * 128, ts_].rearrange("(j p) t -> p j t", p=64), stgr[sg][:, :, :], [b_stgr[sg]], [])
                    return [s0, s1, s2, s3, s4, s5, s6, s7, s8, s9]

                units = []
                u = 0
                for tt in range(NT):
                    for g in range(4):
                        units.append(mq_unit(u, tt, g)); u += 1
                    for g in range(4):
                        units.append(mkv_unit(u, tt, g)); u += 1
                run_skewed(units)
            P.barrier()

            with ExitStack() as st:
                kaS = sbuf(st, "kaS", [128, 2, S], BF16); b_kaS = Buf()
                vaS = sbuf(st, "vaS", [128, NT, 256], BF16); b_vaS = Buf()
                DMA("sp", kaS[:], kaT.rearrange("(g p) s -> p g s", p=128), [], [b_kaS])
                DMA("sp", vaS[:], va.rearrange("(t p) c -> p t c", p=128), [], [b_vaS])
                mkn = sbuf(st, "mkn", [128, 8, S], BF16); b_mkn = Buf()
                mkrS = sbuf(st, "mkrS", [64, 8, S], BF16); b_mkrS = Buf()
                mvS = sbuf(st, "mvS", [128, NT, 1024], BF16); b_mvS = Buf()
                DMA("sp", mkn[:], mkTn.rearrange("(h p) s -> p h s", p=128), [], [b_mkn])
                DMA("sp", mkrS[:], mkTr.rearrange("(h p) s -> p h s", p=64), [], [b_mkrS])
                DMA("sp", mvS[:], mv.rearrange("(t p) c -> p t c", p=128), [], [b_mvS])
                qT_ = [sbuf(st, "a_q%d" % i, [128, 512], BF16) for i in range(3)]; b_q = [Buf(), Buf(), Buf()]
                qR_ = [sbuf(st, "a_qr%d" % i, [64, 512], BF16) for i in range(3)]; b_qr = [Buf(), Buf(), Buf()]
                mts = [sbuf(st, "a_mts%d" % i, [128, NT, 512], BF16) for i in range(2)]; b_mts = [Buf(), Buf()]
                pT = [sbuf(st, "a_pT%d" % i, [128, 512], BF16) for i in range(4)]; b_pT = [Buf() for _ in range(4)]
                rd_ = sbuf(st, "a_rd", [128, 512], F32); b_rd = Buf()
                ob = [sbuf(st, "a_ob%d" % i, [128, 512], BF16) for i in range(2)]; b_ob = [Buf(), Buf()]
                hcount = [0]
                icount = [0]
                heads_all = [(qc_, h_) for qc_ in range(NCH) for h_ in range(16)]

                def load_q(g):
                    if g >= len(heads_all):
                        return
                    qc_, h_ = heads_all[g]
                    cs_ = slice(qc_ * 512, (qc_ + 1) * 512)
                    isA_ = h_ < 8
                    hh_ = h_ if isA_ else h_ - 8
                    i_ = g % 3
                    if isA_:
                        DMA("sp", qT_[i_][:], qaT[hh_ * 128:(hh_ + 1) * 128, cs_], [], [b_q[i_]])
                    else:
                        DMA("sp", qT_[i_][:], mqTn[hh_ * 128:(hh_ + 1) * 128, cs_], [], [b_q[i_]])
                        DMA("sp", qR_[i_][:], mqTr[hh_ * 64:(hh_ + 1) * 64, cs_], [], [b_qr[i_]])
                load_q(0)
                for qc in range(NCH):
                    cs = slice(qc * 512, (qc + 1) * 512)
                    nkb = 4 * qc + 4
                    mi = qc % 2
                    if qc == 0:
                        DMA("sp", mts[0][:, 0:4, :], MT[0:512, 0:512].rearrange("(k p) t -> p k t", p=128), [], [b_mts[0]])
                    if qc + 1 < NCH:
                        nkb2 = 4 * (qc + 1) + 4
                        DMA("sp", mts[(qc + 1) % 2][:, 0:nkb2, :], MT[0:nkb2 * 128, (qc + 1) * 512:(qc + 2) * 512].rearrange("(k p) t -> p k t", p=128), [], [b_mts[(qc + 1) % 2]])
                    items = []
                    for h in range(16):
                        hq = hcount[0]
                        hcount[0] += 1
                        for kb in range(nkb):
                            items.append((h, kb, hq, icount[0]))
                            icount[0] += 1

                    def a_scores(it, qc=qc, cs=cs):
                        h, kb, hq, n = it
                        isA = h < 8
                        hh = h if isA else h - 8
                        i = hq % 3
                        if kb == 0:
                            load_q(hq + 1)
                        q0 = max(0, kb - 4 * qc) * 128
                        ks = slice(kb * 128, (kb + 1) * 128)
                        bk = n % 4
                        if isA:
                            MM(PB[bk][:, q0:512], kaS[:, hh // 4, ks], qT_[i][:, q0:512], True, True, [b_kaS, b_q[i]], [BK[bk]])
                        else:
                            MM(PB[bk][:, q0:512], mkn[:, hh, ks], qT_[i][:, q0:512], True, False, [b_mkn, b_q[i]], [BK[bk]])
                            MM(PB[bk][:, q0:512], mkrS[:, hh, ks], qR_[i][:, q0:512], False, True, [b_mkrS, b_qr[i]], [BK[bk]])

                    def a_rest(it, qc=qc, cs=cs, nkb=nkb, mi=mi):
                        h, kb, hq, n = it
                        isA = h < 8
                        hh = h if isA else h - 8
                        scale = float(128 ** -0.5) if isA else float(192 ** -0.5)
                        q0 = max(0, kb - 4 * qc) * 128
                        bk = n % 4
                        j = n % 4
                        bo = 4 + (hq % 2) * 2
                        ACT(pT[j][:, q0:512], PB[bk][:, q0:512], AF.Exp, [BK[bk]], [b_pT[j]], scale=scale)
                        if isA:
                            TT("dve" if n % 2 else "pool", pT[j][:, q0:512], pT[j][:, q0:512], mts[mi][:, kb, q0:512], ALU.mult, [b_pT[j], b_mts[mi]], [b_pT[j]])
                            vv = vaS[:, kb, (hh // 4) * 128:(hh // 4 + 1) * 128]
                            b_v = b_vaS
                        else:
                            if kb >= 4 * qc:
                                TT("dve" if n % 2 else "pool", pT[j][:, q0:q0 + 128], pT[j][:, q0:q0 + 128], tri[:], ALU.mult, [b_pT[j], b_tri], [b_pT[j]])
                            vv = mvS[:, kb, hh * 128:(hh + 1) * 128]
                            b_v = b_mvS
                        MM(PB[bo][:, q0:512], vv, pT[j][:, q0:512], kb == 0, kb == nkb - 1, [b_v, b_pT[j]], [BK[bo]])
                        MM(PB[bo + 1][:, q0:512], ones_b[:], pT[j][:, q0:512], kb == 0, kb == nkb - 1, [b_1b, b_pT[j]], [BK[bo + 1]])
                        if kb == nkb - 1:
                            i2 = hq % 2
                            RECIP(rd_[:], PB[bo + 1][:], [BK[bo + 1]], [b_rd])
                            TT("dve", ob[i2][:], PB[bo][:], rd_[:], ALU.mult, [BK[bo], b_rd], [b_ob[i2]])
                            dst = oaT if isA else obT
                            DMA("sp", dst[hh * 128:(hh + 1) * 128, cs], ob[i2][:], [b_ob[i2]], [])

                    DEPTH_PF = 2
                    for k in range(min(DEPTH_PF, len(items))):
                        a_scores(items[k])
                    for k in range(len(items)):
                        if k + DEPTH_PF < len(items):
                            a_scores(items[k + DEPTH_PF])
                        a_rest(items[k])
            P.barrier()

            with ExitStack() as st:
                mT_ = sbuf(st, "mT_", [128, KC, S], BF16); b_mT = Buf()
                with ExitStack() as st1:
                    oaS = sbuf(st1, "oaS", [128, 8, S], BF16); b_oaS = Buf()
                    obS = sbuf(st1, "obS", [128, 8, S], BF16); b_obS = Buf()
                    DMA("sp", oaS[:], oaT.rearrange("(k p) s -> p k s", p=128), [], [b_oaS])
                    DMA("sp", obS[:], obT.rearrange("(k p) s -> p k s", p=128), [], [b_obS])
                    wpa = [sbuf(st1, "wpa%d" % i, [128, 8, 128], BF16) for i in range(2)]; b_wpa = [Buf(), Buf()]
                    wpb = [sbuf(st1, "wpb%d" % i, [128, 8, 128], BF16) for i in range(2)]; b_wpb = [Buf(), Buf()]
                    ga = [sbuf(st1, "ga%d" % i, [128, 512], F32) for i in range(4)]; b_ga = [Buf() for _ in range(4)]
                    gb = [sbuf(st1, "gb%d" % i, [128, 512], F32) for i in range(4)]; b_gb = [Buf() for _ in range(4)]
                    t1 = [sbuf(st1, "p5t%d" % i, [128, 512], F32) for i in range(4)]; b_t1 = [Buf() for _ in range(4)]
                    n = 0
                    for m in range(KC):
                        i = m % 2
                        ms = slice(m * 128, (m + 1) * 128)
                        DMA("pool", wpa[i][:], Wd["w_pa"][l][:, ms].rearrange("(k p) n -> p k n", p=128), [], [b_wpa[i]])
                        DMA("pool", wpb[i][:], Wd["w_pb"][l][:, ms].rearrange("(k p) n -> p k n", p=128), [], [b_wpb[i]])
                        for ch in range(NCH):
                            cs = slice(ch * 512, (ch + 1) * 512)
                            ba = (n % 4) * 2
                            for k in range(8):
                                MM(PB[ba][:], wpa[i][:, k, :], oaS[:, k, cs], k == 0, k == 7, [b_wpa[i], b_oaS], [BK[ba]])
                            for k in range(8):
                                MM(PB[ba + 1][:], wpb[i][:, k, :], obS[:, k, cs], k == 0, k == 7, [b_wpb[i], b_obS], [BK[ba + 1]])
                            j = n % 4
                            DMA("sp", ga[j][:], sgaT[ms, cs], [], [b_ga[j]])
                            DMA("sp", gb[j][:], sgbT[ms, cs], [], [b_gb[j]])
                            TT("dve", t1[j][:], PB[ba][:], ga[j][:], ALU.mult, [BK[ba], b_ga[j]], [b_t1[j]])
                            TT("dve", ga[j][:], PB[ba + 1][:], gb[j][:], ALU.mult, [BK[ba + 1], b_gb[j], b_ga[j]], [b_ga[j]])
                            TT("pool", mT_[:, m, cs], t1[j][:], ga[j][:], ALU.add, [b_t1[j], b_ga[j]], [b_mT])
                            n += 1
                P.barrier()
                with ExitStack() as st1:
                    wo = [sbuf(st1, "wo%d" % i, [128, KC, 128], BF16) for i in range(2)]; b_wo = [Buf(), Buf()]
                    xm = [sbuf(st1, "xm%d" % i, [128, S], F32) for i in range(3)]; b_xm = [Buf(), Buf(), Buf()]
                    n = 0
                    DMA("sp", xm[0][:], xT[0:128, :], [], [b_xm[0]])
                    for m in range(KC):
                        i = m % 2
                        ms = slice(m * 128, (m + 1) * 128)
                        DMA("pool", wo[i][:], Wd["w_o"][l][:, ms].rearrange("(k p) n -> p k n", p=128), [], [b_wo[i]])
                        if m + 1 < KC:
                            DMA("sp", xm[(m + 1) % 3][:], xT[(m + 1) * 128:(m + 2) * 128, :], [], [b_xm[(m + 1) % 3]])
                        i3 = m % 3
                        for ch in range(NCH):
                            cs = slice(ch * 512, (ch + 1) * 512)
                            bk = n % 4
                            for k in range(KC):
                                MM(PB[bk][:], wo[i][:, k, :], mT_[:, k, cs], k == 0, k == KC - 1, [b_wo[i], b_mT], [BK[bk]])
                            STT("dve", xm[i3][:, cs], PB[bk][:], modT[:, l, 32 + m:33 + m], xm[i3][:, cs], ALU.mult, ALU.add, [BK[bk], b_mod, b_xm[i3]], [b_xm[i3]])
                            n += 1
                        DMA("sp", xT[ms, :], xm[i3][:], [b_xm[i3]], [])
            P.barrier()

            with ExitStack() as st:
                hT = sbuf(st, "h2T", [128, KC, S], BF16); b_hT = Buf()
                with ExitStack() as st1:
                    norm_phase(st1, hT, b_hT, AB[:, l, 16:32], modT[:, l, 48:64])
                P.barrier()
                wcT = sbuf(st, "wcT", [128, 3, 88], F32); b_wc = Buf()
                bcT = sbuf(st, "bcT", [128, 88], F32); b_bc = Buf()
                for t in range(3):
                    rows_to_cols(st, Wd["w_conv"][l][t].rearrange("(k p) -> k p", p=128), 88, wcT[:, t, :], b_wc, 7, "wc%d" % t)
                rows_to_cols(st, Wd["b_conv"][l].rearrange("(k p) -> k p", p=128), 88, bcT[:], b_bc, 7, "bc")
                wu = [sbuf(st, "wu%d" % i, [128, KC, 256], BF16) for i in range(2)]; b_wu = [Buf(), Buf()]
                ug = [sbuf(st, "ug%d" % i, [128, S + 2], F32) for i in range(2)]; b_ug = [Buf(), Buf()]
                uv = [sbuf(st, "uv%d" % i, [128, S + 2], F32) for i in range(2)]; b_uv = [Buf(), Buf()]
                cg = [sbuf(st, "cg%d" % i, [128, S], F32) for i in range(2)]; b_cg = [Buf(), Buf()]
                cv = [sbuf(st, "cv%d" % i, [128, S], F32) for i in range(2)]; b_cv = [Buf(), Buf()]
                ab = [sbuf(st, "ab%d" % i, [128, S], BF16) for i in range(2)]; b_ab = [Buf(), Buf()]
                for i in range(2):
                    MS("pool", ug[i][:, 0:2], 0.0, [b_ug[i]])
                    MS("pool", uv[i][:, 0:2], 0.0, [b_uv[i]])
                n = 0
                for j in range(44):
                    i = j % 2
                    DMA("pool", wu[i][:, :, 0:128], Wd["w_up"][l][:, j * 128:(j + 1) * 128].rearrange("(k p) n -> p k n", p=128), [], [b_wu[i]])
                    DMA("pool", wu[i][:, :, 128:256], Wd["w_up"][l][:, DFF + j * 128:DFF + (j + 1) * 128].rearrange("(k p) n -> p k n", p=128), [], [b_wu[i]])
                    for ch in range(NCH):
                        cs2 = slice(2 + ch * 512, 2 + (ch + 1) * 512)
                        cs = slice(ch * 512, (ch + 1) * 512)
                        bg = (n % 4) * 2
                        for k in range(KC):
                            MM(PB[bg][:], wu[i][:, k, 0:128], hT[:, k, cs], k == 0, k == KC - 1, [b_wu[i], b_hT], [BK[bg]])
                        for k in range(KC):
                            MM(PB[bg + 1][:], wu[i][:, k, 128:256], hT[:, k, cs], k == 0, k == KC - 1, [b_wu[i], b_hT], [BK[bg + 1]])
                        CP("act", ug[i][:, cs2], PB[bg][:], [BK[bg]], [b_ug[i]])
                        CP("dve", uv[i][:, cs2], PB[bg + 1][:], [BK[bg + 1]], [b_uv[i]])
                        n += 1
                    for (u_, c_, b_u, b_c, col, e1, e2) in ((ug[i], cg[i], b_ug[i], b_cg[i], j, "pool", "dve"), (uv[i], cv[i], b_uv[i], b_cv[i], 44 + j, "dve", "pool")):
                        ACT(c_[:], u_[:, 2:S + 2], AF.Identity, [b_u, b_wc, b_bc], [b_c], scale=wcT[:, 2, col:col + 1], bias=bcT[:, col:col + 1])
                        STT(e1, c_[:], u_[:, 1:S + 1], wcT[:, 1, col:col + 1], c_[:], ALU.mult, ALU.add, [b_u, b_wc, b_c], [b_c])
                        STT(e2, c_[:], u_[:, 0:S], wcT[:, 0, col:col + 1], c_[:], ALU.mult, ALU.add, [b_u, b_wc, b_c], [b_c])
                    ACT(cg[i][:], cg[i][:], AF.Silu, [b_cg[i]], [b_cg[i]])
                    TT("dve", ab[i][:], cg[i][:], cv[i][:], ALU.mult, [b_cg[i], b_cv[i]], [b_ab[i]])
                    DMA("sp", actT[j * 128:(j + 1) * 128, :], ab[i][:], [b_ab[i]], [])
            P.barrier()

            with ExitStack() as st:
                TC = min(1024, S)
                aS = sbuf(st, "aS", [128, 44, TC], BF16); b_aS = Buf()
                wd = [sbuf(st, "wd%d" % i, [128, 44, 128], BF16) for i in range(2)]; b_wd = [Buf(), Buf()]
                xm = [sbuf(st, "xd%d" % i, [128, TC], F32) for i in range(3)]; b_xm = [Buf(), Buf(), Buf()]
                its = [(c0, m) for c0 in range(0, S, TC) for m in range(KC)]

                def p8_load(n_):
                    if n_ < len(its):
                        c0_, m_ = its[n_]
                        DMA("sp", xm[n_ % 3][:], xT[m_ * 128:(m_ + 1) * 128, c0_:c0_ + TC], [], [b_xm[n_ % 3]])
                p8_load(0)
                for n, (c0, m) in enumerate(its):
                    if m == 0:
                        DMA("sp", aS[:], actT[:, c0:c0 + TC].rearrange("(k p) s -> p k s", p=128), [], [b_aS])
                    i = n % 2
                    i3 = n % 3
                    ms = slice(m * 128, (m + 1) * 128)
                    DMA("pool", wd[i][:], Wd["w_down"][l][:, ms].rearrange("(k p) n -> p k n", p=128), [], [b_wd[i]])
                    p8_load(n + 1)
                    for s0 in range(0, TC, 512):
                        bk = (n * 2 + s0 // 512) % 8
                        for k in range(44):
                            MM(PB[bk][:], wd[i][:, k, :], aS[:, k, s0:s0 + 512], k == 0, k == 43, [b_wd[i], b_aS], [BK[bk]])
                        STT("dve", xm[i3][:, s0:s0 + 512], PB[bk][:], modT[:, l, 80 + m:81 + m], xm[i3][:, s0:s0 + 512], ALU.mult, ALU.add, [BK[bk], b_mod, b_xm[i3]], [b_xm[i3]])
                    DMA("sp", xT[ms, c0:c0 + TC], xm[i3][:], [b_xm[i3]], [])
            P.barrier()

        with ExitStack() as st:
            xc = [sbuf(st, "f_xc%d" % i, [128, KC, 128], F32) for i in range(2)]; b_xc = [Buf(), Buf()]
            ot = [sbuf(st, "f_ot%d" % i, [128, D], F32) for i in range(2)]; b_ot = [Buf(), Buf()]
            for tt in range(NT):
                i = tt % 2
                DMA("sp", xc[i][:], xTv[:, :, tt * 128:(tt + 1) * 128], [], [b_xc[i]])
                for g4 in range(4):
                    bk = (tt * 4 + g4) % 8
                    for j in range(4):
                        kc = g4 * 4 + j
                        TR(PB[bk][:, j * 128:(j + 1) * 128], xc[i][:, kc, :], identf[:], [b_xc[i], b_idf], [BK[bk]])
                    CP("act" if g4 % 2 else "dve", ot[i][:, g4 * 512:(g4 + 1) * 512], PB[bk][:], [BK[bk]], [b_ot[i]])
                DMA("sp", out[tt * 128:(tt + 1) * 128, :], ot[i][:], [b_ot[i]], [])
        for k, ap in dbg_out.items():
            src = {"xT": xT, "qaT": qaT, "kaT": kaT, "va": va, "qiT": qiT, "kiT2": kiT2, "wis": wis, "mkr": mkr, "mqlT": mqlT,
                   "mkvlT": mkvlT, "sgaT": sgaT, "sgbT": sgbT, "mqTn": mqTn, "mqTr": mqTr, "mkTn": mkTn, "mkTr": mkTr, "mv": mv,
                   "MT": MT, "oaT": oaT, "obT": obT, "actT": actT}[k]
            P.dma("sp", lambda e, ap=ap, src=src: e.dma_start(out=ap, in_=src), (), ())
        P.barrier()
        with nc.Block() as block:
            P.emit(block)
    return nc


_NC_CACHE = {}


def kernel(**inputs):
    S = inputs["x"].shape[1]
    B = inputs["x"].shape[0]
    L = inputs["w_in"].shape[0]
    n_sel = min(256, S // 4)
    key = (S, L, n_sel)
    if key not in _NC_CACHE:
        _NC_CACHE[key] = build(S, L, n_sel)
    nc = _NC_CACHE[key]
    invf = inv_freq_table()
    shared = {k: np.ascontiguousarray(np.asarray(inputs[k], dtype=np.float32)) for k in W_SHAPES}
    in_maps = []
    for b in range(B):
        m = dict(shared)
        m["x"] = np.ascontiguousarray(np.asarray(inputs["x"][b], dtype=np.float32))
        m["c"] = np.ascontiguousarray(np.asarray(inputs["c"][b], dtype=np.float32))
        m["positions"] = np.ascontiguousarray(np.asarray(inputs["positions"][b], dtype=np.int32))
        m["invf"] = invf
        in_maps.append(m)
    res = run_bass_kernel_spmd(nc, in_maps, core_ids=list(range(B)))
    return np.stack([np.asarray(r["out"], dtype=np.float32) for r in res.results], axis=0)
```

```python
from contextlib import ExitStack
import numpy as np
import concourse.bass as bass
import concourse.mybir as mybir
from concourse.bass_utils import run_bass_kernel_spmd

F32 = mybir.dt.float32
BF16 = mybir.dt.bfloat16
I32 = mybir.dt.int32
AF = mybir.ActivationFunctionType
ALU = mybir.AluOpType

ENGS = ("pe", "act", "dve", "pool", "sp")
NRING = 12

D = 2048
KC = 16
IN_DIM = 7568
DFF = 5632
EPS = 1e-6
PI = float(np.pi)
TWO_PI = float(2 * np.pi)


class Buf:
    __slots__ = ("w", "r", "excl")

    def __init__(self, excl=False):
        self.w = None
        self.r = {}
        self.excl = excl


class Prog:
    def __init__(self, nc, stack):
        self.nc = nc
        self.q = {e: [] for e in ENGS}
        self.cnt = {e: 0 for e in ENGS}
        self.dcnt = {e: 0 for e in ENGS}
        self.seen = {e: {} for e in ENGS}
        self.sem = {}
        for e in ENGS:
            self.sem[("c", e)] = stack.enter_context(nc.semaphore("c_" + e))
        for e in ("sp", "pool", "act"):
            for r in range(NRING):
                self.sem[("d", e, r)] = stack.enter_context(nc.semaphore("d_%s_%d" % (e, r)))

    def _need(self, eng, waits, tok):
        if tok is None:
            return
        key, val = tok
        if eng == "pe" and key == ("c", "pe"):
            return
        if self.seen[eng].get(key, 0) >= val:
            return
        self.seen[eng][key] = val
        waits.append((key, val))

    def _deps(self, eng, reads, writes):
        waits = []
        own = ("c", eng)
        for b in reads:
            self._need(eng, waits, b.w)
            if b.excl:
                for k, v in b.r.items():
                    if k != own:
                        self._need(eng, waits, (k, v))
        for b in writes:
            self._need(eng, waits, b.w)
            for k, v in b.r.items():
                self._need(eng, waits, (k, v))
        return waits

    def _mark(self, tok, reads, writes):
        k, v = tok
        for b in reads:
            if b.r.get(k, 0) < v:
                b.r[k] = v
        for b in writes:
            b.w = tok
            b.r = {}

    def op(self, eng, fn, reads=(), writes=()):
        waits = self._deps(eng, reads, writes)
        self.cnt[eng] += 1
        tok = (("c", eng), self.cnt[eng])
        self.q[eng].append((fn, waits, tok[0], 1))
        self._mark(tok, reads, writes)

    def dma(self, eng, fn, reads=(), writes=()):
        waits = self._deps(eng, reads, writes)
        j = self.dcnt[eng]
        self.dcnt[eng] += 1
        slot = j % NRING
        use = j // NRING
        key = ("d", eng, slot)
        if use > 0:
            self._need(eng, waits, (key, 16 * use))
        tok = (key, 16 * (use + 1))
        self.q[eng].append((fn, waits, key, 16))
        self._mark(tok, reads, writes)

    def wait_all(self, eng):
        waits = []
        for e in ENGS:
            if self.cnt[e] > 0:
                self._need(eng, waits, (("c", e), self.cnt[e]))
        for e in ("sp", "pool", "act"):
            j = self.dcnt[e]
            for slot in range(NRING):
                if j > slot:
                    uses = (j - slot + NRING - 1) // NRING
                    self._need(eng, waits, (("d", e, slot), 16 * uses))
        self.q[eng].append((None, waits, None, 0))

    def barrier(self):
        for e in ENGS:
            self.wait_all(e)

    def _replay(self, name, e):
        sem = self.sem
        for fn, waits, key, inc in self.q[name]:
            for k, v in waits:
                e.wait_ge(sem[k], v)
            if fn is not None:
                fn(e).then_inc(sem[key], inc)

    def emit(self, block):
        P = self

        @block.tensor
        def _(e):
            P._replay("pe", e)

        @block.scalar
        def _(e):
            P._replay("act", e)

        @block.vector
        def _(e):
            P._replay("dve", e)

        @block.gpsimd
        def _(e):
            P._replay("pool", e)

        @block.sync
        def _(e):
            P._replay("sp", e)


W_SHAPES = {
    "g_attn": (D,), "g_ffn": (D,), "w_ada": (D, 6 * D), "b_ada": (6 * D,), "w_in": (D, IN_DIM),
    "g_qa": (128,), "g_ka": (128,), "g_mq_lat": (512,), "w_mq_up": (512, 1536), "g_mkv_lat": (256,),
    "w_mkv_up": (256, 2048), "g_qm": (192,), "g_km": (192,), "w_pa": (1024, D), "w_pb": (1024, D),
    "w_o": (D, D), "w_up": (D, 2 * DFF), "w_conv": (3, 2 * DFF), "b_conv": (2 * DFF,), "w_down": (DFF, D),
}


def inv_freq_table():
    def inv(rot):
        return (500000.0 ** (-np.arange(0, rot, 2, dtype=np.float32) / np.float32(rot))).astype(np.float32)
    t = np.concatenate([inv(32), inv(16), inv(64)]).astype(np.float32)
    return np.ascontiguousarray(np.broadcast_to(t[None, :], (128, 56))).astype(np.float32)


def build(S, L, n_sel, dbg=()):
    NT = S // 128
    NCH = S // 512
    nc = bass.Bass("TRN2", target_bir_lowering=False)

    def din(name, shape, dt=F32):
        return nc.dram_tensor(name, list(shape), dt, kind="ExternalInput").ap()

    def dscr(name, shape, dt):
        return nc.dram_tensor(name, list(shape), dt, kind="Internal").ap()

    x_in = din("x", (S, D))
    c_in = din("c", (D,))
    pos_in = din("positions", (S,), I32)
    invf_in = din("invf", (128, 56))
    Wd = {k: din(k, (L,) + v) for k, v in W_SHAPES.items()}
    out = nc.dram_tensor("out", [S, D], F32, kind="ExternalOutput").ap()
    dbg_out = {k: nc.dram_tensor("dbg_" + k, list(shp), dt, kind="ExternalOutput").ap() for k, (shp, dt) in dbg}

    xT = dscr("xT", (D, S), F32)
    qaT = dscr("qaT", (1024, S), BF16)
    kaT = dscr("kaT", (256, S), BF16)
    va = dscr("va", (S, 256), BF16)
    qiT = dscr("qiT", (1024, S), BF16)
    kiT2 = dscr("kiT2", (128, S), BF16)
    wis = dscr("wis", (S, 16), F32)
    mkr = dscr("mkr", (S, 64), F32)
    mqlT = dscr("mqlT", (512, S), BF16)
    mkvlT = dscr("mkvlT", (256, S), BF16)
    sgaT = dscr("sgaT", (D, S), F32)
    sgbT = dscr("sgbT", (D, S), F32)
    mqTn = dscr("mqTn", (1024, S), BF16)
    mqTr = dscr("mqTr", (512, S), BF16)
    mkTn = dscr("mkTn", (1024, S), BF16)
    mkTr = dscr("mkTr", (512, S), BF16)
    mv = dscr("mv", (S, 1024), BF16)
    MT = dscr("MT", (S, S), BF16)
    oaT = dscr("oaT", (1024, S), BF16)
    obT = dscr("obT", (1024, S), BF16)
    actT = dscr("actT", (DFF, S), BF16)

    with ExitStack() as st0:
        P = Prog(nc, st0)

        def MM(o, lhsT, rhs, start, stop, rd, wr):
            P.op("pe", lambda e: e.matmul(o, lhsT=lhsT, rhs=rhs, start=start, stop=stop), rd, wr)

        def TR(o, in_, ident, rd, wr):
            P.op("pe", lambda e: e.transpose(out=o, in_=in_, identity=ident), rd, wr)

        def ACT(o, in_, func, rd, wr, **kw):
            P.op("act", lambda e: e.activation(out=o, in_=in_, func=func, **kw), rd, wr)

        def TS(eng, o, in0, s1, s2, op0, op1, rd, wr):
            if s2 is None:
                P.op(eng, lambda e: e.tensor_scalar(out=o, in0=in0, scalar1=s1, scalar2=None, op0=op0), rd, wr)
            else:
                P.op(eng, lambda e: e.tensor_scalar(out=o, in0=in0, scalar1=s1, scalar2=s2, op0=op0, op1=op1), rd, wr)

        def TT(eng, o, in0, in1, op, rd, wr):
            P.op(eng, lambda e: e.tensor_tensor(out=o, in0=in0, in1=in1, op=op), rd, wr)

        def STT(eng, o, in0, scalar, in1, op0, op1, rd, wr):
            eng = "dve"
            P.op(eng, lambda e: e.scalar_tensor_tensor(out=o, in0=in0, scalar=scalar, in1=in1, op0=op0, op1=op1), rd, wr)

        def CP(eng, o, in_, rd, wr):
            if eng == "act":
                P.op("act", lambda e: e.copy(out=o, in_=in_), rd, wr)
            else:
                P.op(eng, lambda e: e.tensor_copy(out=o, in_=in_), rd, wr)

        def MS(eng, o, val, wr):
            P.op(eng, lambda e: e.memset(o, val), (), wr)

        def DMA(eng, o, in_, rd, wr):
            P.dma(eng, lambda e: e.dma_start(out=o, in_=in_), rd, wr)

        def MAX8(o, in_, rd, wr):
            P.op("dve", lambda e: e.max(out=o, in_=in_), rd, wr)

        def MREP(o, rep, vals, rd, wr):
            P.op("dve", lambda e: e.match_replace(out=o, in_to_replace=rep, in_values=vals, imm_value=-1e30), rd, wr)

        def RECIP(o, in_, rd, wr):
            P.op("dve", lambda e: e.reciprocal(out=o, in_=in_), rd, wr)

        def RSTD(t, inv_n, rd_wr):
            TS("dve", t, t, inv_n, EPS, ALU.mult, ALU.add, rd_wr, rd_wr)
            ACT(t, t, AF.Sqrt, rd_wr, rd_wr)
            P.op("dve", lambda e: e.reciprocal(out=t, in_=t), rd_wr, rd_wr)

        def skewed_steps(units):
            ns = max(len(x) for x in units)
            steps = []
            for t in range(len(units) + ns - 1):
                def step(t=t):
                    for k in reversed(range(ns)):
                        u = t - k
                        if 0 <= u < len(units) and k < len(units[u]):
                            units[u][k]()
                steps.append(step)
            return steps

        def interleave(A, B):
            tot = sum(w for _, w in A) or 1.0
            j = 0
            acc = 0.0
            for fn, w in A:
                fn()
                acc += w
                target = int(round(acc / tot * len(B)))
                while j < target:
                    B[j]()
                    j += 1
            while j < len(B):
                B[j]()
                j += 1

        def run_skewed(units):
            ns = max(len(x) for x in units)
            for t in range(len(units) + ns - 1):
                for k in reversed(range(ns)):
                    u = t - k
                    if 0 <= u < len(units) and k < len(units[u]):
                        units[u][k]()

        PB = [st0.enter_context(nc.psum_tensor("pb%d" % i, [128, 512], F32)) for i in range(8)]
        PBb = [b[:].bitcast(BF16) for b in PB]
        BK = [Buf(excl=True) for _ in range(8)]

        uid = [0]

        def sbuf(st, name, shape, dt):
            uid[0] += 1
            return st.enter_context(nc.sbuf_tensor("%s_u%d" % (name, uid[0]), list(shape), dt))

        identf = sbuf(st0, "identf", [128, 128], F32); b_idf = Buf()
        identb = sbuf(st0, "identb", [128, 128], BF16); b_idb = Buf()
        ones_f = sbuf(st0, "ones_f", [128, 128], F32); b_1f = Buf()
        ones_b = sbuf(st0, "ones_b", [128, 128], BF16); b_1b = Buf()
        tri = sbuf(st0, "tri", [128, 128], BF16); b_tri = Buf()
        trif = sbuf(st0, "trif", [128, 128], F32); b_trif = Buf()
        negm = sbuf(st0, "negm", [128, 128], F32); b_negm = Buf()
        tab = sbuf(st0, "tab", [128, NT, 112], F32); b_tab = Buf()
        modT = sbuf(st0, "modT", [128, L, 96], F32); b_mod = Buf()
        gT = sbuf(st0, "gT", [128, L, 32], F32); b_gT = Buf()
        AB = sbuf(st0, "AB", [128, L, 32], F32); b_AB = Buf()
        eps_t = sbuf(st0, "eps_t", [128, 1], F32); b_eps = Buf()
        cTb = sbuf(st0, "cTb", [128, 16], BF16); b_cTb = Buf()
        CONST = [b_idf, b_idb, b_1f, b_1b, b_tri, b_negm, b_tab, b_mod, b_AB]

        MS("pool", eps_t[:], EPS, [b_eps])
        MS("pool", identf[:], 1.0, [b_idf])
        P.op("pool", lambda e: e.affine_select(out=identf[:], in_=identf[:], pattern=[[-1, 128]], compare_op=ALU.is_equal, fill=0.0, base=0, channel_multiplier=1), [b_idf], [b_idf])
        CP("dve", identb[:], identf[:], [b_idf], [b_idb])
        MS("pool", ones_f[:], 1.0, [b_1f])
        MS("pool", ones_b[:], 1.0, [b_1b])
        MS("pool", trif[:], 1.0, [b_trif])
        P.op("pool", lambda e: e.affine_select(out=trif[:], in_=trif[:], pattern=[[1, 128]], compare_op=ALU.is_ge, fill=0.0, base=0, channel_multiplier=-1), [b_trif], [b_trif])
        CP("dve", tri[:], trif[:], [b_trif], [b_tri])
        MS("pool", negm[:], 0.0, [b_negm])
        P.op("pool", lambda e: e.affine_select(out=negm[:], in_=negm[:], pattern=[[-1, 128]], compare_op=ALU.is_ge, fill=-1e30, base=0, channel_multiplier=1), [b_negm], [b_negm])

        def rows_to_cols(st, src_rows_ap, nrows, dst, dst_buf, bank, nm):
            rt = sbuf(st, "rt_" + nm, [128, 128], F32); b_rt = Buf()
            DMA("sp", rt[0:nrows, :], src_rows_ap, [], [b_rt])
            TR(PB[bank][:, 0:nrows], rt[0:nrows, :], identf[0:nrows, 0:nrows], [b_rt, b_idf], [BK[bank]])
            CP("dve", dst, PB[bank][:, 0:nrows], [BK[bank]], [dst_buf])

        def mod_items(st, l2, bank_acc, bank_rc):
            wsl = [sbuf(st, "adasl%d" % i, [128, KC, 512], BF16) for i in range(2)]; b_wsl = [Buf(), Buf()]
            badaT = sbuf(st, "badaT", [128, 96], F32); b_bada = Buf()
            rtb = sbuf(st, "rt_bada", [128, 128], F32); b_rtb = Buf()
            items = []

            def slab_dma(sl):
                if sl < 24:
                    DMA("pool", wsl[sl % 2][:], Wd["w_ada"][l2][:, sl * 512:(sl + 1) * 512].rearrange("(kc p) n -> p kc n", p=128), [], [b_wsl[sl % 2]])

            def mk(sl):
                def f():
                    if sl == 0:
                        slab_dma(0)
                    slab_dma(sl + 1)
                    i = sl % 2
                    for jj in range(4):
                        j = sl * 4 + jj
                        for kc in range(KC):
                            MM(PB[bank_acc][:, j:j + 1], wsl[i][:, kc, jj * 128:(jj + 1) * 128], cTb[:, kc:kc + 1], kc == 0, kc == KC - 1, [b_wsl[i], b_cTb], [BK[bank_acc]])
                return f
            for sl in range(24):
                items.append(mk(sl))

            def fin():
                DMA("sp", rtb[0:96, :], Wd["b_ada"][l2].rearrange("(k p) -> k p", p=128), [], [b_rtb])
                TR(PB[bank_rc][:, 0:96], rtb[0:96, :], identf[0:96, 0:96], [b_rtb, b_idf], [BK[bank_rc]])
                CP("act", badaT[:], PB[bank_rc][:, 0:96], [BK[bank_rc]], [b_bada])
                TT("dve", modT[:, l2, :], PB[bank_acc][:, 0:96], badaT[:], ALU.add, [BK[bank_acc], b_bada], [b_mod])
                STT("dve", AB[:, l2, 0:16], modT[:, l2, 16:32], 1.0, gT[:, l2, 0:16], ALU.add, ALU.mult, [b_mod, b_gT], [b_AB])
                STT("dve", AB[:, l2, 16:32], modT[:, l2, 64:80], 1.0, gT[:, l2, 16:32], ALU.add, ALU.mult, [b_mod, b_gT], [b_AB])
            items.append(fin)
            return items

        with ExitStack() as st:
            xt = [sbuf(st, "xt%d" % i, [128, D], F32) for i in range(2)]; b_xt = [Buf(), Buf()]
            xs = [sbuf(st, "xs%d" % i, [128, KC, 128], F32) for i in range(2)]; b_xs = [Buf(), Buf()]
            xTv = xT.rearrange("(kc p) s -> p kc s", p=128)
            for tt in range(NT):
                i = tt % 2
                DMA("sp", xt[i][:], x_in[tt * 128:(tt + 1) * 128, :], [], [b_xt[i]])
                for g4 in range(4):
                    bk = (tt * 4 + g4) % 8
                    for j in range(4):
                        kc = g4 * 4 + j
                        TR(PB[bk][:, j * 128:(j + 1) * 128], xt[i][:, kc * 128:(kc + 1) * 128], identf[:], [b_xt[i], b_idf], [BK[bk]])
                    CP("act" if g4 % 2 else "dve", xs[i][:, g4 * 4:(g4 + 1) * 4, :], PB[bk][:].rearrange("p (j t) -> p j t", j=4), [BK[bk]], [b_xs[i]])
                DMA("sp", xTv[:, :, tt * 128:(tt + 1) * 128], xs[i][:], [b_xs[i]], [])

            posr = sbuf(st, "posr", [16, 128], I32); b_posr = Buf()
            posf = sbuf(st, "posf", [16, 128], F32); b_posf = Buf()
            posT = sbuf(st, "posT", [128, 16], F32); b_posT = Buf()
            invf = sbuf(st, "invf", [128, 56], F32); b_invf = Buf()
            kf = sbuf(st, "kf", [128, NT, 56], F32); b_kf = Buf()
            ki_ = sbuf(st, "ki_", [128, NT, 56], I32); b_ki = Buf()
            DMA("sp", posr[0:NT, :], pos_in.rearrange("(t p) -> t p", p=128), [], [b_posr])
            DMA("sp", invf[:], invf_in[:, :], [], [b_invf])
            CP("dve", posf[0:NT, :], posr[0:NT, :], [b_posr], [b_posf])
            TR(PB[0][:, 0:NT], posf[0:NT, :], identf[0:NT, 0:NT], [b_posf, b_idf], [BK[0]])
            CP("dve", posT[:, 0:NT], PB[0][:, 0:NT], [BK[0]], [b_posT])
            sinv = tab[:, :, 0:56]
            cosv = tab[:, :, 56:112]
            for tt in range(NT):
                TS("dve", tab[:, tt, 0:56], invf[:], posT[:, tt:tt + 1], None, ALU.mult, None, [b_invf, b_posT], [b_tab])
            C1 = 6.28125
            C2 = float(2 * np.pi - 6.28125)
            RW = [b_tab, b_kf, b_ki]
            TS("dve", kf[:], sinv, float(1.0 / (2 * np.pi)), None, ALU.mult, None, RW, RW)
            CP("dve", ki_[:], kf[:], RW, RW)
            CP("dve", kf[:], ki_[:], RW, RW)
            STT("dve", sinv, kf[:], -C1, sinv, ALU.mult, ALU.add, RW, RW)
            STT("dve", sinv, kf[:], -C2, sinv, ALU.mult, ALU.add, RW, RW)
            TS("dve", kf[:], sinv, PI, -TWO_PI, ALU.is_gt, ALU.mult, RW, RW)
            TT("dve", sinv, sinv, kf[:], ALU.add, RW, RW)
            TS("dve", kf[:], sinv, -PI, TWO_PI, ALU.is_lt, ALU.mult, RW, RW)
            TT("dve", sinv, sinv, kf[:], ALU.add, RW, RW)
            TS("dve", cosv, sinv, PI / 2, None, ALU.add, None, RW, RW)
            TS("dve", kf[:], cosv, PI, -TWO_PI, ALU.is_gt, ALU.mult, RW, RW)
            TT("dve", cosv, cosv, kf[:], ALU.add, RW, RW)
            ACT(tab[:], tab[:], AF.Sin, RW, RW)

            cT = sbuf(st, "cT", [128, 16], F32); b_cT = Buf()
            rows_to_cols(st, c_in.rearrange("(k p) -> k p", p=128), 16, cT[:], b_cT, 1, "c")
            ACT(cT[:], cT[:], AF.Silu, [b_cT], [b_cT])
            CP("dve", cTb[:], cT[:], [b_cT], [b_cTb])
            for l in range(L):
                rows_to_cols(st, Wd["g_attn"][l].rearrange("(k p) -> k p", p=128), 16, gT[:, l, 0:16], b_gT, 2, "ga%d" % l)
                rows_to_cols(st, Wd["g_ffn"][l].rearrange("(k p) -> k p", p=128), 16, gT[:, l, 16:32], b_gT, 3, "gf%d" % l)
            for it_ in mod_items(st, 0, 5, 4):
                it_()
        P.barrier()

        xTv = xT.rearrange("(kc p) s -> p kc s", p=128)

        def norm_phase(st, hT, b_hT, A_ap, B_ap):
            xc = [sbuf(st, "n_xc%d" % i, [128, KC, 512], F32) for i in range(2)]; b_xc = [Buf(), Buf()]
            sq = [sbuf(st, "n_sq%d" % i, [128, 512], F32) for i in range(2)]; b_sq = [Buf(), Buf()]
            rs = sbuf(st, "n_rs", [128, 512], F32); b_rs = Buf()
            tmp = [sbuf(st, "n_tmp%d" % i, [128, 512], F32) for i in range(2)]; b_tmp = [Buf(), Buf()]
            for ch in range(NCH):
                i = ch % 2
                cs = slice(ch * 512, (ch + 1) * 512)
                DMA("sp", xc[i][:], xTv[:, :, cs], [], [b_xc[i]])
                bk = ch % 2
                for kc in range(KC):
                    j = kc % 2
                    ACT(sq[j][:], xc[i][:, kc, :], AF.Square, [b_xc[i]], [b_sq[j]])
                    MM(PB[bk][:], ones_f[:], sq[j][:], kc == 0, kc == KC - 1, [b_1f, b_sq[j]], [BK[bk]])
                CP("dve", rs[:], PB[bk][:], [BK[bk]], [b_rs])
                RSTD(rs[:], 1.0 / D, [b_rs])
                for kc in range(KC):
                    j = kc % 2
                    TT("dve" if kc % 2 else "pool", tmp[j][:], xc[i][:, kc, :], rs[:], ALU.mult, [b_xc[i], b_rs], [b_tmp[j]])
                    ACT(hT[:, kc, cs], tmp[j][:], AF.Identity, [b_tmp[j], b_AB, b_mod], [b_hT], scale=A_ap[:, kc:kc + 1], bias=B_ap[:, kc:kc + 1])

        def bcast_load(st, name, src_row_ap, n):
            t = sbuf(st, name, [128, n], F32)
            b = Buf()
            DMA("sp", t[:], src_row_ap.to_broadcast([128, n]), [], [b])
            return t, b

        def rope(st_tiles, src3, dst3, nh, half, cos_ap, sin_ap, rd, wr, eng="pool"):
            ta, tb_, b_t = st_tiles
            cosb = cos_ap.unsqueeze(1).to_broadcast([128, nh, half])
            sinb = sin_ap.unsqueeze(1).to_broadcast([128, nh, half])
            t1 = src3[:, :, 0:half]
            t2 = src3[:, :, half:2 * half]
            av = ta[:, 0:nh * half].rearrange("p (h d) -> p h d", h=nh)
            bv = tb_[:, 0:nh * half].rearrange("p (h d) -> p h d", h=nh)
            TT(eng, av, t1, cosb, ALU.mult, rd + [b_tab], [b_t])
            TT(eng, bv, t2, sinb, ALU.mult, rd + [b_tab], [b_t])
            TT(eng, dst3[:, :, 0:half], av, bv, ALU.subtract, [b_t], wr)
            TT(eng, av, t2, cosb, ALU.mult, rd + [b_tab], [b_t] + wr)
            TT(eng, bv, t1, sinb, ALU.mult, rd + [b_tab], [b_t])
            TT(eng, dst3[:, :, half:2 * half], av, bv, ALU.add, [b_t], wr)

        def make_p3(st, qt_list, banks):
            kiS = sbuf(st, "kiS", [128, S], BF16); b_kiS = Buf()
            qit = [sbuf(st, "qit%d" % i, [128, 8, 128], BF16) for i in range(2)]; b_qit = [Buf(), Buf()]
            sc = [sbuf(st, "sc%d" % i, [128, S], F32) for i in range(2)]; b_sc = [Buf(), Buf()]
            wk = sbuf(st, "wk", [128, S], F32); b_wk = Buf()
            rl = [sbuf(st, "rl%d" % i, [128, 512], BF16) for i in range(3)]; b_rl = [Buf(), Buf(), Buf()]
            wv = [sbuf(st, "wv%d" % i, [128, 16], F32) for i in range(2)]; b_wv = [Buf(), Buf()]
            dg = [sbuf(st, "dg%d" % i, [128, 16, 128], BF16) for i in range(2)]; b_dg = [Buf(), Buf()]
            mx = sbuf(st, "mx", [128, 8], F32); b_mx = Buf()
            Mb = [sbuf(st, "Mb%d" % i, [128, S], BF16) for i in range(2)]; b_Mb = [Buf(), Buf()]
            mstg = [sbuf(st, "mstg%d" % i, [128, 4, 128], BF16) for i in range(2)]; b_mstg = [Buf(), Buf()]
            qiTv = qiT.rearrange("(j p) s -> p j s", p=128)
            cnt = {"s": 0, "a": 0, "t": 0, "r": 0}
            items = []

            def first():
                DMA("sp", kiS[:], kiT2[:, :], [], [b_kiS])
            items.append((first, 0.1))

            def tile_items(qt, i):
                W = (qt + 1) * 128
                qs = slice(qt * 128, (qt + 1) * 128)
                out = []

                def prep():
                    DMA("sp", wv[i][:], wis[qs, :], [], [b_wv[i]])
                    DMA("sp", qit[i][:], qiTv[:, :, qs], [], [b_qit[i]])
                    for h in range(16):
                        ACT(dg[i][:, h, :], identb[:], AF.Copy, [b_idb, b_wv[i]], [b_dg[i]], scale=wv[i][:, h:h + 1])
                out.append((prep, 0.5))

                def chunk(k0, kw):
                    def f():
                        ab = banks["acc"][cnt["a"] % len(banks["acc"])]
                        cnt["a"] += 1
                        sbs = []

                        def dots(h):
                            sb = banks["score"][cnt["s"] % len(banks["score"])]
                            cnt["s"] += 1
                            pr = slice((h % 2) * 64, (h % 2) * 64 + 64)
                            MM(PB[sb][:, 0:kw], qit[i][pr, h // 2, :], kiS[pr, k0:k0 + kw], True, True, [b_qit[i], b_kiS], [BK[sb]])
                            sbs.append(sb)
                        dots(0)
                        for h in range(16):
                            if h + 1 < 16:
                                dots(h + 1)
                            r = cnt["r"] % 3
                            cnt["r"] += 1
                            ACT(rl[r][:, 0:kw], PB[sbs[h]][:, 0:kw], AF.Relu, [BK[sbs[h]]], [b_rl[r]])
                            MM(PB[ab][:, 0:kw], dg[i][:, h, :], rl[r][:, 0:kw], h == 0, h == 15, [b_dg[i], b_rl[r]], [BK[ab]])
                        CP("act", sc[i][:, k0:k0 + kw], PB[ab][:, 0:kw], [BK[ab]], [b_sc[i]])
                    return f
                for k0 in range(0, W, 512):
                    out.append((chunk(k0, min(512, W - k0)), 2.0 * min(512, W - k0) / 512))

                def causal():
                    TT("dve", sc[i][:, qt * 128:W], sc[i][:, qt * 128:W], negm[:], ALU.add, [b_sc[i], b_negm], [b_sc[i]])
                out.append((causal, 0.2))
                if W > n_sel:
                    rounds = n_sel // 8

                    def topk(r0, r1):
                        def f():
                            for r in range(r0, r1):
                                src = sc[i] if r == 0 else wk
                                MAX8(mx[:], src[:, 0:W], [b_sc[i], b_wk], [b_mx])
                                if r < rounds - 1:
                                    MREP(wk[:, 0:W], mx[:], src[:, 0:W], [b_sc[i], b_mx, b_wk], [b_wk])
                        return f
                    for r0 in range(0, rounds, 4):
                        out.append((topk(r0, min(rounds, r0 + 4)), (min(rounds, r0 + 4) - r0) * 2.0 * W / 1000.0))

                    def mask():
                        TS("dve", Mb[i][:, 0:W], sc[i][:, 0:W], mx[:, 7:8], None, ALU.is_ge, None, [b_sc[i], b_mx], [b_Mb[i]])
                else:
                    def mask():
                        TS("dve", Mb[i][:, 0:W], sc[i][:, 0:W], -1e29, None, ALU.is_ge, None, [b_sc[i]], [b_Mb[i]])
                out.append((mask, W / 1000.0))

                def trans(kb0, nb):
                    def f():
                        bk = banks["tr"][cnt["t"] % len(banks["tr"])]
                        sgi = cnt["t"] % 2
                        cnt["t"] += 1
                        for j in range(nb):
                            TR(PBb[bk][:, j * 128:(j + 1) * 128], Mb[i][:, (kb0 + j) * 128:(kb0 + j + 1) * 128], identb[:], [b_Mb[i], b_idb], [BK[bk]])
                        CP("act", mstg[sgi][:, 0:nb, :], PBb[bk][:, 0:nb * 128].rearrange("p (j t) -> p j t", j=nb), [BK[bk]], [b_mstg[sgi]])
                        DMA("sp", MT[kb0 * 128:(kb0 + nb) * 128, qs].rearrange("(j p) t -> p j t", p=128), mstg[sgi][:, 0:nb, :], [b_mstg[sgi]], [])
                    return f
                for kb0 in range(0, qt + 1, 4):
                    out.append((trans(kb0, min(4, qt + 1 - kb0)), 0.5))
                nfront = 1 + len(range(0, W, 512))
                ntr = len(range(0, qt + 1, 4))
                return out[:nfront], out[nfront:len(out) - ntr], out[len(out) - ntr:]

            fb = [tile_items(qt, n_ % 2) for n_, qt in enumerate(qt_list)]
            if fb:
                items.extend(fb[0][0])
            for n_ in range(len(fb)):
                if n_ + 1 < len(fb):
                    items.extend(fb[n_ + 1][0])
                items.extend(fb[n_][1])
                if n_ >= 1:
                    items.extend(fb[n_ - 1][2])
            if fb:
                items.extend(fb[-1][2])
            return items

        for l in range(L):
            with ExitStack() as st:
                hT = sbuf(st, "hT", [128, KC, S], BF16); b_hT = Buf()
                with ExitStack() as st1:
                    norm_phase(st1, hT, b_hT, AB[:, l, 0:16], modT[:, l, 0:16])
                P.barrier()
                slab = [sbuf(st, "slab%d" % i, [128, KC, 512], BF16) for i in range(2)]; b_slab = [Buf(), Buf()]
                st_tm = ExitStack()
                gqa, b_gqa = bcast_load(st_tm, "gqa", Wd["g_qa"][l:l + 1, :], 128)
                gka, b_gka = bcast_load(st_tm, "gka", Wd["g_ka"][l:l + 1, :], 128)
                NB = 8
                qn = [sbuf(st_tm, "qn%d" % i, [128, 512], F32) for i in range(NB)]; b_qn = [Buf() for _ in range(NB)]
                qb = [sbuf(st_tm, "qb%d" % i, [128, 512], BF16) for i in range(NB)]; b_qb = [Buf() for _ in range(NB)]
                ssq = [sbuf(st_tm, "ssq%d" % i, [128, 8], F32) for i in range(NB)]; b_ssq = [Buf() for _ in range(NB)]
                junk = [sbuf(st_tm, "junk%d" % i, [128, 128], F32) for i in range(2)]; b_junk = [Buf(), Buf()]
                rts = [(sbuf(st_tm, "rt_a%d" % i, [128, 64], F32), sbuf(st_tm, "rt_b%d" % i, [128, 64], F32), Buf()) for i in range(4)]
                stg = [sbuf(st_tm, "stg%d" % i, [128, 4, 128], BF16) for i in range(4)]; b_stg = [Buf() for _ in range(4)]
                ki2 = [sbuf(st_tm, "ki2_%d" % i, [128, 128], BF16) for i in range(4)]; b_ki2 = [Buf() for _ in range(4)]
                slab_n = [0]

                def load_slab(col_ranges):
                    i = slab_n[0] % 2
                    slab_n[0] += 1
                    off = 0
                    for (c0, n) in col_ranges:
                        DMA("pool", slab[i][:, :, off:off + n], Wd["w_in"][l][:, c0:c0 + n].rearrange("(kc p) n -> p kc n", p=128), [], [b_slab[i]])
                        off += n
                    return i, off

                def tok_mm(i, ncols, tt, bk):
                    for kc in range(KC):
                        MM(PB[bk][:, 0:ncols], hT[:, kc, tt * 128:(tt + 1) * 128], slab[i][:, kc, 0:ncols], kc == 0, kc == KC - 1, [b_hT, b_slab[i]], [BK[bk]])

                units = []

                def qk_unit(u, i, tt, nh, g_t, b_g, dst_rows, nblk, has_va):
                    b = u % NB; bk = u % 3; tb = 4 + u % 4; sg = u % 4; rt = rts[u % 4]; jk = u % 2
                    ts_ = slice(tt * 128, (tt + 1) * 128)

                    def s0():
                        tok_mm(i, 512, tt, bk)

                    def s1():
                        CP("act", qn[b][:], PB[bk][:], [BK[bk]], [b_qn[b]])

                    def s2():
                        for h in range(nh):
                            ACT(junk[jk][:], qn[b][:, h * 128:(h + 1) * 128], AF.Square, [b_qn[b]], [b_junk[jk], b_ssq[b]], accum_out=ssq[b][:, h:h + 1])

                    def s3():
                        ACT(ssq[b][:, 0:nh], ssq[b][:, 0:nh], AF.Sqrt, [b_ssq[b], b_eps], [b_ssq[b]], scale=1.0 / 128, bias=eps_t[:, 0:1])

                    def s4():
                        RECIP(ssq[b][:, 0:nh], ssq[b][:, 0:nh], [b_ssq[b]], [b_ssq[b]])

                    def s5():
                        for h in range(nh):
                            hs = slice(h * 128, (h + 1) * 128)
                            STT("dve", qn[b][:, hs], qn[b][:, hs], ssq[b][:, h:h + 1], g_t[:], ALU.mult, ALU.mult, [b_qn[b], b_ssq[b], b_g], [b_qn[b]])

                    def s6():
                        CP("pool", qb[b][:], qn[b][:], [b_qn[b]], [b_qb[b]])
                        s3_ = qn[b][:, 0:nh * 128].rearrange("p (h d) -> p h d", h=nh)[:, :, 0:32]
                        d3_ = qb[b][:, 0:nh * 128].rearrange("p (h d) -> p h d", h=nh)[:, :, 0:32]
                        rope(rt, s3_, d3_, nh, 16, tab[:, tt, 56:72], tab[:, tt, 0:16], [b_qn[b]], [b_qb[b]])

                    def s7():
                        for j in range(nblk):
                            TR(PBb[tb][:, j * 128:(j + 1) * 128], qb[b][:, j * 128:(j + 1) * 128], identb[:], [b_qb[b], b_idb], [BK[tb]])

                    def s8():
                        CP("act", stg[sg][:, 0:nblk, :], PBb[tb][:, 0:nblk * 128].rearrange("p (j t) -> p j t", j=nblk), [BK[tb]], [b_stg[sg]])
                        DMA("sp", dst_rows[0:nblk * 128, ts_].rearrange("(j p) t -> p j t", p=128), stg[sg][:, 0:nblk, :], [b_stg[sg]], [])
                        if has_va:
                            DMA("sp", va[ts_, :], qb[b][:, 256:512], [b_qb[b]], [])
                    return [s0, s1, s2, s3, s4, s5, s6, s7, s8]

                def qi_unit(u, i, tt, dst_rows):
                    b = u % NB; bk = u % 3; tb = 4 + u % 4; sg = u % 4; rt = rts[u % 4]
                    ts_ = slice(tt * 128, (tt + 1) * 128)

                    def s0():
                        tok_mm(i, 512, tt, bk)

                    def s1():
                        CP("act", qn[b][:], PB[bk][:], [BK[bk]], [b_qn[b]])

                    def s2():
                        CP("pool", qb[b][:], qn[b][:], [b_qn[b]], [b_qb[b]])
                        s3_ = qn[b][:].rearrange("p (h d) -> p h d", h=8)[:, :, 0:16]
                        d3_ = qb[b][:].rearrange("p (h d) -> p h d", h=8)[:, :, 0:16]
                        rope(rt, s3_, d3_, 8, 8, tab[:, tt, 72:80], tab[:, tt, 16:24], [b_qn[b]], [b_qb[b]])

                    def s3():
                        for j in range(4):
                            TR(PBb[tb][:, j * 128:(j + 1) * 128], qb[b][:, j * 128:(j + 1) * 128], identb[:], [b_qb[b], b_idb], [BK[tb]])

                    def s4():
                        CP("act", stg[sg][:, 0:4, :], PBb[tb][:, 0:512].rearrange("p (j t) -> p j t", j=4), [BK[tb]], [b_stg[sg]])
                        DMA("sp", dst_rows[0:512, ts_].rearrange("(j p) t -> p j t", p=128), stg[sg][:, 0:4, :], [b_stg[sg]], [])
                    nop = lambda: None
                    return [s0, s1, nop, nop, nop, nop, s2, s3, s4]

                def misc_unit(u, i, tt):
                    b = u % NB; bk = u % 3; tb = 4 + u % 4; sg = u % 4; rt = rts[u % 4]; kk = u % 4
                    ts_ = slice(tt * 128, (tt + 1) * 128)

                    def s0():
                        tok_mm(i, 144, tt, bk)

                    def s1():
                        CP("act", qn[b][:, 0:144], PB[bk][:, 0:144], [BK[bk]], [b_qn[b]])

                    def s2():
                        CP("pool", ki2[kk][:, 0:64], qn[b][:, 0:64], [b_qn[b]], [b_ki2[kk]])
                        s3_ = qn[b][:, 0:64].rearrange("p (h d) -> p h d", h=1)[:, :, 0:16]
                        d3_ = ki2[kk][:, 0:64].rearrange("p (h d) -> p h d", h=1)[:, :, 0:16]
                        rope(rt, s3_, d3_, 1, 8, tab[:, tt, 72:80], tab[:, tt, 16:24], [b_qn[b]], [b_ki2[kk]])
                        CP("pool", ki2[kk][:, 64:128], ki2[kk][:, 0:64], [b_ki2[kk]], [b_ki2[kk]])
                        TS("pool", qn[b][:, 64:80], qn[b][:, 64:80], float(1024 ** -0.5), None, ALU.mult, None, [b_qn[b]], [b_qn[b]])

                    def s3():
                        TR(PBb[tb][:, 0:128], ki2[kk][:], identb[:], [b_ki2[kk], b_idb], [BK[tb]])
                        DMA("sp", wis[ts_, :], qn[b][:, 64:80], [b_qn[b]], [])
                        DMA("sp", mkr[ts_, :], qn[b][:, 80:144], [b_qn[b]], [])

                    def s4():
                        CP("act", stg[sg][:, 0, :], PBb[tb][:, 0:128], [BK[tb]], [b_stg[sg]])
                        DMA("sp", kiT2[:, ts_], stg[sg][:, 0, :], [b_stg[sg]], [])
                    nop = lambda: None
                    return [s0, s1, nop, nop, nop, nop, s2, s3, s4]

                groups = [("qa", [(0, 512)]), ("qa", [(512, 512)]), ("ka", [(1024, 512)]), ("qi", [(1536, 512)]), ("qi", [(2048, 512)]), ("misc", [(2560, 80), (3408, 64)])]

                def slab_load(gi):
                    i = gi % 2
                    off = 0
                    for (c0, n) in groups[gi][1]:
                        DMA("pool", slab[i][:, :, off:off + n], Wd["w_in"][l][:, c0:c0 + n].rearrange("(kc p) n -> p kc n", p=128), [], [b_slab[i]])
                        off += n

                def with_pre(stages, pre):
                    s0 = stages[0]

                    def s0p():
                        pre()
                        s0()
                    return [s0p] + stages[1:]

                u = 0
                for gi, (kind, cr) in enumerate(groups):
                    i = gi % 2
                    for tt in range(NT):
                        if kind == "qa":
                            stages = qk_unit(u, i, tt, 4, gqa, b_gqa, qaT[gi * 512:(gi + 1) * 512, :], 4, False)
                        elif kind == "ka":
                            stages = qk_unit(u, i, tt, 2, gka, b_gka, kaT, 2, True)
                        elif kind == "qi":
                            stages = qi_unit(u, i, tt, qiT[(gi - 3) * 512:(gi - 2) * 512, :])
                        else:
                            stages = misc_unit(u, i, tt)
                        if tt == 0:
                            def pre(gi=gi):
                                if gi == 0:
                                    slab_load(0)
                                if gi + 1 < len(groups):
                                    slab_load(gi + 1)
                            stages = with_pre(stages, pre)
                        units.append(stages)
                        u += 1
                run_skewed(units)
                slab_n[0] = len(groups)
                P.barrier()
                st_tm.close()
                st_f = ExitStack()
                lat = sbuf(st_f, "lat", [128, 4, 512], F32); b_lat = Buf()
                latb = [sbuf(st_f, "latb%d" % i, [128, 512], BF16) for i in range(2)]; b_latb = [Buf(), Buf()]
                lsq = [sbuf(st_f, "lsq%d" % i, [128, 512], F32) for i in range(2)]; b_lsq = [Buf(), Buf()]
                lrs = sbuf(st_f, "lrs", [128, 512], F32); b_lrs = Buf()
                glatT = sbuf(st_f, "glatT", [128, 8], F32); b_glat = Buf()
                rows_to_cols(st_f, Wd["g_mq_lat"][l].rearrange("(k p) -> k p", p=128), 4, glatT[:, 0:4], b_glat, 6, "gmq")
                rows_to_cols(st_f, Wd["g_mkv_lat"][l].rearrange("(k p) -> k p", p=128), 2, glatT[:, 4:6], b_glat, 6, "gmkv")
                for (c0, nm, dst, goff) in ((2640, 4, mqlT, 0), (3152, 2, mkvlT, 4)):
                    i, _ = load_slab([(c0, nm * 128)])
                    for ch in range(NCH):
                        cs = slice(ch * 512, (ch + 1) * 512)
                        for m in range(nm):
                            bk = m % 4
                            for kc in range(KC):
                                MM(PB[bk][:], slab[i][:, kc, m * 128:(m + 1) * 128], hT[:, kc, cs], kc == 0, kc == KC - 1, [b_slab[i], b_hT], [BK[bk]])
                            CP("dve", lat[:, m, :], PB[bk][:], [BK[bk]], [b_lat])
                            ACT(lsq[m % 2][:], PB[bk][:], AF.Square, [BK[bk]], [b_lsq[m % 2]])
                            MM(PB[4][:], ones_f[:], lsq[m % 2][:], m == 0, m == nm - 1, [b_1f, b_lsq[m % 2]], [BK[4]])
                        CP("dve", lrs[:], PB[4][:], [BK[4]], [b_lrs])
                        RSTD(lrs[:], 1.0 / (nm * 128), [b_lrs])
                        for m in range(nm):
                            STT("dve", latb[m % 2][:], lat[:, m, :], glatT[:, goff + m:goff + m + 1], lrs[:], ALU.mult, ALU.mult, [b_lat, b_glat, b_lrs], [b_latb[m % 2]])
                            DMA("sp", dst[m * 128:(m + 1) * 128, cs], latb[m % 2][:], [b_latb[m % 2]], [])
                P.barrier()
                st_f.close()
                sg = [sbuf(st, "sg%d" % i, [128, 512], F32) for i in range(2)]; b_sg = [Buf(), Buf()]
                gslabs = [(c0 + s4 * 512, dst, s4) for (c0, dst) in ((3472, sgaT), (5520, sgbT)) for s4 in range(4)]
                gslab_i = {}

                def gate_slab(gn):
                    if gn < len(gslabs) and gn not in gslab_i:
                        gslab_i[gn] = load_slab([(gslabs[gn][0], 512)])[0]
                gate_items = []
                gcnt = [0]

                def gate_item(gn, ch, m):
                    def f():
                        if ch == 0 and m == 0:
                            gate_slab(gn)
                            gate_slab(gn + 1)
                        i = gslab_i[gn]
                        c0_, dst, s4 = gslabs[gn]
                        n = gcnt[0]
                        gcnt[0] += 1
                        cs = slice(ch * 512, (ch + 1) * 512)
                        bk = 4 + n % 2
                        for kc in range(KC):
                            MM(PB[bk][:], slab[i][:, kc, m * 128:(m + 1) * 128], hT[:, kc, cs], kc == 0, kc == KC - 1, [b_slab[i], b_hT], [BK[bk]])
                        ACT(sg[n % 2][:], PB[bk][:], AF.Sigmoid, [BK[bk]], [b_sg[n % 2]])
                        r0 = s4 * 512 + m * 128
                        DMA("sp", dst[r0:r0 + 128, cs], sg[n % 2][:], [b_sg[n % 2]], [])
                    return f
                for gn in range(len(gslabs)):
                    for ch in range(NCH):
                        for m in range(4):
                            gate_items.append(gate_item(gn, ch, m))
                p3 = make_p3(st, list(range(NT)), dict(score=[0, 1], acc=[2], tr=[3]))
                side = gate_items
                if l + 1 < L:
                    mi_ = mod_items(st, l + 1, 6, 7)
                    side = []
                    step_ = max(1, len(gate_items) // len(mi_))
                    k_ = 0
                    for gi_, g_ in enumerate(gate_items):
                        side.append(g_)
                        if (gi_ + 1) % step_ == 0 and k_ < len(mi_):
                            side.append(mi_[k_]); k_ += 1
                    side.extend(mi_[k_:])
                interleave(p3, side)
            P.barrier()

            with ExitStack() as st:
                mqlS = sbuf(st, "mqlS", [128, 4, S], BF16); b_mqlS = Buf()
                mkvlS = sbuf(st, "mkvlS", [128, 2, S], BF16); b_mkvlS = Buf()
                wq = sbuf(st, "wq", [128, 4, 1536], BF16); b_wq = Buf()
                wkv = sbuf(st, "wkv", [128, 2, 2048], BF16); b_wkv = Buf()
                DMA("sp", mqlS[:], mqlT.rearrange("(kc p) s -> p kc s", p=128), [], [b_mqlS])
                DMA("sp", mkvlS[:], mkvlT.rearrange("(kc p) s -> p kc s", p=128), [], [b_mkvlS])
                DMA("pool", wq[:], Wd["w_mq_up"][l].rearrange("(kc p) n -> p kc n", p=128), [], [b_wq])
                DMA("pool", wkv[:], Wd["w_mkv_up"][l].rearrange("(kc p) n -> p kc n", p=128), [], [b_wkv])
                gqm, b_gqm = bcast_load(st, "gqm", Wd["g_qm"][l:l + 1, :], 192)
                gkm, b_gkm = bcast_load(st, "gkm", Wd["g_km"][l:l + 1, :], 192)
                NB = 8
                qn = [sbuf(st, "m_qn%d" % i, [128, 512], F32) for i in range(NB)]; b_qn = [Buf() for _ in range(NB)]
                qb = [sbuf(st, "m_qb%d" % i, [128, 384], BF16) for i in range(NB)]; b_qb = [Buf() for _ in range(NB)]
                kb_ = [sbuf(st, "m_kb%d" % i, [128, 384], BF16) for i in range(NB)]; b_kb = [Buf() for _ in range(NB)]
                kr = [sbuf(st, "m_kr%d" % i, [128, 128], F32) for i in range(NB)]; b_kr = [Buf() for _ in range(NB)]
                vb = [sbuf(st, "m_vb%d" % i, [128, 256], BF16) for i in range(NB)]; b_vb = [Buf() for _ in range(NB)]
                ssq = [sbuf(st, "m_ssq%d" % i, [128, 4], F32) for i in range(NB)]; b_ssq = [Buf() for _ in range(NB)]
                junk = [sbuf(st, "m_junk%d" % i, [128, 192], F32) for i in range(2)]; b_junk = [Buf(), Buf()]
                rts = [(sbuf(st, "m_rt_a%d" % i, [128, 64], F32), sbuf(st, "m_rt_b%d" % i, [128, 64], F32), Buf()) for i in range(4)]
                stg = [sbuf(st, "m_stg%d" % i, [128, 2, 128], BF16) for i in range(4)]; b_stg = [Buf() for _ in range(4)]
                stgr = [sbuf(st, "m_stgr%d" % i, [64, 2, 128], BF16) for i in range(4)]; b_stgr = [Buf() for _ in range(4)]
                mkr_t = [sbuf(st, "m_mkr%d" % i, [128, 64], F32) for i in range(3)]; b_mkr = [Buf() for _ in range(3)]
                ssr = [sbuf(st, "m_ssr%d" % i, [128, 1], F32) for i in range(3)]; b_ssr = [Buf() for _ in range(3)]

                def mq_unit(u, tt, g):
                    b = u % NB; bk = 6 + u % 2; tb = 4 + u % 2; sg = u % 4; rt = rts[u % 4]; jk = u % 2
                    ts_ = slice(tt * 128, (tt + 1) * 128)

                    def s0():
                        for kc in range(4):
                            MM(PB[bk][:, 0:384], mqlS[:, kc, ts_], wq[:, kc, g * 384:(g + 1) * 384], kc == 0, kc == 3, [b_mqlS, b_wq], [BK[bk]])

                    def s1():
                        CP("act", qn[b][:, 0:384], PB[bk][:, 0:384], [BK[bk]], [b_qn[b]])

                    def s2():
                        for h in range(2):
                            ACT(junk[jk][:, 0:192], qn[b][:, h * 192:(h + 1) * 192], AF.Square, [b_qn[b]], [b_junk[jk], b_ssq[b]], accum_out=ssq[b][:, h:h + 1])

                    def s3():
                        pass

                    def s4():
                        ACT(ssq[b][:, 0:2], ssq[b][:, 0:2], AF.Sqrt, [b_ssq[b], b_eps], [b_ssq[b]], scale=1.0 / 192, bias=eps_t[:, 0:1])

                    def s5():
                        RECIP(ssq[b][:, 0:2], ssq[b][:, 0:2], [b_ssq[b]], [b_ssq[b]])

                    def s6():
                        for h in range(2):
                            hs = slice(h * 192, (h + 1) * 192)
                            STT("dve", qn[b][:, hs], qn[b][:, hs], ssq[b][:, h:h + 1], gqm[:], ALU.mult, ALU.mult, [b_qn[b], b_ssq[b], b_gqm], [b_qn[b]])

                    def s7():
                        CP("pool", qb[b][:], qn[b][:, 0:384], [b_qn[b]], [b_qb[b]])
                        s3_ = qn[b][:, 0:384].rearrange("p (h d) -> p h d", h=2)[:, :, 128:192]
                        d3_ = qb[b][:].rearrange("p (h d) -> p h d", h=2)[:, :, 128:192]
                        rope(rt, s3_, d3_, 2, 32, tab[:, tt, 80:112], tab[:, tt, 24:56], [b_qn[b]], [b_qb[b]], eng="dve")

                    def s8():
                        for h in range(2):
                            TR(PBb[tb][:, h * 128:(h + 1) * 128], qb[b][:, h * 192:h * 192 + 128], identb[:], [b_qb[b], b_idb], [BK[tb]])
                            TR(PBb[tb][0:64, 256 + h * 128:256 + (h + 1) * 128], qb[b][:, h * 192 + 128:(h + 1) * 192], identb[:], [b_qb[b], b_idb], [BK[tb]])

                    def s9():
                        CP("act", stg[sg][:, :, :], PBb[tb][:, 0:256].rearrange("p (j t) -> p j t", j=2), [BK[tb]], [b_stg[sg]])
                        CP("act", stgr[sg][:, :, :], PBb[tb][0:64, 256:512].rearrange("p (j t) -> p j t", j=2), [BK[tb]], [b_stgr[sg]])
                        DMA("sp", mqTn[g * 256:(g + 1) * 256, ts_].rearrange("(j p) t -> p j t", p=128), stg[sg][:, :, :], [b_stg[sg]], [])
                        DMA("sp", mqTr[g * 128:(g + 1) * 128, ts_].rearrange("(j p) t -> p j t", p=64), stgr[sg][:, :, :], [b_stgr[sg]], [])
                    return [s0, s1, s2, s3, s4, s5, s6, s7, s8, s9]

                def mkv_unit(u, tt, g):
                    b = u % NB; bk = 6 + u % 2; tb = 4 + u % 2; sg = u % 4; rt = rts[u % 4]; jk = u % 2; tr3 = tt % 3
                    ts_ = slice(tt * 128, (tt + 1) * 128)

                    def s0():
                        for kc in range(2):
                            MM(PB[bk][:], mkvlS[:, kc, ts_], wkv[:, kc, g * 512:(g + 1) * 512], kc == 0, kc == 1, [b_mkvlS, b_wkv], [BK[bk]])
                        if g == 0:
                            DMA("sp", mkr_t[tr3][:], mkr[ts_, :], [], [b_mkr[tr3]])

                    def s1():
                        CP("act", qn[b][:], PB[bk][:], [BK[bk]], [b_qn[b]])

                    def s2():
                        if g == 0:
                            ACT(junk[jk][:, 0:64], mkr_t[tr3][:], AF.Square, [b_mkr[tr3]], [b_junk[jk], b_ssr[tr3]], accum_out=ssr[tr3][:, 0:1])
                        for h in range(2):
                            ACT(junk[jk][:, 0:128], qn[b][:, h * 256:h * 256 + 128], AF.Square, [b_qn[b]], [b_junk[jk], b_ssq[b]], accum_out=ssq[b][:, h:h + 1])

                    def s3():
                        TS("dve", ssq[b][:, 0:2], ssq[b][:, 0:2], ssr[tr3][:, 0:1], None, ALU.add, None, [b_ssq[b], b_ssr[tr3]], [b_ssq[b]])

                    def s4():
                        ACT(ssq[b][:, 0:2], ssq[b][:, 0:2], AF.Sqrt, [b_ssq[b], b_eps], [b_ssq[b]], scale=1.0 / 192, bias=eps_t[:, 0:1])

                    def s5():
                        RECIP(ssq[b][:, 0:2], ssq[b][:, 0:2], [b_ssq[b]], [b_ssq[b]])

                    def s6():
                        for h in range(2):
                            STT("dve", kb_[b][:, h * 128:(h + 1) * 128], qn[b][:, h * 256:h * 256 + 128], ssq[b][:, h:h + 1], gkm[:, 0:128], ALU.mult, ALU.mult, [b_qn[b], b_ssq[b], b_gkm], [b_kb[b]])
                            STT("dve", kr[b][:, h * 64:(h + 1) * 64], mkr_t[tr3][:], ssq[b][:, h:h + 1], gkm[:, 128:192], ALU.mult, ALU.mult, [b_mkr[tr3], b_ssq[b], b_gkm], [b_kr[b]])

                    def s7():
                        s3_ = kr[b][:].rearrange("p (h d) -> p h d", h=2)
                        d3_ = kb_[b][:, 256:384].rearrange("p (h d) -> p h d", h=2)
                        rope(rt, s3_, d3_, 2, 32, tab[:, tt, 80:112], tab[:, tt, 24:56], [b_kr[b]], [b_kb[b]], eng="dve")
                        CP("pool", vb[b][:].rearrange("p (h d) -> p h d", h=2), qn[b][:].rearrange("p (h d) -> p h d", h=2)[:, :, 128:256], [b_qn[b]], [b_vb[b]])

                    def s8():
                        for h in range(2):
                            TR(PBb[tb][:, h * 128:(h + 1) * 128], kb_[b][:, h * 128:(h + 1) * 128], identb[:], [b_kb[b], b_idb], [BK[tb]])
                            TR(PBb[tb][0:64, 256 + h * 128:256 + (h + 1) * 128], kb_[b][:, 256 + h * 64:256 + (h + 1) * 64], identb[:], [b_kb[b], b_idb], [BK[tb]])
                        DMA("sp", mv[ts_, g * 256:(g + 1) * 256], vb[b][:], [b_vb[b]], [])

                    def s9():
                        CP("act", stg[sg][:, :, :], PBb[tb][:, 0:256].rearrange("p (j t) -> p j t", j=2), [BK[tb]], [b_stg[sg]])
                        CP("act", stgr[sg][:, :, :], PBb[tb][0:64, 256:512].rearrange("p (j t) -> p j t", j=2), [BK[tb]], [b_stgr[sg]])
                        DMA("sp", mkTn[g * 256:(g + 1) * 256, ts_].rearrange("(j p) t -> p j t", p=128), stg[sg][:, :, :], [b_stg[sg]], [])
                        DMA("sp", mkTr[g * 128:(g + 1) * 128, ts_].rearrange("(j p) t -> p j t", p=64), stgr[sg][:, :, :], [b_stgr[sg]], [])
                    return [s0, s1, s2, s3, s4, s5, s6, s7, s8, s9]

                units = []
                u = 0
                for tt in range(NT):
                    for g in range(4):
                        units.append(mq_unit(u, tt, g)); u += 1
                    for g in range(4):
                        units.append(mkv_unit(u, tt, g)); u += 1
                run_skewed(units)
            P.barrier()

            with ExitStack() as st:
                kaS = sbuf(st, "kaS", [128, 2, S], BF16); b_kaS = Buf()
                vaS = sbuf(st, "vaS", [128, NT, 256], BF16); b_vaS = Buf()
                DMA("sp", kaS[:], kaT.rearrange("(g p) s -> p g s", p=128), [], [b_kaS])
                DMA("sp", vaS[:], va.rearrange("(t p) c -> p t c", p=128), [], [b_vaS])
                mkn = sbuf(st, "mkn", [128, 8, S], BF16); b_mkn = Buf()
                mkrS = sbuf(st, "mkrS", [64, 8, S], BF16); b_mkrS = Buf()
                mvS = sbuf(st, "mvS", [128, NT, 1024], BF16); b_mvS = Buf()
                DMA("sp", mkn[:], mkTn.rearrange("(h p) s -> p h s", p=128), [], [b_mkn])
                DMA("sp", mkrS[:], mkTr.rearrange("(h p) s -> p h s", p=64), [], [b_mkrS])
                DMA("sp", mvS[:], mv.rearrange("(t p) c -> p t c", p=128), [], [b_mvS])
                qT_ = [sbuf(st, "a_q%d" % i, [128, 512], BF16) for i in range(3)]; b_q = [Buf(), Buf(), Buf()]
                qR_ = [sbuf(st, "a_qr%d" % i, [64, 512], BF16) for i in range(3)]; b_qr = [Buf(), Buf(), Buf()]
                mts = [sbuf(st, "a_mts%d" % i, [128, NT, 512], BF16) for i in range(2)]; b_mts = [Buf(), Buf()]
                pT = [sbuf(st, "a_pT%d" % i, [128, 512], BF16) for i in range(4)]; b_pT = [Buf() for _ in range(4)]
                rd_ = sbuf(st, "a_rd", [128, 512], F32); b_rd = Buf()
                ob = [sbuf(st, "a_ob%d" % i, [128, 512], BF16) for i in range(2)]; b_ob = [Buf(), Buf()]
                hcount = [0]
                icount = [0]
                heads_all = [(qc_, h_) for qc_ in range(NCH) for h_ in range(16)]

                def load_q(g):
                    if g >= len(heads_all):
                        return
                    qc_, h_ = heads_all[g]
                    cs_ = slice(qc_ * 512, (qc_ + 1) * 512)
                    isA_ = h_ < 8
                    hh_ = h_ if isA_ else h_ - 8
                    i_ = g % 3
                    if isA_:
                        DMA("sp", qT_[i_][:], qaT[hh_ * 128:(hh_ + 1) * 128, cs_], [], [b_q[i_]])
                    else:
                        DMA("sp", qT_[i_][:], mqTn[hh_ * 128:(hh_ + 1) * 128, cs_], [], [b_q[i_]])
                        DMA("sp", qR_[i_][:], mqTr[hh_ * 64:(hh_ + 1) * 64, cs_], [], [b_qr[i_]])
                load_q(0)
                for qc in range(NCH):
                    cs = slice(qc * 512, (qc + 1) * 512)
                    nkb = 4 * qc + 4
                    mi = qc % 2
                    if qc == 0:
                        DMA("sp", mts[0][:, 0:4, :], MT[0:512, 0:512].rearrange("(k p) t -> p k t", p=128), [], [b_mts[0]])
                    if qc + 1 < NCH:
                        nkb2 = 4 * (qc + 1) + 4
                        DMA("sp", mts[(qc + 1) % 2][:, 0:nkb2, :], MT[0:nkb2 * 128, (qc + 1) * 512:(qc + 2) * 512].rearrange("(k p) t -> p k t", p=128), [], [b_mts[(qc + 1) % 2]])
                    items = []
                    for h in range(16):
                        hq = hcount[0]
                        hcount[0] += 1
                        for kb in range(nkb):
                            items.append((h, kb, hq, icount[0]))
                            icount[0] += 1

                    def a_scores(it, qc=qc, cs=cs):
                        h, kb, hq, n = it
                        isA = h < 8
                        hh = h if isA else h - 8
                        i = hq % 3
                        if kb == 0:
                            load_q(hq + 1)
                        q0 = max(0, kb - 4 * qc) * 128
                        ks = slice(kb * 128, (kb + 1) * 128)
                        bk = n % 4
                        if isA:
                            MM(PB[bk][:, q0:512], kaS[:, hh // 4, ks], qT_[i][:, q0:512], True, True, [b_kaS, b_q[i]], [BK[bk]])
                        else:
                            MM(PB[bk][:, q0:512], mkn[:, hh, ks], qT_[i][:, q0:512], True, False, [b_mkn, b_q[i]], [BK[bk]])
                            MM(PB[bk][:, q0:512], mkrS[:, hh, ks], qR_[i][:, q0:512], False, True, [b_mkrS, b_qr[i]], [BK[bk]])

                    def a_rest(it, qc=qc, cs=cs, nkb=nkb, mi=mi):
                        h, kb, hq, n = it
                        isA = h < 8
                        hh = h if isA else h - 8
                        scale = float(128 ** -0.5) if isA else float(192 ** -0.5)
                        q0 = max(0, kb - 4 * qc) * 128
                        bk = n % 4
                        j = n % 4
                        bo = 4 + (hq % 2) * 2
                        ACT(pT[j][:, q0:512], PB[bk][:, q0:512], AF.Exp, [BK[bk]], [b_pT[j]], scale=scale)
                        if isA:
                            TT("dve" if n % 2 else "pool", pT[j][:, q0:512], pT[j][:, q0:512], mts[mi][:, kb, q0:512], ALU.mult, [b_pT[j], b_mts[mi]], [b_pT[j]])
                            vv = vaS[:, kb, (hh // 4) * 128:(hh // 4 + 1) * 128]
                            b_v = b_vaS
                        else:
                            if kb >= 4 * qc:
                                TT("dve" if n % 2 else "pool", pT[j][:, q0:q0 + 128], pT[j][:, q0:q0 + 128], tri[:], ALU.mult, [b_pT[j], b_tri], [b_pT[j]])
                            vv = mvS[:, kb, hh * 128:(hh + 1) * 128]
                            b_v = b_mvS
                        MM(PB[bo][:, q0:512], vv, pT[j][:, q0:512], kb == 0, kb == nkb - 1, [b_v, b_pT[j]], [BK[bo]])
                        MM(PB[bo + 1][:, q0:512], ones_b[:], pT[j][:, q0:512], kb == 0, kb == nkb - 1, [b_1b, b_pT[j]], [BK[bo + 1]])
                        if kb == nkb - 1:
                            i2 = hq % 2
                            RECIP(rd_[:], PB[bo + 1][:], [BK[bo + 1]], [b_rd])
                            TT("dve", ob[i2][:], PB[bo][:], rd_[:], ALU.mult, [BK[bo], b_rd], [b_ob[i2]])
                            dst = oaT if isA else obT
                            DMA("sp", dst[hh * 128:(hh + 1) * 128, cs], ob[i2][:], [b_ob[i2]], [])

                    DEPTH_PF = 2
                    for k in range(min(DEPTH_PF, len(items))):
                        a_scores(items[k])
                    for k in range(len(items)):
                        if k + DEPTH_PF < len(items):
                            a_scores(items[k + DEPTH_PF])
                        a_rest(items[k])
            P.barrier()

            with ExitStack() as st:
                mT_ = sbuf(st, "mT_", [128, KC, S], BF16); b_mT = Buf()
                with ExitStack() as st1:
                    oaS = sbuf(st1, "oaS", [128, 8, S], BF16); b_oaS = Buf()
                    obS = sbuf(st1, "obS", [128, 8, S], BF16); b_obS = Buf()
                    DMA("sp", oaS[:], oaT.rearrange("(k p) s -> p k s", p=128), [], [b_oaS])
                    DMA("sp", obS[:], obT.rearrange("(k p) s -> p k s", p=128), [], [b_obS])
                    wpa = [sbuf(st1, "wpa%d" % i, [128, 8, 128], BF16) for i in range(2)]; b_wpa = [Buf(), Buf()]
                    wpb = [sbuf(st1, "wpb%d" % i, [128, 8, 128], BF16) for i in range(2)]; b_wpb = [Buf(), Buf()]
                    ga = [sbuf(st1, "ga%d" % i, [128, 512], F32) for i in range(4)]; b_ga = [Buf() for _ in range(4)]
                    gb = [sbuf(st1, "gb%d" % i, [128, 512], F32) for i in range(4)]; b_gb = [Buf() for _ in range(4)]
                    t1 = [sbuf(st1, "p5t%d" % i, [128, 512], F32) for i in range(4)]; b_t1 = [Buf() for _ in range(4)]
                    n = 0
                    for m in range(KC):
                        i = m % 2
                        ms = slice(m * 128, (m + 1) * 128)
                        DMA("pool", wpa[i][:], Wd["w_pa"][l][:, ms].rearrange("(k p) n -> p k n", p=128), [], [b_wpa[i]])
                        DMA("pool", wpb[i][:], Wd["w_pb"][l][:, ms].rearrange("(k p) n -> p k n", p=128), [], [b_wpb[i]])
                        for ch in range(NCH):
                            cs = slice(ch * 512, (ch + 1) * 512)
                            ba = (n % 4) * 2
                            for k in range(8):
                                MM(PB[ba][:], wpa[i][:, k, :], oaS[:, k, cs], k == 0, k == 7, [b_wpa[i], b_oaS], [BK[ba]])
                            for k in range(8):
                                MM(PB[ba + 1][:], wpb[i][:, k, :], obS[:, k, cs], k == 0, k == 7, [b_wpb[i], b_obS], [BK[ba + 1]])
                            j = n % 4
                            DMA("sp", ga[j][:], sgaT[ms, cs], [], [b_ga[j]])
                            DMA("sp", gb[j][:], sgbT[ms, cs], [], [b_gb[j]])
                            TT("dve", t1[j][:], PB[ba][:], ga[j][:], ALU.mult, [BK[ba], b_ga[j]], [b_t1[j]])
                            TT("dve", ga[j][:], PB[ba + 1][:], gb[j][:], ALU.mult, [BK[ba + 1], b_gb[j], b_ga[j]], [b_ga[j]])
                            TT("dve", mT_[:, m, cs], t1[j][:], ga[j][:], ALU.add, [b_t1[j], b_ga[j]], [b_mT])
                            n += 1
                P.barrier()
                with ExitStack() as st1:
                    wo = [sbuf(st1, "wo%d" % i, [128, KC, 128], BF16) for i in range(2)]; b_wo = [Buf(), Buf()]
                    xm = [sbuf(st1, "xm%d" % i, [128, S], F32) for i in range(3)]; b_xm = [Buf(), Buf(), Buf()]
                    n = 0
                    DMA("sp", xm[0][:], xT[0:128, :], [], [b_xm[0]])
                    for m in range(KC):
                        i = m % 2
                        ms = slice(m * 128, (m + 1) * 128)
                        DMA("pool", wo[i][:], Wd["w_o"][l][:, ms].rearrange("(k p) n -> p k n", p=128), [], [b_wo[i]])
                        if m + 1 < KC:
                            DMA("sp", xm[(m + 1) % 3][:], xT[(m + 1) * 128:(m + 2) * 128, :], [], [b_xm[(m + 1) % 3]])
                        i3 = m % 3
                        for ch in range(NCH):
                            cs = slice(ch * 512, (ch + 1) * 512)
                            bk = n % 4
                            for k in range(KC):
                                MM(PB[bk][:], wo[i][:, k, :], mT_[:, k, cs], k == 0, k == KC - 1, [b_wo[i], b_mT], [BK[bk]])
                            STT("dve", xm[i3][:, cs], PB[bk][:], modT[:, l, 32 + m:33 + m], xm[i3][:, cs], ALU.mult, ALU.add, [BK[bk], b_mod, b_xm[i3]], [b_xm[i3]])
                            n += 1
                        DMA("sp", xT[ms, :], xm[i3][:], [b_xm[i3]], [])
            P.barrier()

            with ExitStack() as st:
                hT = sbuf(st, "h2T", [128, KC, S], BF16); b_hT = Buf()
                with ExitStack() as st1:
                    norm_phase(st1, hT, b_hT, AB[:, l, 16:32], modT[:, l, 48:64])
                P.barrier()
                wcT = sbuf(st, "wcT", [128, 3, 88], F32); b_wc = Buf()
                bcT = sbuf(st, "bcT", [128, 88], F32); b_bc = Buf()
                for t in range(3):
                    rows_to_cols(st, Wd["w_conv"][l][t].rearrange("(k p) -> k p", p=128), 88, wcT[:, t, :], b_wc, 7, "wc%d" % t)
                rows_to_cols(st, Wd["b_conv"][l].rearrange("(k p) -> k p", p=128), 88, bcT[:], b_bc, 7, "bc")
                wu = [sbuf(st, "wu%d" % i, [128, KC, 256], BF16) for i in range(2)]; b_wu = [Buf(), Buf()]
                ug = [sbuf(st, "ug%d" % i, [128, S + 2], F32) for i in range(2)]; b_ug = [Buf(), Buf()]
                uv = [sbuf(st, "uv%d" % i, [128, S + 2], F32) for i in range(2)]; b_uv = [Buf(), Buf()]
                cg = [sbuf(st, "cg%d" % i, [128, S], F32) for i in range(2)]; b_cg = [Buf(), Buf()]
                cv = [sbuf(st, "cv%d" % i, [128, S], F32) for i in range(2)]; b_cv = [Buf(), Buf()]
                ab = [sbuf(st, "ab%d" % i, [128, S], BF16) for i in range(2)]; b_ab = [Buf(), Buf()]
                for i in range(2):
                    MS("pool", ug[i][:, 0:2], 0.0, [b_ug[i]])
                    MS("pool", uv[i][:, 0:2], 0.0, [b_uv[i]])
                n = 0
                for j in range(44):
                    i = j % 2
                    DMA("pool", wu[i][:, :, 0:128], Wd["w_up"][l][:, j * 128:(j + 1) * 128].rearrange("(k p) n -> p k n", p=128), [], [b_wu[i]])
                    DMA("pool", wu[i][:, :, 128:256], Wd["w_up"][l][:, DFF + j * 128:DFF + (j + 1) * 128].rearrange("(k p) n -> p k n", p=128), [], [b_wu[i]])
                    for ch in range(NCH):
                        cs2 = slice(2 + ch * 512, 2 + (ch + 1) * 512)
                        cs = slice(ch * 512, (ch + 1) * 512)
                        bg = (n % 4) * 2
                        for k in range(KC):
                            MM(PB[bg][:], wu[i][:, k, 0:128], hT[:, k, cs], k == 0, k == KC - 1, [b_wu[i], b_hT], [BK[bg]])
                        for k in range(KC):
                            MM(PB[bg + 1][:], wu[i][:, k, 128:256], hT[:, k, cs], k == 0, k == KC - 1, [b_wu[i], b_hT], [BK[bg + 1]])
                        CP("act", ug[i][:, cs2], PB[bg][:], [BK[bg]], [b_ug[i]])
                        CP("dve", uv[i][:, cs2], PB[bg + 1][:], [BK[bg + 1]], [b_uv[i]])
                        n += 1
                    for (u_, c_, b_u, b_c, col, e1, e2) in ((ug[i], cg[i], b_ug[i], b_cg[i], j, "pool", "dve"), (uv[i], cv[i], b_uv[i], b_cv[i], 44 + j, "dve", "pool")):
                        ACT(c_[:], u_[:, 2:S + 2], AF.Identity, [b_u, b_wc, b_bc], [b_c], scale=wcT[:, 2, col:col + 1], bias=bcT[:, col:col + 1])
                        STT(e1, c_[:], u_[:, 1:S + 1], wcT[:, 1, col:col + 1], c_[:], ALU.mult, ALU.add, [b_u, b_wc, b_c], [b_c])
                        STT(e2, c_[:], u_[:, 0:S], wcT[:, 0, col:col + 1], c_[:], ALU.mult, ALU.add, [b_u, b_wc, b_c], [b_c])
                    ACT(cg[i][:], cg[i][:], AF.Silu, [b_cg[i]], [b_cg[i]])
                    TT("dve", ab[i][:], cg[i][:], cv[i][:], ALU.mult, [b_cg[i], b_cv[i]], [b_ab[i]])
                    DMA("sp", actT[j * 128:(j + 1) * 128, :], ab[i][:], [b_ab[i]], [])
            P.barrier()

            with ExitStack() as st:
                TC = min(1024, S)
                aS = sbuf(st, "aS", [128, 44, TC], BF16); b_aS = Buf()
                wd = [sbuf(st, "wd%d" % i, [128, 44, 128], BF16) for i in range(2)]; b_wd = [Buf(), Buf()]
                xm = [sbuf(st, "xd%d" % i, [128, TC], F32) for i in range(3)]; b_xm = [Buf(), Buf(), Buf()]
                its = [(c0, m) for c0 in range(0, S, TC) for m in range(KC)]

                def p8_load(n_):
                    if n_ < len(its):
                        c0_, m_ = its[n_]
                        DMA("sp", xm[n_ % 3][:], xT[m_ * 128:(m_ + 1) * 128, c0_:c0_ + TC], [], [b_xm[n_ % 3]])
                p8_load(0)
                for n, (c0, m) in enumerate(its):
                    if m == 0:
                        DMA("sp", aS[:], actT[:, c0:c0 + TC].rearrange("(k p) s -> p k s", p=128), [], [b_aS])
                    i = n % 2
                    i3 = n % 3
                    ms = slice(m * 128, (m + 1) * 128)
                    DMA("pool", wd[i][:], Wd["w_down"][l][:, ms].rearrange("(k p) n -> p k n", p=128), [], [b_wd[i]])
                    p8_load(n + 1)
                    for s0 in range(0, TC, 512):
                        bk = (n * 2 + s0 // 512) % 8
                        for k in range(44):
                            MM(PB[bk][:], wd[i][:, k, :], aS[:, k, s0:s0 + 512], k == 0, k == 43, [b_wd[i], b_aS], [BK[bk]])
                        STT("dve", xm[i3][:, s0:s0 + 512], PB[bk][:], modT[:, l, 80 + m:81 + m], xm[i3][:, s0:s0 + 512], ALU.mult, ALU.add, [BK[bk], b_mod, b_xm[i3]], [b_xm[i3]])
                    DMA("sp", xT[ms, c0:c0 + TC], xm[i3][:], [b_xm[i3]], [])
            P.barrier()

        with ExitStack() as st:
            xc = [sbuf(st, "f_xc%d" % i, [128, KC, 128], F32) for i in range(2)]; b_xc = [Buf(), Buf()]
            ot = [sbuf(st, "f_ot%d" % i, [128, D], F32) for i in range(2)]; b_ot = [Buf(), Buf()]
            for tt in range(NT):
                i = tt % 2
                DMA("sp", xc[i][:], xTv[:, :, tt * 128:(tt + 1) * 128], [], [b_xc[i]])
                for g4 in range(4):
                    bk = (tt * 4 + g4) % 8
                    for j in range(4):
                        kc = g4 * 4 + j
                        TR(PB[bk][:, j * 128:(j + 1) * 128], xc[i][:, kc, :], identf[:], [b_xc[i], b_idf], [BK[bk]])
                    CP("act" if g4 % 2 else "dve", ot[i][:, g4 * 512:(g4 + 1) * 512], PB[bk][:], [BK[bk]], [b_ot[i]])
                DMA("sp", out[tt * 128:(tt + 1) * 128, :], ot[i][:], [b_ot[i]], [])
        for k, ap in dbg_out.items():
            src = {"xT": xT, "qaT": qaT, "kaT": kaT, "va": va, "qiT": qiT, "kiT2": kiT2, "wis": wis, "mkr": mkr, "mqlT": mqlT,
                   "mkvlT": mkvlT, "sgaT": sgaT, "sgbT": sgbT, "mqTn": mqTn, "mqTr": mqTr, "mkTn": mkTn, "mkTr": mkTr, "mv": mv,
                   "MT": MT, "oaT": oaT, "obT": obT, "actT": actT}[k]
            P.dma("sp", lambda e, ap=ap, src=src: e.dma_start(out=ap, in_=src), (), ())
        P.barrier()
        with nc.Block() as block:
            P.emit(block)
    return nc


_NC_CACHE = {}


def kernel(**inputs):
    S = inputs["x"].shape[1]
    B = inputs["x"].shape[0]
    L = inputs["w_in"].shape[0]
    n_sel = min(256, S // 4)
    key = (S, L, n_sel)
    if key not in _NC_CACHE:
        _NC_CACHE[key] = build(S, L, n_sel)
    nc = _NC_CACHE[key]
    invf = inv_freq_table()
    shared = {k: np.ascontiguousarray(np.asarray(inputs[k], dtype=np.float32)) for k in W_SHAPES}
    in_maps = []
    for b in range(B):
        m = dict(shared)
        m["x"] = np.ascontiguousarray(np.asarray(inputs["x"][b], dtype=np.float32))
        m["c"] = np.ascontiguousarray(np.asarray(inputs["c"][b], dtype=np.float32))
        m["positions"] = np.ascontiguousarray(np.asarray(inputs["positions"][b], dtype=np.int32))
        m["invf"] = invf
        in_maps.append(m)
    res = run_bass_kernel_spmd(nc, in_maps, core_ids=list(range(B)))
    return np.stack([np.asarray(r["out"], dtype=np.float32) for r in res.results], axis=0)
```

```python
from contextlib import ExitStack
import numpy as np
import concourse.bass as bass
import concourse.mybir as mybir
from concourse.bass_utils import run_bass_kernel_spmd

F32 = mybir.dt.float32
BF16 = mybir.dt.bfloat16
I32 = mybir.dt.int32
AF = mybir.ActivationFunctionType
ALU = mybir.AluOpType

ENGS = ("pe", "act", "dve", "pool", "sp")
NRING = 12

D = 2048
KC = 16
IN_DIM = 7568
DFF = 5632
EPS = 1e-6
PI = float(np.pi)
TWO_PI = float(2 * np.pi)


class Buf:
    __slots__ = ("w", "r", "excl")

    def __init__(self, excl=False):
        self.w = None
        self.r = {}
        self.excl = excl


class Prog:
    def __init__(self, nc, stack):
        self.nc = nc
        self.q = {e: [] for e in ENGS}
        self.cnt = {e: 0 for e in ENGS}
        self.dcnt = {e: 0 for e in ENGS}
        self.seen = {e: {} for e in ENGS}
        self.sem = {}
        for e in ENGS:
            self.sem[("c", e)] = stack.enter_context(nc.semaphore("c_" + e))
        for e in ("sp", "pool", "act"):
            for r in range(NRING):
                self.sem[("d", e, r)] = stack.enter_context(nc.semaphore("d_%s_%d" % (e, r)))

    def _need(self, eng, waits, tok):
        if tok is None:
            return
        key, val = tok
        if eng == "pe" and key == ("c", "pe"):
            return
        if self.seen[eng].get(key, 0) >= val:
            return
        self.seen[eng][key] = val
        waits.append((key, val))

    def _deps(self, eng, reads, writes):
        waits = []
        own = ("c", eng)
        for b in reads:
            self._need(eng, waits, b.w)
            if b.excl:
                for k, v in b.r.items():
                    if k != own:
                        self._need(eng, waits, (k, v))
        for b in writes:
            self._need(eng, waits, b.w)
            for k, v in b.r.items():
                self._need(eng, waits, (k, v))
        return waits

    def _mark(self, tok, reads, writes):
        k, v = tok
        for b in reads:
            if b.r.get(k, 0) < v:
                b.r[k] = v
        for b in writes:
            b.w = tok
            b.r = {}

    def op(self, eng, fn, reads=(), writes=()):
        waits = self._deps(eng, reads, writes)
        self.cnt[eng] += 1
        tok = (("c", eng), self.cnt[eng])
        self.q[eng].append((fn, waits, tok[0], 1))
        self._mark(tok, reads, writes)

    def dma(self, eng, fn, reads=(), writes=()):
        waits = self._deps(eng, reads, writes)
        j = self.dcnt[eng]
        self.dcnt[eng] += 1
        slot = j % NRING
        use = j // NRING
        key = ("d", eng, slot)
        if use > 0:
            self._need(eng, waits, (key, 16 * use))
        tok = (key, 16 * (use + 1))
        self.q[eng].append((fn, waits, key, 16))
        self._mark(tok, reads, writes)

    def wait_all(self, eng):
        waits = []
        for e in ENGS:
            if self.cnt[e] > 0:
                self._need(eng, waits, (("c", e), self.cnt[e]))
        for e in ("sp", "pool", "act"):
            j = self.dcnt[e]
            for slot in range(NRING):
                if j > slot:
                    uses = (j - slot + NRING - 1) // NRING
                    self._need(eng, waits, (("d", e, slot), 16 * uses))
        self.q[eng].append((None, waits, None, 0))

    def barrier(self):
        for e in ENGS:
            self.wait_all(e)

    def _replay(self, name, e):
        sem = self.sem
        for fn, waits, key, inc in self.q[name]:
            for k, v in waits:
                e.wait_ge(sem[k], v)
            if fn is not None:
                fn(e).then_inc(sem[key], inc)

    def emit(self, block):
        P = self

        @block.tensor
        def _(e):
            P._replay("pe", e)

        @block.scalar
        def _(e):
            P._replay("act", e)

        @block.vector
        def _(e):
            P._replay("dve", e)

        @block.gpsimd
        def _(e):
            P._replay("pool", e)

        @block.sync
        def _(e):
            P._replay("sp", e)


W_SHAPES = {
    "g_attn": (D,), "g_ffn": (D,), "w_ada": (D, 6 * D), "b_ada": (6 * D,), "w_in": (D, IN_DIM),
    "g_qa": (128,), "g_ka": (128,), "g_mq_lat": (512,), "w_mq_up": (512, 1536), "g_mkv_lat": (256,),
    "w_mkv_up": (256, 2048), "g_qm": (192,), "g_km": (192,), "w_pa": (1024, D), "w_pb": (1024, D),
    "w_o": (D, D), "w_up": (D, 2 * DFF), "w_conv": (3, 2 * DFF), "b_conv": (2 * DFF,), "w_down": (DFF, D),
}


def inv_freq_table():
    def inv(rot):
        return (500000.0 ** (-np.arange(0, rot, 2, dtype=np.float32) / np.float32(rot))).astype(np.float32)
    t = np.concatenate([inv(32), inv(16), inv(64)]).astype(np.float32)
    return np.ascontiguousarray(np.broadcast_to(t[None, :], (128, 56))).astype(np.float32)


def build(S, L, n_sel, dbg=()):
    NT = S // 128
    NCH = S // 512
    nc = bass.Bass("TRN2", target_bir_lowering=False)

    def din(name, shape, dt=F32):
        return nc.dram_tensor(name, list(shape), dt, kind="ExternalInput").ap()

    def dscr(name, shape, dt):
        return nc.dram_tensor(name, list(shape), dt, kind="Internal").ap()

    x_in = din("x", (S, D))
    c_in = din("c", (D,))
    pos_in = din("positions", (S,), I32)
    invf_in = din("invf", (128, 56))
    Wd = {k: din(k, (L,) + v) for k, v in W_SHAPES.items()}
    out = nc.dram_tensor("out", [S, D], F32, kind="ExternalOutput").ap()
    dbg_out = {k: nc.dram_tensor("dbg_" + k, list(shp), dt, kind="ExternalOutput").ap() for k, (shp, dt) in dbg}

    xT = dscr("xT", (D, S), F32)
    qaT = dscr("qaT", (1024, S), BF16)
    kaT = dscr("kaT", (256, S), BF16)
    va = dscr("va", (S, 256), BF16)
    qiT = dscr("qiT", (1024, S), BF16)
    kiT2 = dscr("kiT2", (128, S), BF16)
    wis = dscr("wis", (S, 16), F32)
    mkr = dscr("mkr", (S, 64), F32)
    mqlT = dscr("mqlT", (512, S), BF16)
    mkvlT = dscr("mkvlT", (256, S), BF16)
    sgaT = dscr("sgaT", (D, S), F32)
    sgbT = dscr("sgbT", (D, S), F32)
    mqTn = dscr("mqTn", (1024, S), BF16)
    mqTr = dscr("mqTr", (512, S), BF16)
    mkTn = dscr("mkTn", (1024, S), BF16)
    mkTr = dscr("mkTr", (512, S), BF16)
    mv = dscr("mv", (S, 1024), BF16)
    MT = dscr("MT", (S, S), BF16)
    oaT = dscr("oaT", (1024, S), BF16)
    obT = dscr("obT", (1024, S), BF16)
    actT = dscr("actT", (DFF, S), BF16)

    with ExitStack() as st0:
        P = Prog(nc, st0)

        def MM(o, lhsT, rhs, start, stop, rd, wr):
            P.op("pe", lambda e: e.matmul(o, lhsT=lhsT, rhs=rhs, start=start, stop=stop), rd, wr)

        def TR(o, in_, ident, rd, wr):
            P.op("pe", lambda e: e.transpose(out=o, in_=in_, identity=ident), rd, wr)

        def ACT(o, in_, func, rd, wr, **kw):
            P.op("act", lambda e: e.activation(out=o, in_=in_, func=func, **kw), rd, wr)

        def TS(eng, o, in0, s1, s2, op0, op1, rd, wr):
            if s2 is None:
                P.op(eng, lambda e: e.tensor_scalar(out=o, in0=in0, scalar1=s1, scalar2=None, op0=op0), rd, wr)
            else:
                P.op(eng, lambda e: e.tensor_scalar(out=o, in0=in0, scalar1=s1, scalar2=s2, op0=op0, op1=op1), rd, wr)

        def TT(eng, o, in0, in1, op, rd, wr):
            P.op(eng, lambda e: e.tensor_tensor(out=o, in0=in0, in1=in1, op=op), rd, wr)

        def STT(eng, o, in0, scalar, in1, op0, op1, rd, wr):
            eng = "dve"
            P.op(eng, lambda e: e.scalar_tensor_tensor(out=o, in0=in0, scalar=scalar, in1=in1, op0=op0, op1=op1), rd, wr)

        def CP(eng, o, in_, rd, wr):
            if eng == "act":
                P.op("act", lambda e: e.copy(out=o, in_=in_), rd, wr)
            else:
                P.op(eng, lambda e: e.tensor_copy(out=o, in_=in_), rd, wr)

        def MS(eng, o, val, wr):
            P.op(eng, lambda e: e.memset(o, val), (), wr)

        def DMA(eng, o, in_, rd, wr):
            P.dma(eng, lambda e: e.dma_start(out=o, in_=in_), rd, wr)

        def MAX8(o, in_, rd, wr):
            P.op("dve", lambda e: e.max(out=o, in_=in_), rd, wr)

        def MREP(o, rep, vals, rd, wr):
            P.op("dve", lambda e: e.match_replace(out=o, in_to_replace=rep, in_values=vals, imm_value=-1e30), rd, wr)

        def RECIP(o, in_, rd, wr):
            P.op("dve", lambda e: e.reciprocal(out=o, in_=in_), rd, wr)

        def RSTD(t, inv_n, rd_wr):
            TS("dve", t, t, inv_n, EPS, ALU.mult, ALU.add, rd_wr, rd_wr)
            ACT(t, t, AF.Sqrt, rd_wr, rd_wr)
            P.op("dve", lambda e: e.reciprocal(out=t, in_=t), rd_wr, rd_wr)

        def skewed_steps(units):
            ns = max(len(x) for x in units)
            steps = []
            for t in range(len(units) + ns - 1):
                def step(t=t):
                    for k in reversed(range(ns)):
                        u = t - k
                        if 0 <= u < len(units) and k < len(units[u]):
                            units[u][k]()
                steps.append(step)
            return steps

        def interleave(A, B):
            tot = sum(w for _, w in A) or 1.0
            j = 0
            acc = 0.0
            for fn, w in A:
                fn()
                acc += w
                target = int(round(acc / tot * len(B)))
                while j < target:
                    B[j]()
                    j += 1
            while j < len(B):
                B[j]()
                j += 1

        def run_skewed(units):
            ns = max(len(x) for x in units)
            for t in range(len(units) + ns - 1):
                for k in reversed(range(ns)):
                    u = t - k
                    if 0 <= u < len(units) and k < len(units[u]):
                        units[u][k]()

        PB = [st0.enter_context(nc.psum_tensor("pb%d" % i, [128, 512], F32)) for i in range(8)]
        PBb = [b[:].bitcast(BF16) for b in PB]
        BK = [Buf(excl=True) for _ in range(8)]

        uid = [0]

        def sbuf(st, name, shape, dt):
            uid[0] += 1
            return st.enter_context(nc.sbuf_tensor("%s_u%d" % (name, uid[0]), list(shape), dt))

        identf = sbuf(st0, "identf", [128, 128], F32); b_idf = Buf()
        identb = sbuf(st0, "identb", [128, 128], BF16); b_idb = Buf()
        ones_f = sbuf(st0, "ones_f", [128, 128], F32); b_1f = Buf()
        ones_b = sbuf(st0, "ones_b", [128, 128], BF16); b_1b = Buf()
        tri = sbuf(st0, "tri", [128, 128], BF16); b_tri = Buf()
        trif = sbuf(st0, "trif", [128, 128], F32); b_trif = Buf()
        negm = sbuf(st0, "negm", [128, 128], F32); b_negm = Buf()
        tab = sbuf(st0, "tab", [128, NT, 112], F32); b_tab = Buf()
        modT = sbuf(st0, "modT", [128, L, 96], F32); b_mod = Buf()
        gT = sbuf(st0, "gT", [128, L, 32], F32); b_gT = Buf()
        AB = sbuf(st0, "AB", [128, L, 32], F32); b_AB = Buf()
        eps_t = sbuf(st0, "eps_t", [128, 1], F32); b_eps = Buf()
        cTb = sbuf(st0, "cTb", [128, 16], BF16); b_cTb = Buf()
        CONST = [b_idf, b_idb, b_1f, b_1b, b_tri, b_negm, b_tab, b_mod, b_AB]

        MS("pool", eps_t[:], EPS, [b_eps])
        MS("pool", identf[:], 1.0, [b_idf])
        P.op("pool", lambda e: e.affine_select(out=identf[:], in_=identf[:], pattern=[[-1, 128]], compare_op=ALU.is_equal, fill=0.0, base=0, channel_multiplier=1), [b_idf], [b_idf])
        CP("dve", identb[:], identf[:], [b_idf], [b_idb])
        MS("pool", ones_f[:], 1.0, [b_1f])
        MS("pool", ones_b[:], 1.0, [b_1b])
        MS("pool", trif[:], 1.0, [b_trif])
        P.op("pool", lambda e: e.affine_select(out=trif[:], in_=trif[:], pattern=[[1, 128]], compare_op=ALU.is_ge, fill=0.0, base=0, channel_multiplier=-1), [b_trif], [b_trif])
        CP("dve", tri[:], trif[:], [b_trif], [b_tri])
        MS("pool", negm[:], 0.0, [b_negm])
        P.op("pool", lambda e: e.affine_select(out=negm[:], in_=negm[:], pattern=[[-1, 128]], compare_op=ALU.is_ge, fill=-1e30, base=0, channel_multiplier=1), [b_negm], [b_negm])

        def rows_to_cols(st, src_rows_ap, nrows, dst, dst_buf, bank, nm):
            rt = sbuf(st, "rt_" + nm, [128, 128], F32); b_rt = Buf()
            DMA("sp", rt[0:nrows, :], src_rows_ap, [], [b_rt])
            TR(PB[bank][:, 0:nrows], rt[0:nrows, :], identf[0:nrows, 0:nrows], [b_rt, b_idf], [BK[bank]])
            CP("dve", dst, PB[bank][:, 0:nrows], [BK[bank]], [dst_buf])

        def mod_items(st, l2, bank_acc, bank_rc):
            wsl = [sbuf(st, "adasl%d" % i, [128, KC, 512], BF16) for i in range(2)]; b_wsl = [Buf(), Buf()]
            badaT = sbuf(st, "badaT", [128, 96], F32); b_bada = Buf()
            rtb = sbuf(st, "rt_bada", [128, 128], F32); b_rtb = Buf()
            items = []

            def slab_dma(sl):
                if sl < 24:
                    DMA("pool", wsl[sl % 2][:], Wd["w_ada"][l2][:, sl * 512:(sl + 1) * 512].rearrange("(kc p) n -> p kc n", p=128), [], [b_wsl[sl % 2]])

            def mk(sl):
                def f():
                    if sl == 0:
                        slab_dma(0)
                    slab_dma(sl + 1)
                    i = sl % 2
                    for jj in range(4):
                        j = sl * 4 + jj
                        for kc in range(KC):
                            MM(PB[bank_acc][:, j:j + 1], wsl[i][:, kc, jj * 128:(jj + 1) * 128], cTb[:, kc:kc + 1], kc == 0, kc == KC - 1, [b_wsl[i], b_cTb], [BK[bank_acc]])
                return f
            for sl in range(24):
                items.append(mk(sl))

            def fin():
                DMA("sp", rtb[0:96, :], Wd["b_ada"][l2].rearrange("(k p) -> k p", p=128), [], [b_rtb])
                TR(PB[bank_rc][:, 0:96], rtb[0:96, :], identf[0:96, 0:96], [b_rtb, b_idf], [BK[bank_rc]])
                CP("act", badaT[:], PB[bank_rc][:, 0:96], [BK[bank_rc]], [b_bada])
                TT("dve", modT[:, l2, :], PB[bank_acc][:, 0:96], badaT[:], ALU.add, [BK[bank_acc], b_bada], [b_mod])
                STT("dve", AB[:, l2, 0:16], modT[:, l2, 16:32], 1.0, gT[:, l2, 0:16], ALU.add, ALU.mult, [b_mod, b_gT], [b_AB])
                STT("dve", AB[:, l2, 16:32], modT[:, l2, 64:80], 1.0, gT[:, l2, 16:32], ALU.add, ALU.mult, [b_mod, b_gT], [b_AB])
            items.append(fin)
            return items

        with ExitStack() as st:
            xt = [sbuf(st, "xt%d" % i, [128, D], F32) for i in range(2)]; b_xt = [Buf(), Buf()]
            xs = [sbuf(st, "xs%d" % i, [128, KC, 128], F32) for i in range(2)]; b_xs = [Buf(), Buf()]
            xTv = xT.rearrange("(kc p) s -> p kc s", p=128)
            for tt in range(NT):
                i = tt % 2
                DMA("sp", xt[i][:], x_in[tt * 128:(tt + 1) * 128, :], [], [b_xt[i]])
                for g4 in range(4):
                    bk = (tt * 4 + g4) % 8
                    for j in range(4):
                        kc = g4 * 4 + j
                        TR(PB[bk][:, j * 128:(j + 1) * 128], xt[i][:, kc * 128:(kc + 1) * 128], identf[:], [b_xt[i], b_idf], [BK[bk]])
                    CP("act" if g4 % 2 else "dve", xs[i][:, g4 * 4:(g4 + 1) * 4, :], PB[bk][:].rearrange("p (j t) -> p j t", j=4), [BK[bk]], [b_xs[i]])
                DMA("sp", xTv[:, :, tt * 128:(tt + 1) * 128], xs[i][:], [b_xs[i]], [])

            posr = sbuf(st, "posr", [16, 128], I32); b_posr = Buf()
            posf = sbuf(st, "posf", [16, 128], F32); b_posf = Buf()
            posT = sbuf(st, "posT", [128, 16], F32); b_posT = Buf()
            invf = sbuf(st, "invf", [128, 56], F32); b_invf = Buf()
            kf = sbuf(st, "kf", [128, NT, 56], F32); b_kf = Buf()
            ki_ = sbuf(st, "ki_", [128, NT, 56], I32); b_ki = Buf()
            DMA("sp", posr[0:NT, :], pos_in.rearrange("(t p) -> t p", p=128), [], [b_posr])
            DMA("sp", invf[:], invf_in[:, :], [], [b_invf])
            CP("dve", posf[0:NT, :], posr[0:NT, :], [b_posr], [b_posf])
            TR(PB[0][:, 0:NT], posf[0:NT, :], identf[0:NT, 0:NT], [b_posf, b_idf], [BK[0]])
            CP("dve", posT[:, 0:NT], PB[0][:, 0:NT], [BK[0]], [b_posT])
            sinv = tab[:, :, 0:56]
            cosv = tab[:, :, 56:112]
            for tt in range(NT):
                TS("dve", tab[:, tt, 0:56], invf[:], posT[:, tt:tt + 1], None, ALU.mult, None, [b_invf, b_posT], [b_tab])
            C1 = 6.28125
            C2 = float(2 * np.pi - 6.28125)
            RW = [b_tab, b_kf, b_ki]
            TS("dve", kf[:], sinv, float(1.0 / (2 * np.pi)), None, ALU.mult, None, RW, RW)
            CP("dve", ki_[:], kf[:], RW, RW)
            CP("dve", kf[:], ki_[:], RW, RW)
            STT("dve", sinv, kf[:], -C1, sinv, ALU.mult, ALU.add, RW, RW)
            STT("dve", sinv, kf[:], -C2, sinv, ALU.mult, ALU.add, RW, RW)
            TS("dve", kf[:], sinv, PI, -TWO_PI, ALU.is_gt, ALU.mult, RW, RW)
            TT("dve", sinv, sinv, kf[:], ALU.add, RW, RW)
            TS("dve", kf[:], sinv, -PI, TWO_PI, ALU.is_lt, ALU.mult, RW, RW)
            TT("dve", sinv, sinv, kf[:], ALU.add, RW, RW)
            TS("dve", cosv, sinv, PI / 2, None, ALU.add, None, RW, RW)
            TS("dve", kf[:], cosv, PI, -TWO_PI, ALU.is_gt, ALU.mult, RW, RW)
            TT("dve", cosv, cosv, kf[:], ALU.add, RW, RW)
            ACT(tab[:], tab[:], AF.Sin, RW, RW)

            cT = sbuf(st, "cT", [128, 16], F32); b_cT = Buf()
            rows_to_cols(st, c_in.rearrange("(k p) -> k p", p=128), 16, cT[:], b_cT, 1, "c")
            ACT(cT[:], cT[:], AF.Silu, [b_cT], [b_cT])
            CP("dve", cTb[:], cT[:], [b_cT], [b_cTb])
            for l in range(L):
                rows_to_cols(st, Wd["g_attn"][l].rearrange("(k p) -> k p", p=128), 16, gT[:, l, 0:16], b_gT, 2, "ga%d" % l)
                rows_to_cols(st, Wd["g_ffn"][l].rearrange("(k p) -> k p", p=128), 16, gT[:, l, 16:32], b_gT, 3, "gf%d" % l)
            for it_ in mod_items(st, 0, 5, 4):
                it_()
        P.barrier()

        xTv = xT.rearrange("(kc p) s -> p kc s", p=128)

        def norm_phase(st, hT, b_hT, A_ap, B_ap):
            xc = [sbuf(st, "n_xc%d" % i, [128, KC, 512], F32) for i in range(2)]; b_xc = [Buf(), Buf()]
            sq = [sbuf(st, "n_sq%d" % i, [128, 512], F32) for i in range(2)]; b_sq = [Buf(), Buf()]
            rs = sbuf(st, "n_rs", [128, 512], F32); b_rs = Buf()
            tmp = [sbuf(st, "n_tmp%d" % i, [128, 512], F32) for i in range(2)]; b_tmp = [Buf(), Buf()]
            for ch in range(NCH):
                i = ch % 2
                cs = slice(ch * 512, (ch + 1) * 512)
                DMA("sp", xc[i][:], xTv[:, :, cs], [], [b_xc[i]])
                bk = ch % 2
                for kc in range(KC):
                    j = kc % 2
                    ACT(sq[j][:], xc[i][:, kc, :], AF.Square, [b_xc[i]], [b_sq[j]])
                    MM(PB[bk][:], ones_f[:], sq[j][:], kc == 0, kc == KC - 1, [b_1f, b_sq[j]], [BK[bk]])
                CP("dve", rs[:], PB[bk][:], [BK[bk]], [b_rs])
                RSTD(rs[:], 1.0 / D, [b_rs])
                for kc in range(KC):
                    j = kc % 2
                    TT("dve" if kc % 2 else "pool", tmp[j][:], xc[i][:, kc, :], rs[:], ALU.mult, [b_xc[i], b_rs], [b_tmp[j]])
                    ACT(hT[:, kc, cs], tmp[j][:], AF.Identity, [b_tmp[j], b_AB, b_mod], [b_hT], scale=A_ap[:, kc:kc + 1], bias=B_ap[:, kc:kc + 1])

        def bcast_load(st, name, src_row_ap, n):
            t = sbuf(st, name, [128, n], F32)
            b = Buf()
            DMA("sp", t[:], src_row_ap.to_broadcast([128, n]), [], [b])
            return t, b

        def rope(st_tiles, src3, dst3, nh, half, cos_ap, sin_ap, rd, wr, eng="pool"):
            ta, tb_, b_t = st_tiles
            cosb = cos_ap.unsqueeze(1).to_broadcast([128, nh, half])
            sinb = sin_ap.unsqueeze(1).to_broadcast([128, nh, half])
            t1 = src3[:, :, 0:half]
            t2 = src3[:, :, half:2 * half]
            av = ta[:, 0:nh * half].rearrange("p (h d) -> p h d", h=nh)
            bv = tb_[:, 0:nh * half].rearrange("p (h d) -> p h d", h=nh)
            TT(eng, av, t1, cosb, ALU.mult, rd + [b_tab], [b_t])
            TT(eng, bv, t2, sinb, ALU.mult, rd + [b_tab], [b_t])
            TT(eng, dst3[:, :, 0:half], av, bv, ALU.subtract, [b_t], wr)
            TT(eng, av, t2, cosb, ALU.mult, rd + [b_tab], [b_t] + wr)
            TT(eng, bv, t1, sinb, ALU.mult, rd + [b_tab], [b_t])
            TT(eng, dst3[:, :, half:2 * half], av, bv, ALU.add, [b_t], wr)

        def make_p3(st, qt_list, banks):
            kiS = sbuf(st, "kiS", [128, S], BF16); b_kiS = Buf()
            qit = [sbuf(st, "qit%d" % i, [128, 8, 128], BF16) for i in range(2)]; b_qit = [Buf(), Buf()]
            sc = [sbuf(st, "sc%d" % i, [128, S], F32) for i in range(2)]; b_sc = [Buf(), Buf()]
            wk = sbuf(st, "wk", [128, S], F32); b_wk = Buf()
            rl = [sbuf(st, "rl%d" % i, [128, 512], BF16) for i in range(3)]; b_rl = [Buf(), Buf(), Buf()]
            wv = [sbuf(st, "wv%d" % i, [128, 16], F32) for i in range(2)]; b_wv = [Buf(), Buf()]
            dg = [sbuf(st, "dg%d" % i, [128, 16, 128], BF16) for i in range(2)]; b_dg = [Buf(), Buf()]
            mx = sbuf(st, "mx", [128, 8], F32); b_mx = Buf()
            Mb = [sbuf(st, "Mb%d" % i, [128, S], BF16) for i in range(2)]; b_Mb = [Buf(), Buf()]
            mstg = [sbuf(st, "mstg%d" % i, [128, 4, 128], BF16) for i in range(2)]; b_mstg = [Buf(), Buf()]
            qiTv = qiT.rearrange("(j p) s -> p j s", p=128)
            cnt = {"s": 0, "a": 0, "t": 0, "r": 0}
            items = []

            def first():
                DMA("sp", kiS[:], kiT2[:, :], [], [b_kiS])
            items.append((first, 0.1))

            def tile_items(qt, i):
                W = (qt + 1) * 128
                qs = slice(qt * 128, (qt + 1) * 128)
                out = []

                def prep():
                    DMA("sp", wv[i][:], wis[qs, :], [], [b_wv[i]])
                    DMA("sp", qit[i][:], qiTv[:, :, qs], [], [b_qit[i]])
                    for h in range(16):
                        ACT(dg[i][:, h, :], identb[:], AF.Copy, [b_idb, b_wv[i]], [b_dg[i]], scale=wv[i][:, h:h + 1])
                out.append((prep, 0.5))

                def chunk(k0, kw):
                    def f():
                        ab = banks["acc"][cnt["a"] % len(banks["acc"])]
                        cnt["a"] += 1
                        sbs = []

                        def dots(h):
                            sb = banks["score"][cnt["s"] % len(banks["score"])]
                            cnt["s"] += 1
                            pr = slice((h % 2) * 64, (h % 2) * 64 + 64)
                            MM(PB[sb][:, 0:kw], qit[i][pr, h // 2, :], kiS[pr, k0:k0 + kw], True, True, [b_qit[i], b_kiS], [BK[sb]])
                            sbs.append(sb)
                        dots(0)
                        for h in range(16):
                            if h + 1 < 16:
                                dots(h + 1)
                            r = cnt["r"] % 3
                            cnt["r"] += 1
                            ACT(rl[r][:, 0:kw], PB[sbs[h]][:, 0:kw], AF.Relu, [BK[sbs[h]]], [b_rl[r]])
                            MM(PB[ab][:, 0:kw], dg[i][:, h, :], rl[r][:, 0:kw], h == 0, h == 15, [b_dg[i], b_rl[r]], [BK[ab]])
                        CP("act", sc[i][:, k0:k0 + kw], PB[ab][:, 0:kw], [BK[ab]], [b_sc[i]])
                    return f
                for k0 in range(0, W, 512):
                    out.append((chunk(k0, min(512, W - k0)), 2.0 * min(512, W - k0) / 512))

                def causal():
                    TT("dve", sc[i][:, qt * 128:W], sc[i][:, qt * 128:W], negm[:], ALU.add, [b_sc[i], b_negm], [b_sc[i]])
                out.append((causal, 0.2))
                if W > n_sel:
                    rounds = n_sel // 8

                    def topk(r0, r1):
                        def f():
                            for r in range(r0, r1):
                                src = sc[i] if r == 0 else wk
                                MAX8(mx[:], src[:, 0:W], [b_sc[i], b_wk], [b_mx])
                                if r < rounds - 1:
                                    MREP(wk[:, 0:W], mx[:], src[:, 0:W], [b_sc[i], b_mx, b_wk], [b_wk])
                        return f
                    for r0 in range(0, rounds, 4):
                        out.append((topk(r0, min(rounds, r0 + 4)), (min(rounds, r0 + 4) - r0) * 2.0 * W / 1000.0))

                    def mask():
                        TS("dve", Mb[i][:, 0:W], sc[i][:, 0:W], mx[:, 7:8], None, ALU.is_ge, None, [b_sc[i], b_mx], [b_Mb[i]])
                else:
                    def mask():
                        TS("dve", Mb[i][:, 0:W], sc[i][:, 0:W], -1e29, None, ALU.is_ge, None, [b_sc[i]], [b_Mb[i]])
                out.append((mask, W / 1000.0))

                def trans(kb0, nb):
                    def f():
                        bk = banks["tr"][cnt["t"] % len(banks["tr"])]
                        sgi = cnt["t"] % 2
                        cnt["t"] += 1
                        for j in range(nb):
                            TR(PBb[bk][:, j * 128:(j + 1) * 128], Mb[i][:, (kb0 + j) * 128:(kb0 + j + 1) * 128], identb[:], [b_Mb[i], b_idb], [BK[bk]])
                        CP("act", mstg[sgi][:, 0:nb, :], PBb[bk][:, 0:nb * 128].rearrange("p (j t) -> p j t", j=nb), [BK[bk]], [b_mstg[sgi]])
                        DMA("sp", MT[kb0 * 128:(kb0 + nb) * 128, qs].rearrange("(j p) t -> p j t", p=128), mstg[sgi][:, 0:nb, :], [b_mstg[sgi]], [])
                    return f
                for kb0 in range(0, qt + 1, 4):
                    out.append((trans(kb0, min(4, qt + 1 - kb0)), 0.5))
                nfront = 1 + len(range(0, W, 512))
                ntr = len(range(0, qt + 1, 4))
                return out[:nfront], out[nfront:len(out) - ntr], out[len(out) - ntr:]

            fb = [tile_items(qt, n_ % 2) for n_, qt in enumerate(qt_list)]
            if fb:
                items.extend(fb[0][0])
            for n_ in range(len(fb)):
                if n_ + 1 < len(fb):
                    items.extend(fb[n_ + 1][0])
                items.extend(fb[n_][1])
                if n_ >= 1:
                    items.extend(fb[n_ - 1][2])
            if fb:
                items.extend(fb[-1][2])
            return items

        for l in range(L):
            with ExitStack() as st:
                hT = sbuf(st, "hT", [128, KC, S], BF16); b_hT = Buf()
                with ExitStack() as st1:
                    norm_phase(st1, hT, b_hT, AB[:, l, 0:16], modT[:, l, 0:16])
                P.barrier()
                slab = [sbuf(st, "slab%d" % i, [128, KC, 512], BF16) for i in range(2)]; b_slab = [Buf(), Buf()]
                st_tm = ExitStack()
                gqa, b_gqa = bcast_load(st_tm, "gqa", Wd["g_qa"][l:l + 1, :], 128)
                gka, b_gka = bcast_load(st_tm, "gka", Wd["g_ka"][l:l + 1, :], 128)
                NB = 8
                qn = [sbuf(st_tm, "qn%d" % i, [128, 512], F32) for i in range(NB)]; b_qn = [Buf() for _ in range(NB)]
                qb = [sbuf(st_tm, "qb%d" % i, [128, 512], BF16) for i in range(NB)]; b_qb = [Buf() for _ in range(NB)]
                ssq = [sbuf(st_tm, "ssq%d" % i, [128, 8], F32) for i in range(NB)]; b_ssq = [Buf() for _ in range(NB)]
                junk = [sbuf(st_tm, "junk%d" % i, [128, 128], F32) for i in range(2)]; b_junk = [Buf(), Buf()]
                rts = [(sbuf(st_tm, "rt_a%d" % i, [128, 64], F32), sbuf(st_tm, "rt_b%d" % i, [128, 64], F32), Buf()) for i in range(4)]
                stg = [sbuf(st_tm, "stg%d" % i, [128, 4, 128], BF16) for i in range(4)]; b_stg = [Buf() for _ in range(4)]
                ki2 = [sbuf(st_tm, "ki2_%d" % i, [128, 128], BF16) for i in range(4)]; b_ki2 = [Buf() for _ in range(4)]
                slab_n = [0]

                def load_slab(col_ranges):
                    i = slab_n[0] % 2
                    slab_n[0] += 1
                    off = 0
                    for (c0, n) in col_ranges:
                        DMA("pool", slab[i][:, :, off:off + n], Wd["w_in"][l][:, c0:c0 + n].rearrange("(kc p) n -> p kc n", p=128), [], [b_slab[i]])
                        off += n
                    return i, off

                def tok_mm(i, ncols, tt, bk):
                    for kc in range(KC):
                        MM(PB[bk][:, 0:ncols], hT[:, kc, tt * 128:(tt + 1) * 128], slab[i][:, kc, 0:ncols], kc == 0, kc == KC - 1, [b_hT, b_slab[i]], [BK[bk]])

                units = []

                def qk_unit(u, i, tt, nh, g_t, b_g, dst_rows, nblk, has_va):
                    b = u % NB; bk = u % 3; tb = 4 + u % 4; sg = u % 4; rt = rts[u % 4]; jk = u % 2
                    ts_ = slice(tt * 128, (tt + 1) * 128)

                    def s0():
                        tok_mm(i, 512, tt, bk)

                    def s1():
                        CP("act", qn[b][:], PB[bk][:], [BK[bk]], [b_qn[b]])

                    def s2():
                        for h in range(nh):
                            ACT(junk[jk][:], qn[b][:, h * 128:(h + 1) * 128], AF.Square, [b_qn[b]], [b_junk[jk], b_ssq[b]], accum_out=ssq[b][:, h:h + 1])

                    def s3():
                        ACT(ssq[b][:, 0:nh], ssq[b][:, 0:nh], AF.Sqrt, [b_ssq[b], b_eps], [b_ssq[b]], scale=1.0 / 128, bias=eps_t[:, 0:1])

                    def s4():
                        RECIP(ssq[b][:, 0:nh], ssq[b][:, 0:nh], [b_ssq[b]], [b_ssq[b]])

                    def s5():
                        for h in range(nh):
                            hs = slice(h * 128, (h + 1) * 128)
                            STT("dve", qn[b][:, hs], qn[b][:, hs], ssq[b][:, h:h + 1], g_t[:], ALU.mult, ALU.mult, [b_qn[b], b_ssq[b], b_g], [b_qn[b]])

                    def s6():
                        CP("pool", qb[b][:], qn[b][:], [b_qn[b]], [b_qb[b]])
                        s3_ = qn[b][:, 0:nh * 128].rearrange("p (h d) -> p h d", h=nh)[:, :, 0:32]
                        d3_ = qb[b][:, 0:nh * 128].rearrange("p (h d) -> p h d", h=nh)[:, :, 0:32]
                        rope(rt, s3_, d3_, nh, 16, tab[:, tt, 56:72], tab[:, tt, 0:16], [b_qn[b]], [b_qb[b]], eng="dve")

                    def s7():
                        for j in range(nblk):
                            TR(PBb[tb][:, j * 128:(j + 1) * 128], qb[b][:, j * 128:(j + 1) * 128], identb[:], [b_qb[b], b_idb], [BK[tb]])

                    def s8():
                        CP("act", stg[sg][:, 0:nblk, :], PBb[tb][:, 0:nblk * 128].rearrange("p (j t) -> p j t", j=nblk), [BK[tb]], [b_stg[sg]])
                        DMA("sp", dst_rows[0:nblk * 128, ts_].rearrange("(j p) t -> p j t", p=128), stg[sg][:, 0:nblk, :], [b_stg[sg]], [])
                        if has_va:
                            DMA("sp", va[ts_, :], qb[b][:, 256:512], [b_qb[b]], [])
                    return [s0, s1, s2, s3, s4, s5, s6, s7, s8]

                def qi_unit(u, i, tt, dst_rows):
                    b = u % NB; bk = u % 3; tb = 4 + u % 4; sg = u % 4; rt = rts[u % 4]
                    ts_ = slice(tt * 128, (tt + 1) * 128)

                    def s0():
                        tok_mm(i, 512, tt, bk)

                    def s1():
                        CP("act", qn[b][:], PB[bk][:], [BK[bk]], [b_qn[b]])

                    def s2():
                        CP("pool", qb[b][:], qn[b][:], [b_qn[b]], [b_qb[b]])
                        s3_ = qn[b][:].rearrange("p (h d) -> p h d", h=8)[:, :, 0:16]
                        d3_ = qb[b][:].rearrange("p (h d) -> p h d", h=8)[:, :, 0:16]
                        rope(rt, s3_, d3_, 8, 8, tab[:, tt, 72:80], tab[:, tt, 16:24], [b_qn[b]], [b_qb[b]])

                    def s3():
                        for j in range(4):
                            TR(PBb[tb][:, j * 128:(j + 1) * 128], qb[b][:, j * 128:(j + 1) * 128], identb[:], [b_qb[b], b_idb], [BK[tb]])

                    def s4():
                        CP("act", stg[sg][:, 0:4, :], PBb[tb][:, 0:512].rearrange("p (j t) -> p j t", j=4), [BK[tb]], [b_stg[sg]])
                        DMA("sp", dst_rows[0:512, ts_].rearrange("(j p) t -> p j t", p=128), stg[sg][:, 0:4, :], [b_stg[sg]], [])
                    nop = lambda: None
                    return [s0, s1, nop, nop, nop, nop, s2, s3, s4]

                def misc_unit(u, i, tt):
                    b = u % NB; bk = u % 3; tb = 4 + u % 4; sg = u % 4; rt = rts[u % 4]; kk = u % 4
                    ts_ = slice(tt * 128, (tt + 1) * 128)

                    def s0():
                        tok_mm(i, 144, tt, bk)

                    def s1():
                        CP("act", qn[b][:, 0:144], PB[bk][:, 0:144], [BK[bk]], [b_qn[b]])

                    def s2():
                        CP("pool", ki2[kk][:, 0:64], qn[b][:, 0:64], [b_qn[b]], [b_ki2[kk]])
                        s3_ = qn[b][:, 0:64].rearrange("p (h d) -> p h d", h=1)[:, :, 0:16]
                        d3_ = ki2[kk][:, 0:64].rearrange("p (h d) -> p h d", h=1)[:, :, 0:16]
                        rope(rt, s3_, d3_, 1, 8, tab[:, tt, 72:80], tab[:, tt, 16:24], [b_qn[b]], [b_ki2[kk]])
                        CP("pool", ki2[kk][:, 64:128], ki2[kk][:, 0:64], [b_ki2[kk]], [b_ki2[kk]])
                        TS("pool", qn[b][:, 64:80], qn[b][:, 64:80], float(1024 ** -0.5), None, ALU.mult, None, [b_qn[b]], [b_qn[b]])

                    def s3():
                        TR(PBb[tb][:, 0:128], ki2[kk][:], identb[:], [b_ki2[kk], b_idb], [BK[tb]])
                        DMA("sp", wis[ts_, :], qn[b][:, 64:80], [b_qn[b]], [])
                        DMA("sp", mkr[ts_, :], qn[b][:, 80:144], [b_qn[b]], [])

                    def s4():
                        CP("act", stg[sg][:, 0, :], PBb[tb][:, 0:128], [BK[tb]], [b_stg[sg]])
                        DMA("sp", kiT2[:, ts_], stg[sg][:, 0, :], [b_stg[sg]], [])
                    nop = lambda: None
                    return [s0, s1, nop, nop, nop, nop, s2, s3, s4]

                groups = [("qa", [(0, 512)]), ("qa", [(512, 512)]), ("ka", [(1024, 512)]), ("qi", [(1536, 512)]), ("qi", [(2048, 512)]), ("misc", [(2560, 80), (3408, 64)])]

                def slab_load(gi):
                    i = gi % 2
                    off = 0
                    for (c0, n) in groups[gi][1]:
                        DMA("pool", slab[i][:, :, off:off + n], Wd["w_in"][l][:, c0:c0 + n].rearrange("(kc p) n -> p kc n", p=128), [], [b_slab[i]])
                        off += n

                def with_pre(stages, pre):
                    s0 = stages[0]

                    def s0p():
                        pre()
                        s0()
                    return [s0p] + stages[1:]

                u = 0
                for gi, (kind, cr) in enumerate(groups):
                    i = gi % 2
                    for tt in range(NT):
                        if kind == "qa":
                            stages = qk_unit(u, i, tt, 4, gqa, b_gqa, qaT[gi * 512:(gi + 1) * 512, :], 4, False)
                        elif kind == "ka":
                            stages = qk_unit(u, i, tt, 2, gka, b_gka, kaT, 2, True)
                        elif kind == "qi":
                            stages = qi_unit(u, i, tt, qiT[(gi - 3) * 512:(gi - 2) * 512, :])
                        else:
                            stages = misc_unit(u, i, tt)
                        if tt == 0:
                            def pre(gi=gi):
                                if gi == 0:
                                    slab_load(0)
                                if gi + 1 < len(groups):
                                    slab_load(gi + 1)
                            stages = with_pre(stages, pre)
                        units.append(stages)
                        u += 1
                run_skewed(units)
                slab_n[0] = len(groups)
                P.barrier()
                st_tm.close()
                st_f = ExitStack()
                lat = sbuf(st_f, "lat", [128, 4, 512], F32); b_lat = Buf()
                latb = [sbuf(st_f, "latb%d" % i, [128, 512], BF16) for i in range(2)]; b_latb = [Buf(), Buf()]
                lsq = [sbuf(st_f, "lsq%d" % i, [128, 512], F32) for i in range(2)]; b_lsq = [Buf(), Buf()]
                lrs = sbuf(st_f, "lrs", [128, 512], F32); b_lrs = Buf()
                glatT = sbuf(st_f, "glatT", [128, 8], F32); b_glat = Buf()
                rows_to_cols(st_f, Wd["g_mq_lat"][l].rearrange("(k p) -> k p", p=128), 4, glatT[:, 0:4], b_glat, 6, "gmq")
                rows_to_cols(st_f, Wd["g_mkv_lat"][l].rearrange("(k p) -> k p", p=128), 2, glatT[:, 4:6], b_glat, 6, "gmkv")
                for (c0, nm, dst, goff) in ((2640, 4, mqlT, 0), (3152, 2, mkvlT, 4)):
                    i, _ = load_slab([(c0, nm * 128)])
                    for ch in range(NCH):
                        cs = slice(ch * 512, (ch + 1) * 512)
                        for m in range(nm):
                            bk = m % 4
                            for kc in range(KC):
                                MM(PB[bk][:], slab[i][:, kc, m * 128:(m + 1) * 128], hT[:, kc, cs], kc == 0, kc == KC - 1, [b_slab[i], b_hT], [BK[bk]])
                            CP("dve", lat[:, m, :], PB[bk][:], [BK[bk]], [b_lat])
                            ACT(lsq[m % 2][:], PB[bk][:], AF.Square, [BK[bk]], [b_lsq[m % 2]])
                            MM(PB[4][:], ones_f[:], lsq[m % 2][:], m == 0, m == nm - 1, [b_1f, b_lsq[m % 2]], [BK[4]])
                        CP("dve", lrs[:], PB[4][:], [BK[4]], [b_lrs])
                        RSTD(lrs[:], 1.0 / (nm * 128), [b_lrs])
                        for m in range(nm):
                            STT("dve", latb[m % 2][:], lat[:, m, :], glatT[:, goff + m:goff + m + 1], lrs[:], ALU.mult, ALU.mult, [b_lat, b_glat, b_lrs], [b_latb[m % 2]])
                            DMA("sp", dst[m * 128:(m + 1) * 128, cs], latb[m % 2][:], [b_latb[m % 2]], [])
                P.barrier()
                st_f.close()
                sg = [sbuf(st, "sg%d" % i, [128, 512], F32) for i in range(2)]; b_sg = [Buf(), Buf()]
                gslabs = [(c0 + s4 * 512, dst, s4) for (c0, dst) in ((3472, sgaT), (5520, sgbT)) for s4 in range(4)]
                gslab_i = {}

                def gate_slab(gn):
                    if gn < len(gslabs) and gn not in gslab_i:
                        gslab_i[gn] = load_slab([(gslabs[gn][0], 512)])[0]
                gate_items = []
                gcnt = [0]

                def gate_item(gn, ch, m):
                    def f():
                        if ch == 0 and m == 0:
                            gate_slab(gn)
                            gate_slab(gn + 1)
                        i = gslab_i[gn]
                        c0_, dst, s4 = gslabs[gn]
                        n = gcnt[0]
                        gcnt[0] += 1
                        cs = slice(ch * 512, (ch + 1) * 512)
                        bk = 4 + n % 2
                        for kc in range(KC):
                            MM(PB[bk][:], slab[i][:, kc, m * 128:(m + 1) * 128], hT[:, kc, cs], kc == 0, kc == KC - 1, [b_slab[i], b_hT], [BK[bk]])
                        ACT(sg[n % 2][:], PB[bk][:], AF.Sigmoid, [BK[bk]], [b_sg[n % 2]])
                        r0 = s4 * 512 + m * 128
                        DMA("sp", dst[r0:r0 + 128, cs], sg[n % 2][:], [b_sg[n % 2]], [])
                    return f
                for gn in range(len(gslabs)):
                    for ch in range(NCH):
                        for m in range(4):
                            gate_items.append(gate_item(gn, ch, m))
                p3 = make_p3(st, list(range(NT)), dict(score=[0, 1], acc=[2], tr=[3]))
                side = gate_items
                if l + 1 < L:
                    mi_ = mod_items(st, l + 1, 6, 7)
                    side = []
                    step_ = max(1, len(gate_items) // len(mi_))
                    k_ = 0
                    for gi_, g_ in enumerate(gate_items):
                        side.append(g_)
                        if (gi_ + 1) % step_ == 0 and k_ < len(mi_):
                            side.append(mi_[k_]); k_ += 1
                    side.extend(mi_[k_:])
                interleave(p3, side)
            P.barrier()

            with ExitStack() as st:
                mqlS = sbuf(st, "mqlS", [128, 4, S], BF16); b_mqlS = Buf()
                mkvlS = sbuf(st, "mkvlS", [128, 2, S], BF16); b_mkvlS = Buf()
                wq = sbuf(st, "wq", [128, 4, 1536], BF16); b_wq = Buf()
                wkv = sbuf(st, "wkv", [128, 2, 2048], BF16); b_wkv = Buf()
                DMA("sp", mqlS[:], mqlT.rearrange("(kc p) s -> p kc s", p=128), [], [b_mqlS])
                DMA("sp", mkvlS[:], mkvlT.rearrange("(kc p) s -> p kc s", p=128), [], [b_mkvlS])
                DMA("pool", wq[:], Wd["w_mq_up"][l].rearrange("(kc p) n -> p kc n", p=128), [], [b_wq])
                DMA("pool", wkv[:], Wd["w_mkv_up"][l].rearrange("(kc p) n -> p kc n", p=128), [], [b_wkv])
                gqm, b_gqm = bcast_load(st, "gqm", Wd["g_qm"][l:l + 1, :], 192)
                gkm, b_gkm = bcast_load(st, "gkm", Wd["g_km"][l:l + 1, :], 192)
                NB = 8
                qn = [sbuf(st, "m_qn%d" % i, [128, 512], F32) for i in range(NB)]; b_qn = [Buf() for _ in range(NB)]
                qb = [sbuf(st, "m_qb%d" % i, [128, 384], BF16) for i in range(NB)]; b_qb = [Buf() for _ in range(NB)]
                kb_ = [sbuf(st, "m_kb%d" % i, [128, 384], BF16) for i in range(NB)]; b_kb = [Buf() for _ in range(NB)]
                kr = [sbuf(st, "m_kr%d" % i, [128, 128], F32) for i in range(NB)]; b_kr = [Buf() for _ in range(NB)]
                vb = [sbuf(st, "m_vb%d" % i, [128, 256], BF16) for i in range(NB)]; b_vb = [Buf() for _ in range(NB)]
                ssq = [sbuf(st, "m_ssq%d" % i, [128, 4], F32) for i in range(NB)]; b_ssq = [Buf() for _ in range(NB)]
                junk = [sbuf(st, "m_junk%d" % i, [128, 192], F32) for i in range(2)]; b_junk = [Buf(), Buf()]
                rts = [(sbuf(st, "m_rt_a%d" % i, [128, 64], F32), sbuf(st, "m_rt_b%d" % i, [128, 64], F32), Buf()) for i in range(4)]
                stg = [sbuf(st, "m_stg%d" % i, [128, 2, 128], BF16) for i in range(4)]; b_stg = [Buf() for _ in range(4)]
                stgr = [sbuf(st, "m_stgr%d" % i, [64, 2, 128], BF16) for i in range(4)]; b_stgr = [Buf() for _ in range(4)]
                mkr_t = [sbuf(st, "m_mkr%d" % i, [128, 64], F32) for i in range(3)]; b_mkr = [Buf() for _ in range(3)]
                ssr = [sbuf(st, "m_ssr%d" % i, [128, 1], F32) for i in range(3)]; b_ssr = [Buf() for _ in range(3)]

                def mq_unit(u, tt, g):
                    b = u % NB; bk = 6 + u % 2; tb = 4 + u % 2; sg = u % 4; rt = rts[u % 4]; jk = u % 2
                    ts_ = slice(tt * 128, (tt + 1) * 128)

                    def s0():
                        for kc in range(4):
                            MM(PB[bk][:, 0:384], mqlS[:, kc, ts_], wq[:, kc, g * 384:(g + 1) * 384], kc == 0, kc == 3, [b_mqlS, b_wq], [BK[bk]])

                    def s1():
                        CP("act", qn[b][:, 0:384], PB[bk][:, 0:384], [BK[bk]], [b_qn[b]])

                    def s2():
                        for h in range(2):
                            ACT(junk[jk][:, 0:192], qn[b][:, h * 192:(h + 1) * 192], AF.Square, [b_qn[b]], [b_junk[jk], b_ssq[b]], accum_out=ssq[b][:, h:h + 1])

                    def s3():
                        pass

                    def s4():
                        ACT(ssq[b][:, 0:2], ssq[b][:, 0:2], AF.Sqrt, [b_ssq[b], b_eps], [b_ssq[b]], scale=1.0 / 192, bias=eps_t[:, 0:1])

                    def s5():
                        RECIP(ssq[b][:, 0:2], ssq[b][:, 0:2], [b_ssq[b]], [b_ssq[b]])

                    def s6():
                        for h in range(2):
                            hs = slice(h * 192, (h + 1) * 192)
                            STT("dve", qn[b][:, hs], qn[b][:, hs], ssq[b][:, h:h + 1], gqm[:], ALU.mult, ALU.mult, [b_qn[b], b_ssq[b], b_gqm], [b_qn[b]])

                    def s7():
                        CP("pool", qb[b][:], qn[b][:, 0:384], [b_qn[b]], [b_qb[b]])
                        s3_ = qn[b][:, 0:384].rearrange("p (h d) -> p h d", h=2)[:, :, 128:192]
                        d3_ = qb[b][:].rearrange("p (h d) -> p h d", h=2)[:, :, 128:192]
                        rope(rt, s3_, d3_, 2, 32, tab[:, tt, 80:112], tab[:, tt, 24:56], [b_qn[b]], [b_qb[b]])

                    def s8():
                        for h in range(2):
                            TR(PBb[tb][:, h * 128:(h + 1) * 128], qb[b][:, h * 192:h * 192 + 128], identb[:], [b_qb[b], b_idb], [BK[tb]])
                            TR(PBb[tb][0:64, 256 + h * 128:256 + (h + 1) * 128], qb[b][:, h * 192 + 128:(h + 1) * 192], identb[:], [b_qb[b], b_idb], [BK[tb]])

                    def s9():
                        CP("act", stg[sg][:, :, :], PBb[tb][:, 0:256].rearrange("p (j t) -> p j t", j=2), [BK[tb]], [b_stg[sg]])
                        CP("act", stgr[sg][:, :, :], PBb[tb][0:64, 256:512].rearrange("p (j t) -> p j t", j=2), [BK[tb]], [b_stgr[sg]])
                        DMA("sp", mqTn[g * 256:(g + 1) * 256, ts_].rearrange("(j p) t -> p j t", p=128), stg[sg][:, :, :], [b_stg[sg]], [])
                        DMA("sp", mqTr[g * 128:(g + 1) * 128, ts_].rearrange("(j p) t -> p j t", p=64), stgr[sg][:, :, :], [b_stgr[sg]], [])
                    return [s0, s1, s2, s3, s4, s5, s6, s7, s8, s9]

                def mkv_unit(u, tt, g):
                    b = u % NB; bk = 6 + u % 2; tb = 4 + u % 2; sg = u % 4; rt = rts[u % 4]; jk = u % 2; tr3 = tt % 3
                    ts_ = slice(tt * 128, (tt + 1) * 128)

                    def s0():
                        for kc in range(2):
                            MM(PB[bk][:], mkvlS[:, kc, ts_], wkv[:, kc, g * 512:(g + 1) * 512], kc == 0, kc == 1, [b_mkvlS, b_wkv], [BK[bk]])
                        if g == 0:
                            DMA("sp", mkr_t[tr3][:], mkr[ts_, :], [], [b_mkr[tr3]])

                    def s1():
                        CP("act", qn[b][:], PB[bk][:], [BK[bk]], [b_qn[b]])

                    def s2():
                        if g == 0:
                            ACT(junk[jk][:, 0:64], mkr_t[tr3][:], AF.Square, [b_mkr[tr3]], [b_junk[jk], b_ssr[tr3]], accum_out=ssr[tr3][:, 0:1])
                        for h in range(2):
                            ACT(junk[jk][:, 0:128], qn[b][:, h * 256:h * 256 + 128], AF.Square, [b_qn[b]], [b_junk[jk], b_ssq[b]], accum_out=ssq[b][:, h:h + 1])

                    def s3():
                        TS("dve", ssq[b][:, 0:2], ssq[b][:, 0:2], ssr[tr3][:, 0:1], None, ALU.add, None, [b_ssq[b], b_ssr[tr3]], [b_ssq[b]])

                    def s4():
                        ACT(ssq[b][:, 0:2], ssq[b][:, 0:2], AF.Sqrt, [b_ssq[b], b_eps], [b_ssq[b]], scale=1.0 / 192, bias=eps_t[:, 0:1])

                    def s5():
                        RECIP(ssq[b][:, 0:2], ssq[b][:, 0:2], [b_ssq[b]], [b_ssq[b]])

                    def s6():
                        for h in range(2):
                            STT("dve", kb_[b][:, h * 128:(h + 1) * 128], qn[b][:, h * 256:h * 256 + 128], ssq[b][:, h:h + 1], gkm[:, 0:128], ALU.mult, ALU.mult, [b_qn[b], b_ssq[b], b_gkm], [b_kb[b]])
                            STT("dve", kr[b][:, h * 64:(h + 1) * 64], mkr_t[tr3][:], ssq[b][:, h:h + 1], gkm[:, 128:192], ALU.mult, ALU.mult, [b_mkr[tr3], b_ssq[b], b_gkm], [b_kr[b]])

                    def s7():
                        s3_ = kr[b][:].rearrange("p (h d) -> p h d", h=2)
                        d3_ = kb_[b][:, 256:384].rearrange("p (h d) -> p h d", h=2)
                        rope(rt, s3_, d3_, 2, 32, tab[:, tt, 80:112], tab[:, tt, 24:56], [b_kr[b]], [b_kb[b]], eng="dve")
                        CP("pool", vb[b][:].rearrange("p (h d) -> p h d", h=2), qn[b][:].rearrange("p (h d) -> p h d", h=2)[:, :, 128:256], [b_qn[b]], [b_vb[b]])

                    def s8():
                        for h in range(2):
                            TR(PBb[tb][:, h * 128:(h + 1) * 128], kb_[b][:, h * 128:(h + 1) * 128], identb[:], [b_kb[b], b_idb], [BK[tb]])
                            TR(PBb[tb][0:64, 256 + h * 128:256 + (h + 1) * 128], kb_[b][:, 256 + h * 64:256 + (h + 1) * 64], identb[:], [b_kb[b], b_idb], [BK[tb]])
                        DMA("sp", mv[ts_, g * 256:(g + 1) * 256], vb[b][:], [b_vb[b]], [])

                    def s9():
                        CP("act", stg[sg][:, :, :], PBb[tb][:, 0:256].rearrange("p (j t) -> p j t", j=2), [BK[tb]], [b_stg[sg]])
                        CP("act", stgr[sg][:, :, :], PBb[tb][0:64, 256:512].rearrange("p (j t) -> p j t", j=2), [BK[tb]], [b_stgr[sg]])
                        DMA("sp", mkTn[g * 256:(g + 1) * 256, ts_].rearrange("(j p) t -> p j t", p=128), stg[sg][:, :, :], [b_stg[sg]], [])
                        DMA("sp", mkTr[g * 128:(g + 1) * 128, ts_].rearrange("(j p) t -> p j t", p=64), stgr[sg][:, :, :], [b_stgr[sg]], [])
                    return [s0, s1, s2, s3, s4, s5, s6, s7, s8, s9]

                units = []
                u = 0
                for tt in range(NT):
                    for g in range(4):
                        units.append(mq_unit(u, tt, g)); u += 1
                    for g in range(4):
                        units.append(mkv_unit(u, tt, g)); u += 1
                run_skewed(units)
            P.barrier()

            with ExitStack() as st:
                kaS = sbuf(st, "kaS", [128, 2, S], BF16); b_kaS = Buf()
                vaS = sbuf(st, "vaS", [128, NT, 256], BF16); b_vaS = Buf()
                DMA("sp", kaS[:], kaT.rearrange("(g p) s -> p g s", p=128), [], [b_kaS])
                DMA("sp", vaS[:], va.rearrange("(t p) c -> p t c", p=128), [], [b_vaS])
                mkn = sbuf(st, "mkn", [128, 8, S], BF16); b_mkn = Buf()
                mkrS = sbuf(st, "mkrS", [64, 8, S], BF16); b_mkrS = Buf()
                mvS = sbuf(st, "mvS", [128, NT, 1024], BF16); b_mvS = Buf()
                DMA("sp", mkn[:], mkTn.rearrange("(h p) s -> p h s", p=128), [], [b_mkn])
                DMA("sp", mkrS[:], mkTr.rearrange("(h p) s -> p h s", p=64), [], [b_mkrS])
                DMA("sp", mvS[:], mv.rearrange("(t p) c -> p t c", p=128), [], [b_mvS])
                qT_ = [sbuf(st, "a_q%d" % i, [128, 512], BF16) for i in range(3)]; b_q = [Buf(), Buf(), Buf()]
                qR_ = [sbuf(st, "a_qr%d" % i, [64, 512], BF16) for i in range(3)]; b_qr = [Buf(), Buf(), Buf()]
                mts = [sbuf(st, "a_mts%d" % i, [128, NT, 512], BF16) for i in range(2)]; b_mts = [Buf(), Buf()]
                pT = [sbuf(st, "a_pT%d" % i, [128, 512], BF16) for i in range(4)]; b_pT = [Buf() for _ in range(4)]
                rd_ = sbuf(st, "a_rd", [128, 512], F32); b_rd = Buf()
                ob = [sbuf(st, "a_ob%d" % i, [128, 512], BF16) for i in range(2)]; b_ob = [Buf(), Buf()]
                hcount = [0]
                icount = [0]
                heads_all = [(qc_, h_) for qc_ in range(NCH) for h_ in range(16)]

                def load_q(g):
                    if g >= len(heads_all):
                        return
                    qc_, h_ = heads_all[g]
                    cs_ = slice(qc_ * 512, (qc_ + 1) * 512)
                    isA_ = h_ < 8
                    hh_ = h_ if isA_ else h_ - 8
                    i_ = g % 3
                    if isA_:
                        DMA("sp", qT_[i_][:], qaT[hh_ * 128:(hh_ + 1) * 128, cs_], [], [b_q[i_]])
                    else:
                        DMA("sp", qT_[i_][:], mqTn[hh_ * 128:(hh_ + 1) * 128, cs_], [], [b_q[i_]])
                        DMA("sp", qR_[i_][:], mqTr[hh_ * 64:(hh_ + 1) * 64, cs_], [], [b_qr[i_]])
                load_q(0)
                for qc in range(NCH):
                    cs = slice(qc * 512, (qc + 1) * 512)
                    nkb = 4 * qc + 4
                    mi = qc % 2
                    if qc == 0:
                        DMA("sp", mts[0][:, 0:4, :], MT[0:512, 0:512].rearrange("(k p) t -> p k t", p=128), [], [b_mts[0]])
                    if qc + 1 < NCH:
                        nkb2 = 4 * (qc + 1) + 4
                        DMA("sp", mts[(qc + 1) % 2][:, 0:nkb2, :], MT[0:nkb2 * 128, (qc + 1) * 512:(qc + 2) * 512].rearrange("(k p) t -> p k t", p=128), [], [b_mts[(qc + 1) % 2]])
                    items = []
                    for h in range(16):
                        hq = hcount[0]
                        hcount[0] += 1
                        for kb in range(nkb):
                            items.append((h, kb, hq, icount[0]))
                            icount[0] += 1

                    def a_scores(it, qc=qc, cs=cs):
                        h, kb, hq, n = it
                        isA = h < 8
                        hh = h if isA else h - 8
                        i = hq % 3
                        if kb == 0:
                            load_q(hq + 1)
                        q0 = max(0, kb - 4 * qc) * 128
                        ks = slice(kb * 128, (kb + 1) * 128)
                        bk = n % 4
                        if isA:
                            MM(PB[bk][:, q0:512], kaS[:, hh // 4, ks], qT_[i][:, q0:512], True, True, [b_kaS, b_q[i]], [BK[bk]])
                        else:
                            MM(PB[bk][:, q0:512], mkn[:, hh, ks], qT_[i][:, q0:512], True, False, [b_mkn, b_q[i]], [BK[bk]])
                            MM(PB[bk][:, q0:512], mkrS[:, hh, ks], qR_[i][:, q0:512], False, True, [b_mkrS, b_qr[i]], [BK[bk]])

                    def a_rest(it, qc=qc, cs=cs, nkb=nkb, mi=mi):
                        h, kb, hq, n = it
                        isA = h < 8
                        hh = h if isA else h - 8
                        scale = float(128 ** -0.5) if isA else float(192 ** -0.5)
                        q0 = max(0, kb - 4 * qc) * 128
                        bk = n % 4
                        j = n % 4
                        bo = 4 + (hq % 2) * 2
                        ACT(pT[j][:, q0:512], PB[bk][:, q0:512], AF.Exp, [BK[bk]], [b_pT[j]], scale=scale)
                        if isA:
                            TT("dve" if n % 2 else "pool", pT[j][:, q0:512], pT[j][:, q0:512], mts[mi][:, kb, q0:512], ALU.mult, [b_pT[j], b_mts[mi]], [b_pT[j]])
                            vv = vaS[:, kb, (hh // 4) * 128:(hh // 4 + 1) * 128]
                            b_v = b_vaS
                        else:
                            if kb >= 4 * qc:
                                TT("dve" if n % 2 else "pool", pT[j][:, q0:q0 + 128], pT[j][:, q0:q0 + 128], tri[:], ALU.mult, [b_pT[j], b_tri], [b_pT[j]])
                            vv = mvS[:, kb, hh * 128:(hh + 1) * 128]
                            b_v = b_mvS
                        MM(PB[bo][:, q0:512], vv, pT[j][:, q0:512], kb == 0, kb == nkb - 1, [b_v, b_pT[j]], [BK[bo]])
                        MM(PB[bo + 1][:, q0:512], ones_b[:], pT[j][:, q0:512], kb == 0, kb == nkb - 1, [b_1b, b_pT[j]], [BK[bo + 1]])
                        if kb == nkb - 1:
                            i2 = hq % 2
                            RECIP(rd_[:], PB[bo + 1][:], [BK[bo + 1]], [b_rd])
                            TT("dve", ob[i2][:], PB[bo][:], rd_[:], ALU.mult, [BK[bo], b_rd], [b_ob[i2]])
                            dst = oaT if isA else obT
                            DMA("sp", dst[hh * 128:(hh + 1) * 128, cs], ob[i2][:], [b_ob[i2]], [])

                    DEPTH_PF = 2
                    for k in range(min(DEPTH_PF, len(items))):
                        a_scores(items[k])
                    for k in range(len(items)):
                        if k + DEPTH_PF < len(items):
                            a_scores(items[k + DEPTH_PF])
                        a_rest(items[k])
            P.barrier()

            with ExitStack() as st:
                mT_ = sbuf(st, "mT_", [128, KC, S], BF16); b_mT = Buf()
                with ExitStack() as st1:
                    oaS = sbuf(st1, "oaS", [128, 8, S], BF16); b_oaS = Buf()
                    obS = sbuf(st1, "obS", [128, 8, S], BF16); b_obS = Buf()
                    DMA("sp", oaS[:], oaT.rearrange("(k p) s -> p k s", p=128), [], [b_oaS])
                    DMA("sp", obS[:], obT.rearrange("(k p) s -> p k s", p=128), [], [b_obS])
                    wpa = [sbuf(st1, "wpa%d" % i, [128, 8, 128], BF16) for i in range(2)]; b_wpa = [Buf(), Buf()]
                    wpb = [sbuf(st1, "wpb%d" % i, [128, 8, 128], BF16) for i in range(2)]; b_wpb = [Buf(), Buf()]
                    ga = [sbuf(st1, "ga%d" % i, [128, 512], F32) for i in range(4)]; b_ga = [Buf() for _ in range(4)]
                    gb = [sbuf(st1, "gb%d" % i, [128, 512], F32) for i in range(4)]; b_gb = [Buf() for _ in range(4)]
                    t1 = [sbuf(st1, "p5t%d" % i, [128, 512], F32) for i in range(4)]; b_t1 = [Buf() for _ in range(4)]
                    n = 0
                    for m in range(KC):
                        i = m % 2
                        ms = slice(m * 128, (m + 1) * 128)
                        DMA("pool", wpa[i][:], Wd["w_pa"][l][:, ms].rearrange("(k p) n -> p k n", p=128), [], [b_wpa[i]])
                        DMA("pool", wpb[i][:], Wd["w_pb"][l][:, ms].rearrange("(k p) n -> p k n", p=128), [], [b_wpb[i]])
                        for ch in range(NCH):
                            cs = slice(ch * 512, (ch + 1) * 512)
                            ba = (n % 4) * 2
                            for k in range(8):
                                MM(PB[ba][:], wpa[i][:, k, :], oaS[:, k, cs], k == 0, k == 7, [b_wpa[i], b_oaS], [BK[ba]])
                            for k in range(8):
                                MM(PB[ba + 1][:], wpb[i][:, k, :], obS[:, k, cs], k == 0, k == 7, [b_wpb[i], b_obS], [BK[ba + 1]])
                            j = n % 4
                            DMA("sp", ga[j][:], sgaT[ms, cs], [], [b_ga[j]])
                            DMA("sp", gb[j][:], sgbT[ms, cs], [], [b_gb[j]])
                            TT("dve", t1[j][:], PB[ba][:], ga[j][:], ALU.mult, [BK[ba], b_ga[j]], [b_t1[j]])
                            TT("dve", ga[j][:], PB[ba + 1][:], gb[j][:], ALU.mult, [BK[ba + 1], b_gb[j], b_ga[j]], [b_ga[j]])
                            TT("dve", mT_[:, m, cs], t1[j][:], ga[j][:], ALU.add, [b_t1[j], b_ga[j]], [b_mT])
                            n += 1
                P.barrier()
                with ExitStack() as st1:
                    wo = [sbuf(st1, "wo%d" % i, [128, KC, 128], BF16) for i in range(2)]; b_wo = [Buf(), Buf()]
                    xm = [sbuf(st1, "xm%d" % i, [128, S], F32) for i in range(3)]; b_xm = [Buf(), Buf(), Buf()]
                    n = 0
                    DMA("sp", xm[0][:], xT[0:128, :], [], [b_xm[0]])
                    for m in range(KC):
                        i = m % 2
                        ms = slice(m * 128, (m + 1) * 128)
                        DMA("pool", wo[i][:], Wd["w_o"][l][:, ms].rearrange("(k p) n -> p k n", p=128), [], [b_wo[i]])
                        if m + 1 < KC:
                            DMA("sp", xm[(m + 1) % 3][:], xT[(m + 1) * 128:(m + 2) * 128, :], [], [b_xm[(m + 1) % 3]])
                        i3 = m % 3
                        for ch in range(NCH):
                            cs = slice(ch * 512, (ch + 1) * 512)
                            bk = n % 4
                            for k in range(KC):
                                MM(PB[bk][:], wo[i][:, k, :], mT_[:, k, cs], k == 0, k == KC - 1, [b_wo[i], b_mT], [BK[bk]])
                            STT("dve", xm[i3][:, cs], PB[bk][:], modT[:, l, 32 + m:33 + m], xm[i3][:, cs], ALU.mult, ALU.add, [BK[bk], b_mod, b_xm[i3]], [b_xm[i3]])
                            n += 1
                        DMA("sp", xT[ms, :], xm[i3][:], [b_xm[i3]], [])
            P.barrier()

            with ExitStack() as st:
                hT = sbuf(st, "h2T", [128, KC, S], BF16); b_hT = Buf()
                with ExitStack() as st1:
                    norm_phase(st1, hT, b_hT, AB[:, l, 16:32], modT[:, l, 48:64])
                P.barrier()
                wcT = sbuf(st, "wcT", [128, 3, 88], F32); b_wc = Buf()
                bcT = sbuf(st, "bcT", [128, 88], F32); b_bc = Buf()
                for t in range(3):
                    rows_to_cols(st, Wd["w_conv"][l][t].rearrange("(k p) -> k p", p=128), 88, wcT[:, t, :], b_wc, 7, "wc%d" % t)
                rows_to_cols(st, Wd["b_conv"][l].rearrange("(k p) -> k p", p=128), 88, bcT[:], b_bc, 7, "bc")
                wu = [sbuf(st, "wu%d" % i, [128, KC, 256], BF16) for i in range(2)]; b_wu = [Buf(), Buf()]
                ug = [sbuf(st, "ug%d" % i, [128, S + 2], F32) for i in range(2)]; b_ug = [Buf(), Buf()]
                uv = [sbuf(st, "uv%d" % i, [128, S + 2], F32) for i in range(2)]; b_uv = [Buf(), Buf()]
                cg = [sbuf(st, "cg%d" % i, [128, S], F32) for i in range(2)]; b_cg = [Buf(), Buf()]
                cv = [sbuf(st, "cv%d" % i, [128, S], F32) for i in range(2)]; b_cv = [Buf(), Buf()]
                ab = [sbuf(st, "ab%d" % i, [128, S], BF16) for i in range(2)]; b_ab = [Buf(), Buf()]
                for i in range(2):
                    MS("pool", ug[i][:, 0:2], 0.0, [b_ug[i]])
                    MS("pool", uv[i][:, 0:2], 0.0, [b_uv[i]])
                n = 0
                for j in range(44):
                    i = j % 2
                    DMA("pool", wu[i][:, :, 0:128], Wd["w_up"][l][:, j * 128:(j + 1) * 128].rearrange("(k p) n -> p k n", p=128), [], [b_wu[i]])
                    DMA("pool", wu[i][:, :, 128:256], Wd["w_up"][l][:, DFF + j * 128:DFF + (j + 1) * 128].rearrange("(k p) n -> p k n", p=128), [], [b_wu[i]])
                    for ch in range(NCH):
                        cs2 = slice(2 + ch * 512, 2 + (ch + 1) * 512)
                        cs = slice(ch * 512, (ch + 1) * 512)
                        bg = (n % 4) * 2
                        for k in range(KC):
                            MM(PB[bg][:], wu[i][:, k, 0:128], hT[:, k, cs], k == 0, k == KC - 1, [b_wu[i], b_hT], [BK[bg]])
                        for k in range(KC):
                            MM(PB[bg + 1][:], wu[i][:, k, 128:256], hT[:, k, cs], k == 0, k == KC - 1, [b_wu[i], b_hT], [BK[bg + 1]])
                        CP("act", ug[i][:, cs2], PB[bg][:], [BK[bg]], [b_ug[i]])
                        CP("dve", uv[i][:, cs2], PB[bg + 1][:], [BK[bg + 1]], [b_uv[i]])
                        n += 1
                    for (u_, c_, b_u, b_c, col, e1, e2) in ((ug[i], cg[i], b_ug[i], b_cg[i], j, "pool", "dve"), (uv[i], cv[i], b_uv[i], b_cv[i], 44 + j, "dve", "pool")):
                        ACT(c_[:], u_[:, 2:S + 2], AF.Identity, [b_u, b_wc, b_bc], [b_c], scale=wcT[:, 2, col:col + 1], bias=bcT[:, col:col + 1])
                        STT(e1, c_[:], u_[:, 1:S + 1], wcT[:, 1, col:col + 1], c_[:], ALU.mult, ALU.add, [b_u, b_wc, b_c], [b_c])
                        STT(e2, c_[:], u_[:, 0:S], wcT[:, 0, col:col + 1], c_[:], ALU.mult, ALU.add, [b_u, b_wc, b_c], [b_c])
                    ACT(cg[i][:], cg[i][:], AF.Silu, [b_cg[i]], [b_cg[i]])
                    TT("dve", ab[i][:], cg[i][:], cv[i][:], ALU.mult, [b_cg[i], b_cv[i]], [b_ab[i]])
                    DMA("sp", actT[j * 128:(j + 1) * 128, :], ab[i][:], [b_ab[i]], [])
            P.barrier()

            with ExitStack() as st:
                TC = min(1024, S)
                aS = sbuf(st, "aS", [128, 44, TC], BF16); b_aS = Buf()
                wd = [sbuf(st, "wd%d" % i, [128, 44, 128], BF16) for i in range(2)]; b_wd = [Buf(), Buf()]
                xm = [sbuf(st, "xd%d" % i, [128, TC], F32) for i in range(3)]; b_xm = [Buf(), Buf(), Buf()]
                its = [(c0, m) for c0 in range(0, S, TC) for m in range(KC)]

                def p8_load(n_):
                    if n_ < len(its):
                        c0_, m_ = its[n_]
                        DMA("sp", xm[n_ % 3][:], xT[m_ * 128:(m_ + 1) * 128, c0_:c0_ + TC], [], [b_xm[n_ % 3]])
                p8_load(0)
                for n, (c0, m) in enumerate(its):
                    if m == 0:
                        DMA("sp", aS[:], actT[:, c0:c0 + TC].rearrange("(k p) s -> p k s", p=128), [], [b_aS])
                    i = n % 2
                    i3 = n % 3
                    ms = slice(m * 128, (m + 1) * 128)
                    DMA("pool", wd[i][:], Wd["w_down"][l][:, ms].rearrange("(k p) n -> p k n", p=128), [], [b_wd[i]])
                    p8_load(n + 1)
                    for s0 in range(0, TC, 512):
                        bk = (n * 2 + s0 // 512) % 8
                        for k in range(44):
                            MM(PB[bk][:], wd[i][:, k, :], aS[:, k, s0:s0 + 512], k == 0, k == 43, [b_wd[i], b_aS], [BK[bk]])
                        STT("dve", xm[i3][:, s0:s0 + 512], PB[bk][:], modT[:, l, 80 + m:81 + m], xm[i3][:, s0:s0 + 512], ALU.mult, ALU.add, [BK[bk], b_mod, b_xm[i3]], [b_xm[i3]])
                    DMA("sp", xT[ms, c0:c0 + TC], xm[i3][:], [b_xm[i3]], [])
            P.barrier()

        with ExitStack() as st:
            xc = [sbuf(st, "f_xc%d" % i, [128, KC, 128], F32) for i in range(2)]; b_xc = [Buf(), Buf()]
            ot = [sbuf(st, "f_ot%d" % i, [128, D], F32) for i in range(2)]; b_ot = [Buf(), Buf()]
            for tt in range(NT):
                i = tt % 2
                DMA("sp", xc[i][:], xTv[:, :, tt * 128:(tt + 1) * 128], [], [b_xc[i]])
                for g4 in range(4):
                    bk = (tt * 4 + g4) % 8
                    for j in range(4):
                        kc = g4 * 4 + j
                        TR(PB[bk][:, j * 128:(j + 1) * 128], xc[i][:, kc, :], identf[:], [b_xc[i], b_idf], [BK[bk]])
                    CP("act" if g4 % 2 else "dve", ot[i][:, g4 * 512:(g4 + 1) * 512], PB[bk][:], [BK[bk]], [b_ot[i]])
                DMA("sp", out[tt * 128:(tt + 1) * 128, :], ot[i][:], [b_ot[i]], [])
        for k, ap in dbg_out.items():
            src = {"xT": xT, "qaT": qaT, "kaT": kaT, "va": va, "qiT": qiT, "kiT2": kiT2, "wis": wis, "mkr": mkr, "mqlT": mqlT,
                   "mkvlT": mkvlT, "sgaT": sgaT, "sgbT": sgbT, "mqTn": mqTn, "mqTr": mqTr, "mkTn": mkTn, "mkTr": mkTr, "mv": mv,
                   "MT": MT, "oaT": oaT, "obT": obT, "actT": actT}[k]
            P.dma("sp", lambda e, ap=ap, src=src: e.dma_start(out=ap, in_=src), (), ())
        P.barrier()
        with nc.Block() as block:
            P.emit(block)
    return nc


_NC_CACHE = {}


def kernel(**inputs):
    S = inputs["x"].shape[1]
    B = inputs["x"].shape[0]
    L = inputs["w_in"].shape[0]
    n_sel = min(256, S // 4)
    key = (S, L, n_sel)
    if key not in _NC_CACHE:
        _NC_CACHE[key] = build(S, L, n_sel)
    nc = _NC_CACHE[key]
    invf = inv_freq_table()
    shared = {k: np.ascontiguousarray(np.asarray(inputs[k], dtype=np.float32)) for k in W_SHAPES}
    in_maps = []
    for b in range(B):
        m = dict(shared)
        m["x"] = np.ascontiguousarray(np.asarray(inputs["x"][b], dtype=np.float32))
        m["c"] = np.ascontiguousarray(np.asarray(inputs["c"][b], dtype=np.float32))
        m["positions"] = np.ascontiguousarray(np.asarray(inputs["positions"][b], dtype=np.int32))
        m["invf"] = invf
        in_maps.append(m)
    res = run_bass_kernel_spmd(nc, in_maps, core_ids=list(range(B)))
    return np.stack([np.asarray(r["out"], dtype=np.float32) for r in res.results], axis=0)
```
